# Optimizing a Trainium2 kernel written in Bass

```python
import math
import jax
import jax.numpy as jnp
from jax import lax

D_MODEL = 2048
BATCH = 2
SEQ = 4096
DEPTH = 1

CHUNK = 64
Q_BLOCK = 128

ATT_HEADS = 8
ATT_HALF_DIM = 64
ATT_V_DIM = 2 * ATT_HALF_DIM
ATT_QK_WIDTH = ATT_HEADS * 2 * ATT_HALF_DIM
ATT_WIDTH = ATT_HEADS * ATT_V_DIM

REC_HEADS = 8
REC_KEY_DIM = 128
REC_VAL_DIM = 128
REC_WIDTH = REC_HEADS * REC_KEY_DIM

FFN_HIDDEN = ((8 * D_MODEL + 3 * 256 - 1) // (3 * 256)) * 256

IN_SIZES = (ATT_QK_WIDTH, ATT_QK_WIDTH, ATT_WIDTH,
            REC_WIDTH, REC_WIDTH, REC_WIDTH, REC_WIDTH,
            D_MODEL, D_MODEL)
IN_WIDTH = sum(IN_SIZES)

NORM_EPS = 1e-6

kernel_name = "hybrid_diffattn_hgrn2_gated_block"


def rms_norm(x, g):
    xf = x.astype(jnp.float32)
    y = xf * lax.rsqrt(jnp.mean(xf * xf, axis=-1, keepdims=True) + NORM_EPS)
    return (y * g.astype(jnp.float32)).astype(x.dtype)


def modulate(h, shift, scale):
    return h * (1.0 + scale[:, None, :]) + shift[:, None, :]


def lambda_init_fn(layer):
    return 0.8 - 0.6 * math.exp(-0.3 * layer)


def split_columns(p):
    outs = []
    start = 0
    for n in IN_SIZES:
        outs.append(p[..., start:start + n])
        start += n
    return outs


def diff_attention(q, k, v, lam, subln_g, lambda_init):
    B, S, _ = q.shape
    nqb = S // Q_BLOCK
    qh = q.reshape(B, nqb, Q_BLOCK, ATT_HEADS, 2, ATT_HALF_DIM).transpose(1, 0, 2, 3, 4, 5)
    kh = k.reshape(B, S, ATT_HEADS, 2, ATT_HALF_DIM)
    vh = v.reshape(B, S, ATT_HEADS, ATT_V_DIM)
    k_chunk = jnp.arange(S) // CHUNK
    scale = ATT_HALF_DIM ** -0.5

    def block(args):
        qb, bi = args
        q_chunk = (bi * Q_BLOCK + jnp.arange(Q_BLOCK)) // CHUNK
        mask = k_chunk[None, :] <= q_chunk[:, None]
        s = jnp.einsum('bqhmd,bkhmd->bhmqk', qb, kh).astype(jnp.float32) * scale
        s = jnp.where(mask, s, -jnp.inf)
        p = jax.nn.softmax(s, axis=-1)
        w = p[:, :, 0] - lam * p[:, :, 1]
        return jnp.einsum('bhqk,bkhd->bqhd', w.astype(vh.dtype), vh)

    o = lax.map(block, (qh, jnp.arange(nqb)))
    o = o.transpose(1, 0, 2, 3, 4).reshape(B, S, ATT_HEADS, ATT_V_DIM)
    o = rms_norm(o, subln_g) * (1.0 - lambda_init)
    return o.reshape(B, S, ATT_WIDTH)


def hgrn2(q, f_pre, i, g_pre, lb, gnorm_g):
    B, S, _ = q.shape
    nc = S // CHUNK
    f = lb + (1.0 - lb) * jax.nn.sigmoid(f_pre.astype(jnp.float32))
    log_f = jnp.log(f)
    key = 1.0 - f

    def to_chunks(t, d):
        return t.reshape(B, nc, CHUNK, REC_HEADS, d).transpose(1, 0, 3, 2, 4)

    qc = to_chunks(q.astype(jnp.float32), REC_KEY_DIM)
    kc = to_chunks(key, REC_KEY_DIM)
    vc = to_chunks(i.astype(jnp.float32), REC_VAL_DIM)
    gc = to_chunks(log_f, REC_KEY_DIM)
    causal = jnp.tril(jnp.ones((CHUNK, CHUNK), dtype=bool))

    def step(state, inp):
        qt, kt, vt, gt = inp
        b = jnp.cumsum(gt, axis=2)
        o_inter = jnp.einsum('bhtd,bhde->bhte', qt * jnp.exp(b), state)
        diff = b[:, :, :, None, :] - b[:, :, None, :, :]
        decay = jnp.exp(jnp.where(causal[:, :, None], diff, -jnp.inf))
        att = jnp.einsum('bhtd,bhsd,bhtsd->bhts', qt, kt, decay)
        o_intra = jnp.einsum('bhts,bhse->bhte', att, vt)
        b_last = b[:, :, -1:, :]
        new_state = (jnp.exp(b_last[:, :, 0, :])[..., None] * state
                     + jnp.einsum('bhsd,bhse->bhde', kt * jnp.exp(b_last - b), vt))
        return new_state, o_inter + o_intra

    s0 = jnp.zeros((B, REC_HEADS, REC_KEY_DIM, REC_VAL_DIM), jnp.float32)
    _, o = lax.scan(step, s0, (qc, kc, vc, gc))
    o = o.transpose(1, 0, 3, 2, 4).reshape(B, S, REC_HEADS, REC_VAL_DIM)
    o = rms_norm(o, gnorm_g).reshape(B, S, REC_WIDTH)
    o = o * jax.nn.sigmoid(g_pre.astype(jnp.float32))
    return o.astype(q.dtype)


def setup_inputs(seed: int = 0) -> dict:
    key = jax.random.key(seed)
    ks = jax.random.split(key, 24)
    D = D_MODEL
    nrm = jax.random.normal
    f32 = jnp.float32
    return {
        "x": nrm(ks[0], (BATCH, SEQ, D), f32),
        "c": nrm(ks[1], (BATCH, D), f32),
        "w_mod": nrm(ks[2], (DEPTH, D, 6 * D), f32) * D ** -0.5,
        "b_mod": nrm(ks[3], (DEPTH, 6 * D), f32) * 0.01,
        "norm1_g": 1.0 + 0.02 * nrm(ks[4], (DEPTH, D), f32),
        "w_in": nrm(ks[5], (DEPTH, D, IN_WIDTH), f32) * D ** -0.5,
        "lambda_q1": nrm(ks[6], (DEPTH, ATT_HALF_DIM), f32) * 0.1,
        "lambda_k1": nrm(ks[7], (DEPTH, ATT_HALF_DIM), f32) * 0.1,
        "lambda_q2": nrm(ks[8], (DEPTH, ATT_HALF_DIM), f32) * 0.1,
        "lambda_k2": nrm(ks[9], (DEPTH, ATT_HALF_DIM), f32) * 0.1,
        "subln_g": 1.0 + 0.02 * nrm(ks[10], (DEPTH, ATT_V_DIM), f32),
        "lb_logits": nrm(ks[11], (DEPTH + 1, REC_WIDTH), f32) * 0.5,
        "gnorm_g": 1.0 + 0.02 * nrm(ks[12], (DEPTH, REC_HEADS, REC_VAL_DIM), f32),
        "w_att_out": nrm(ks[13], (DEPTH, ATT_WIDTH, D), f32) * ATT_WIDTH ** -0.5,
        "w_rec_out": nrm(ks[14], (DEPTH, REC_WIDTH, D), f32) * REC_WIDTH ** -0.5,
        "w_o": nrm(ks[15], (DEPTH, D, D), f32) * D ** -0.5,
        "norm2_g": 1.0 + 0.02 * nrm(ks[16], (DEPTH, D), f32),
        "w_ffn_gate": nrm(ks[17], (DEPTH, D, FFN_HIDDEN), f32) * D ** -0.5,
        "w_ffn_up": nrm(ks[18], (DEPTH, D, FFN_HIDDEN), f32) * D ** -0.5,
        "w_ffn_down": nrm(ks[19], (DEPTH, FFN_HIDDEN, D), f32) * FFN_HIDDEN ** -0.5,
        "final_g": 1.0 + 0.02 * nrm(ks[20], (D,), f32),
    }


def reference(x, c, w_mod, b_mod, norm1_g, w_in, lambda_q1, lambda_k1, lambda_q2, lambda_k2,
              subln_g, lb_logits, gnorm_g, w_att_out, w_rec_out, w_o, norm2_g,
              w_ffn_gate, w_ffn_up, w_ffn_down, final_g):
    lower_bounds = jnp.cumsum(jax.nn.softmax(lb_logits.astype(jnp.float32), axis=0), axis=0)
    cond = jax.nn.silu(c)
    h = x
    for l in range(DEPTH):
        mod = cond @ w_mod[l] + b_mod[l]
        sh1, sc1, gt1, sh2, sc2, gt2 = jnp.split(mod, 6, axis=-1)

        u = modulate(rms_norm(h, norm1_g[l]), sh1, sc1)
        aq, ak, av, rq, rf, ri, rg, ga, gr = split_columns(u @ w_in[l])
        lam_init = lambda_init_fn(l)
        lam = (jnp.exp(jnp.sum(lambda_q1[l] * lambda_k1[l]).astype(jnp.float32))
               - jnp.exp(jnp.sum(lambda_q2[l] * lambda_k2[l]).astype(jnp.float32)) + lam_init)
        ya = diff_attention(aq, ak, av, lam, subln_g[l], lam_init)
        yr = hgrn2(rq, rf, ri, rg, lower_bounds[l], gnorm_g[l])
        merged = (jax.nn.sigmoid(ga) * (ya @ w_att_out[l])
                  + jax.nn.sigmoid(gr) * (yr @ w_rec_out[l]))
        h = h + gt1[:, None, :] * (merged @ w_o[l])

        u = modulate(rms_norm(h, norm2_g[l]), sh2, sc2)
        ffn = (jax.nn.silu(u @ w_ffn_gate[l]) * (u @ w_ffn_up[l])) @ w_ffn_down[l]
        h = h + gt2[:, None, :] * ffn
    return rms_norm(h, final_g)
```

```python
import math
from contextlib import ExitStack

import numpy as np
import concourse.bass as bass
import concourse.mybir as mybir
from concourse.bass_utils import run_bass_kernel_spmd

F32 = mybir.dt.float32
BF16 = mybir.dt.bfloat16
I32 = mybir.dt.int32
U8 = mybir.dt.uint8
AF = mybir.ActivationFunctionType
ALU = mybir.AluOpType

D = 2048
SEQ = 4096
NTOK = 1024
FF = 5632
EPS = 1e-6
LAM_INIT = 0.8 - 0.6 * math.exp(-0.3 * 0)
ENGS = ("pe", "act", "dve", "pool", "sp")
DEBUG = False
STOP = None
SHAPE_OVR = {}


class Trk:
    __slots__ = ("w", "r")

    def __init__(self):
        self.w = None
        self.r = []


class Prog:
    def __init__(self, nc, stack):
        self.nc = nc
        self.stack = stack
        self.q = {e: [] for e in ENGS}
        self.cnt = {e: 0 for e in ENGS}
        self.sem = {e: stack.enter_context(nc.semaphore("s_" + e)) for e in ENGS}
        self.seen = {e: {} for e in ENGS}
        self.dma_sems = []
        self.ninstr = 0
        self.nwaits = 0

    def ps(self, name, shape, dt):
        return self.stack.enter_context(self.nc.psum_tensor(name, list(shape), dt))

    def dsem(self, name):
        s = self.stack.enter_context(self.nc.semaphore(name))
        d = {"sem": s, "val": 0, "name": name}
        self.dma_sems.append(d)
        return d

    def _wait_for(self, eng, ev):
        if ev is None:
            return
        kind, key, val = ev
        if kind == "eng" and key == "pe" and eng == "pe":
            return
        if kind == "eng":
            semkey = "E" + key
            sem = self.sem[key]
        else:
            semkey = "D" + key["name"]
            sem = key["sem"]
        if self.seen[eng].get(semkey, 0) >= val:
            return
        self.seen[eng][semkey] = val
        self.nwaits += 1
        self.q[eng].append(lambda e, sem=sem, val=val: e.wait_ge(sem, val))

    def _deps(self, eng, reads, writes):
        for t in reads:
            self._wait_for(eng, t.w)
        for t in writes:
            self._wait_for(eng, t.w)
            for ev in t.r:
                self._wait_for(eng, ev)

    def _commit(self, ev, reads, writes):
        for t in reads:
            t.r.append(ev)
            if len(t.r) > 48:
                last = {}
                for e_ in t.r:
                    k = (e_[0], e_[1] if e_[0] == "eng" else e_[1]["name"])
                    if k not in last or last[k][2] < e_[2]:
                        last[k] = e_
                t.r = list(last.values())
        for t in writes:
            t.w = ev
            t.r = []

    def op(self, eng, fn, reads=(), writes=(), signal=True):
        self._deps(eng, reads, writes)
        self.ninstr += 1
        if signal:
            self.cnt[eng] += 1
            sem = self.sem[eng]
            self.q[eng].append(lambda e, fn=fn, sem=sem: fn(e).then_inc(sem, 1))
            ev = ("eng", eng, self.cnt[eng])
        else:
            assert eng == "pe"
            self.q[eng].append(lambda e, fn=fn: fn(e))
            ev = ("eng", eng, self.cnt[eng] + 1)
        self._commit(ev, reads, writes)
        return ev

    def dma(self, eng, out, in_, ds, reads=(), writes=(), **kw):
        self._deps(eng, reads, writes)
        self.ninstr += 1
        ds["val"] += 16
        sem = ds["sem"]
        self.q[eng].append(
            lambda e, out=out, in_=in_, sem=sem, kw=kw: e.dma_start(out=out, in_=in_, **kw).then_inc(sem, 16))
        ev = ("dma", ds, ds["val"])
        self._commit(ev, reads, writes)
        return ev

    def raw(self, eng, fn):
        self.q[eng].append(fn)

    def wait_all(self, eng, trks):
        for t in trks:
            self._wait_for(eng, t.w)
            for ev in t.r:
                self._wait_for(eng, ev)

    def barrier(self):
        for e in ENGS:
            for e2 in ENGS:
                if e2 != e and self.cnt[e2] > 0:
                    self._wait_for(e, ("eng", e2, self.cnt[e2]))
            for d in self.dma_sems:
                if d["val"] > 0:
                    self._wait_for(e, ("dma", d, d["val"]))

    def finish(self):
        nc = self.nc
        q = self.q
        with nc.Block() as block:
            @block.tensor
            def _(e):
                for f in q["pe"]:
                    f(e)

            @block.scalar
            def _(e):
                for f in q["act"]:
                    f(e)

            @block.vector
            def _(e):
                for f in q["dve"]:
                    f(e)

            @block.gpsimd
            def _(e):
                for f in q["pool"]:
                    f(e)

            @block.sync
            def _(e):
                for f in q["sp"]:
                    f(e)


class Arena:
    def __init__(self, ap_u8, size):
        self.ap = ap_u8
        self.size = size
        self.off = 0

    def alloc(self, shape, dt, parts=None):
        esz = {F32: 4, BF16: 2, I32: 4}[dt]
        n = 1
        for s in shape[1:]:
            n *= s
        nb = (n * esz + 31) // 32 * 32
        assert self.off + nb <= self.size, ("arena overflow", self.off, nb, self.size)
        v = self.ap[0:shape[0], self.off:self.off + n * esz].bitcast(dt)
        self.off += nb
        if len(shape) == 3:
            v = v.rearrange("p (a b) -> p a b", a=shape[1])
        elif len(shape) == 4:
            v = v.rearrange("p (a b c) -> p a b c", a=shape[1], b=shape[2])
        return v


def build_nc():
    nc = bass.Bass("TRN2", target_bir_lowering=False)

    def din(name, shape, dt=F32):
        shape = SHAPE_OVR.get(name, shape)
        return nc.dram_tensor(name, list(shape), dt, kind="ExternalInput").ap()

    xb = din("xb", [SEQ, D])
    xo = din("xo", [NTOK, D])
    cT_d = din("cT", [128, 16])
    wmod_d = din("wmod", [D, 3072])
    bmod_d = din("bmod", [1, 3072])
    wfm_d = din("w_fm", [D, 1024])
    wtm_d = din("w_tm", [D, 1024])
    wg_d = din("w_g", [D, 4096])
    wao_d = din("w_ao", [1024, D])
    wro_d = din("w_ro", [1024, D])
    wo_d = din("w_o", [D, D])
    wfg_d = din("w_fg", [D, FF])
    wfu_d = din("w_fu", [D, FF])
    wfd_d = din("w_fd", [FF, D])
    n1g_d = din("n1g", [1, D])
    n2g_d = din("n2g", [1, D])
    fg_d = din("fg", [1, D])
    lbc_d = din("lbc", [128, 4])
    lbr_d = din("lbr", [1, 512])
    gnr_d = din("gnr", [1, 256])
    subg_d = din("subg", [128, 1])
    lamv_d = din("lamv", [1, 256])
    consts_d = din("consts", [128, 512])
    joff_d = din("joff", [1, 2], I32)
    out_d = nc.dram_tensor("out", [NTOK, D], F32, kind="ExternalOutput").ap()
    if DEBUG:
        dbg_y = nc.dram_tensor("dbg_y", [8, 512, 512], BF16, kind="ExternalOutput").ap()
        dbg_h = nc.dram_tensor("dbg_h", [NTOK, D], F32, kind="ExternalOutput").ap()
        dbg_mod = nc.dram_tensor("dbg_mod", [4, 3072], F32, kind="ExternalOutput").ap()

    mod_in = nc.dram_tensor("mod_in", [1, 3072], F32)
    mod_out = nc.dram_tensor("mod_out", [4, 3072], F32)
    ex_in = nc.dram_tensor("ex_in", [8, 512, 512], BF16)
    ex_out = nc.dram_tensor("ex_out", [8, 2048, 512], BF16)
    t_modin, t_modout, t_exin, t_exout = Trk(), Trk(), Trk(), Trk()

    with ExitStack() as st:
        P = Prog(nc, st)
        ARENA_SZ = 206 * 1024
        arena_t = st.enter_context(nc.sbuf_tensor("arena", [128, ARENA_SZ], U8))
        AR = Arena(arena_t, ARENA_SZ)
        banks = [P.ps("bank%d" % i, [128, 512], F32)[:, :] for i in range(8)]
        bk = [Trk() for _ in range(8)]
        dout = P.dsem("dout")
        ccs = st.enter_context(nc.semaphore("ccs"))
        cc_n = [0]
        ccsd = {"sem": ccs, "val": 0, "name": "ccs"}

        class _Stop(Exception):
            pass

        def checkpoint(name):
            if STOP == name:
                raise _Stop()

        def act(out, in_, func, R, W, **kw):
            P.op("act", lambda e: e.activation(out=out, in_=in_, func=func, **kw), R, W)

        def tt(eng, out, in0, in1, op, R, W):
            P.op(eng, lambda e: e.tensor_tensor(out=out, in0=in0, in1=in1, op=op), R, W)

        def ts(eng, out, in0, s1, s2, op0, op1, R, W):
            P.op(eng, lambda e: e.tensor_scalar(out=out, in0=in0, scalar1=s1, scalar2=s2, op0=op0, op1=op1), R, W)

        def stt(eng, out, in0, scalar, in1, op0, op1, R, W):
            P.op(eng, lambda e: e.scalar_tensor_tensor(out=out, in0=in0, scalar=scalar, in1=in1, op0=op0, op1=op1),
                 R, W)

        def cp(eng, out, in_, R, W):
            if eng == "act":
                act(out, in_, AF.Copy, R, W)
            else:
                P.op(eng, lambda e: e.tensor_copy(out=out, in_=in_), R, W)

        def recip(out, in_, R, W):
            P.op("dve", lambda e: e.reciprocal(out=out, in_=in_), R, W)

        def mm(out, lhsT, rhs, start, stop, R, W, signal=True, **kw):
            P.op("pe", lambda e: e.matmul(out, lhsT=lhsT, rhs=rhs, start=start, stop=stop, **kw), R, W, signal=signal)

        def tr(out, in_, ident, R, W, signal=True):
            P.op("pe", lambda e: e.transpose(out=out, in_=in_, identity=ident), R, W, signal=signal)

        def memset(eng, ap, val, W):
            P.op(eng, lambda e: e.memset(ap, val), (), W)

        def collective(kind, groups, src, dst, R, W):
            P.wait_all("pool", R + W)
            cc_n[0] += 1
            n = cc_n[0]
            P.raw("pool", lambda e: e.collective_compute(kind, ALU.bypass, replica_groups=groups,
                                                         ins=[src], outs=[dst]).then_inc(ccs, 1))
            P.raw("pool", lambda e: e.wait_ge(ccs, n))
            ev = P.op("pool", lambda e: e.memset(cc_dummy[:], 0.0), (), [t_ccd])
            for t in W:
                t.w = ev
                t.r = []

        try:
            cc_dummy = AR.alloc([128, 8], F32)
            t_ccd = Trk()
            consts32 = AR.alloc([128, 4, 128], F32); t_c32 = Trk()
            ident_bf = AR.alloc([128, 128], BF16); t_idb = Trk()
            maskU_bf = AR.alloc([128, 128], BF16); t_mku = Trk()
            ones_bf = AR.alloc([128, 128], BF16); t_ones = Trk()
            d_c = P.dsem("d_c")
            P.dma("sp", consts32.rearrange("p a b -> p (a b)"), consts_d, d_c, writes=[t_c32])
            cp("dve", ident_bf, consts32[:, 0, :], [t_c32], [t_idb])
            cp("dve", maskU_bf, consts32[:, 1, :], [t_c32], [t_mku])
            memset("pool", ones_bf, 1.0, [t_ones])
            U32 = consts32[:, 1, :]
            Um32 = consts32[:, 2, :]
            L32 = consts32[:, 3, :]

            lbc = AR.alloc([128, 4], F32); t_lbc = Trk()
            lbr = AR.alloc([128, 512], F32); t_lbr = Trk()
            gnr = AR.alloc([128, 256], F32); t_gnr = Trk()
            subg = AR.alloc([128, 1], F32); t_subg = Trk()
            lamv = AR.alloc([128, 256], F32); t_lamv = Trk()
            d_sm = P.dsem("d_sm")
            P.dma("sp", lbc, lbc_d, d_sm, writes=[t_lbc])
            P.dma("sp", lbr.unsqueeze(1), lbr_d.partition_broadcast(128), d_sm, writes=[t_lbr])
            P.dma("sp", gnr.unsqueeze(1), gnr_d.partition_broadcast(128), d_sm, writes=[t_gnr])
            P.dma("sp", subg, subg_d, d_sm, writes=[t_subg])
            P.dma("sp", lamv.unsqueeze(1), lamv_d.partition_broadcast(128), d_sm, writes=[t_lamv])
            joff_sb = AR.alloc([1, 8], I32); t_joff = Trk()
            P.dma("sp", joff_sb[0:1, 0:2], joff_d, d_sm, writes=[t_joff])
            cT = AR.alloc([128, 16], F32); t_cT = Trk()
            P.dma("sp", cT, cT_d, d_sm, writes=[t_cT])
            for t_ in (t_lbc, t_lbr, t_gnr, t_subg, t_lamv, t_joff, t_cT):
                t_.w = ("dma", d_sm, d_sm["val"])
            lbcol = AR.alloc([128, 2], F32)
            omlcol = AR.alloc([128, 2], F32)
            dcol = AR.alloc([128, 2], F32)
            t_lbcol = Trk()
            lbc3 = lbc.rearrange("p (h l) -> p h l", h=2)
            tt("dve", dcol, lbc3[:, :, 0], lbc3[:, :, 1], ALU.subtract, [t_lbc], [t_lbcol])
            act(lbcol, dcol, AF.Sigmoid, [t_lbcol], [t_lbcol])
            act(omlcol, dcol, AF.Sigmoid, [t_lbcol], [t_lbcol], scale=-1.0)
            lbrow = AR.alloc([128, 2, 128], F32)
            omlrow = AR.alloc([128, 2, 128], F32)
            drow = AR.alloc([128, 2, 128], F32)
            t_lbrow = Trk()
            lbr4 = lbr.rearrange("p (h l d) -> p h l d", h=2, l=2)
            tt("dve", drow, lbr4[:, :, 0, :], lbr4[:, :, 1, :], ALU.subtract, [t_lbr], [t_lbrow])
            act(lbrow, drow, AF.Sigmoid, [t_lbrow], [t_lbrow])
            act(omlrow, drow, AF.Sigmoid, [t_lbrow], [t_lbrow], scale=-1.0)
            gnrow = gnr.rearrange("p (h e) -> p h e", h=2)
            gsub = AR.alloc([128, 1], F32); t_gsub = Trk()
            ts("dve", gsub, subg, 1.0 - LAM_INIT, 0.0, ALU.mult, ALU.add, [t_subg], [t_gsub])
            lam_t = AR.alloc([128, 8], F32); t_lam = Trk()
            lprod = AR.alloc([128, 2, 64], F32)
            lv = lamv.rearrange("p (a d) -> p a d", a=4)
            tt("dve", lprod[:, 0, :], lv[:, 0, :], lv[:, 1, :], ALU.mult, [t_lamv], [t_lam])
            tt("dve", lprod[:, 1, :], lv[:, 2, :], lv[:, 3, :], ALU.mult, [t_lamv], [t_lam])
            P.op("dve", lambda e: e.tensor_reduce(out=lam_t[:, 0:2], in_=lprod, axis=mybir.AxisListType.X, op=ALU.add),
                 [t_lam], [t_lam])
            act(lam_t[:, 2:4], lam_t[:, 0:2], AF.Exp, [t_lam], [t_lam])
            tt("dve", lam_t[:, 4:5], lam_t[:, 3:4], lam_t[:, 2:3], ALU.subtract, [t_lam], [t_lam])
            ts("dve", lam_t[:, 5:6], lam_t[:, 4:5], 1.0, -LAM_INIT, ALU.mult, ALU.add, [t_lam], [t_lam])
            neglam = lam_t[:, 5:6]

            Arow = AR.alloc([128, D], F32); t_Arow = Trk()
            Brow = AR.alloc([128, D], F32); t_Brow = Trk()
            base_off = AR.off

            wfm = AR.alloc([128, 16, 1024], BF16)
            wtm = AR.alloc([128, 16, 1024], BF16)
            p1_off = AR.off
            cond = AR.alloc([128, 16], BF16); t_cond = Trk()
            bmod = AR.alloc([1, 3072], F32); t_bmod = Trk()
            modrow = AR.alloc([1, 3072], F32); t_modrow = Trk()
            wm = [AR.alloc([128, 16, 512], BF16) for _ in range(2)]
            t_wm = [Trk(), Trk()]
            d_wm = [P.dsem("d_wm0"), P.dsem("d_wm1")]
            d_bm = P.dsem("d_bm")
            P.dma("sp", bmod, bmod_d, d_bm, writes=[t_bmod])
            act(cond, cT, AF.Silu, [t_cT], [t_cond])
            wmod_v = wmod_d.rearrange("(kt p) n -> p kt n", p=128)
            for cb in range(6):
                s = cb % 2
                P.dma("pool", wm[s], wmod_v[:, :, cb * 512:(cb + 1) * 512], d_wm[s], writes=[t_wm[s]])
                for kt in range(16):
                    mm(banks[cb % 2][0:1, :], cond[:, kt:kt + 1], wm[s][:, kt, :], kt == 0, kt == 15,
                       [t_cond, t_wm[s]], [bk[cb % 2]], signal=(kt == 15))
                tt("dve", modrow[0:1, cb * 512:(cb + 1) * 512], banks[cb % 2][0:1, :], bmod[0:1, cb * 512:(cb + 1) * 512],
                   ALU.add, [bk[cb % 2], t_bmod], [t_modrow])
            d_mod = P.dsem("d_mod")
            P.dma("sp", mod_in.ap(), modrow, d_mod, reads=[t_modrow], writes=[t_modin])
            wfm_v = wfm_d.rearrange("(kt p) n -> p kt n", p=128)
            wtm_v = wtm_d.rearrange("(kt p) n -> p kt n", p=128)
            t_wfmh = [Trk(), Trk()]
            t_wtmh = [Trk(), Trk()]
            for hlf in range(2):
                P.dma("pool", wfm[:, hlf * 8:(hlf + 1) * 8, :], wfm_v[:, hlf * 8:(hlf + 1) * 8, :],
                      P.dsem("d_wfm%d" % hlf), writes=[t_wfmh[hlf]])
                P.dma("pool", wtm[:, hlf * 8:(hlf + 1) * 8, :], wtm_v[:, hlf * 8:(hlf + 1) * 8, :],
                      P.dsem("d_wtm%d" % hlf), writes=[t_wtmh[hlf]])
            collective("AllGather", [[0, 1, 2, 3], [4, 5, 6, 7]], mod_in.ap(), mod_out.ap(), [t_modin], [t_modout])
            modflat = mod_out.ap().rearrange("r n -> (r n)").unsqueeze(0)

            row_sems = {}

            def load_row(dst, src_row_ap, ds, W, R=()):
                key = id(W[0])
                if key not in row_sems:
                    row_sems[key] = P.dsem("d_row%d" % len(row_sems))
                P.dma("sp", dst.unsqueeze(1), src_row_ap.partition_broadcast(128), row_sems[key], reads=list(R), writes=W)

            load_row(Arow, modflat[0:1, 2048:4096], d_mod, [t_Arow], [t_modout])
            load_row(Brow, n1g_d, d_mod, [t_Brow], [])
            stt("dve", Arow, Arow, 1.0, Brow, ALU.add, ALU.mult, [t_Arow, t_Brow], [t_Arow])
            load_row(Brow, modflat[0:1, 0:2048], d_mod, [t_Brow], [t_modout])
            if DEBUG:
                P.dma("sp", dbg_mod, mod_out.ap(), P.dsem("dbgm"), reads=[t_modout])

            P.barrier()
            checkpoint("p0")
            AR.off = p1_off

            KT = [AR.alloc([128, SEQ], BF16) for _ in range(2)]
            t_KT = [[Trk() for _ in range(8)] for _ in range(2)]
            V = [AR.alloc([128, 32, 128], BF16) for _ in range(2)]
            t_V = [[Trk() for _ in range(8)] for _ in range(2)]
            xs = [AR.alloc([128, D], F32) for _ in range(2)]
            t_xs = [Trk(), Trk()]
            d_xs = [P.dsem("d_xs0"), P.dsem("d_xs1")]
            u_bfs = [AR.alloc([128, D], BF16) for _ in range(2)]; t_us = [Trk(), Trk()]
            uT_off = AR.off
            uT = AR.alloc([128, 16, 512], BF16)
            t_uT = [Trk() for _ in range(4)]
            stats = [AR.alloc([128, 8], F32) for _ in range(2)]; t_stats = [Trk(), Trk()]
            QT = [AR.alloc([128, 512], BF16) for _ in range(2)]; t_QT = [Trk(), Trk()]
            rqT = [AR.alloc([128, 512], F32) for _ in range(2)]; t_rqT = [Trk(), Trk()]
            snT = [AR.alloc([128, 512], F32) for _ in range(2)]; t_snT = [Trk(), Trk()]
            vtok = [AR.alloc([128, 4, 128], BF16) for _ in range(2)]
            t_vtok = [[Trk() for _ in range(4)] for _ in range(2)]
            sgf = [AR.alloc([128, 4, 256], F32) for _ in range(2)]
            t_sgf = [[Trk() for _ in range(4)] for _ in range(2)]
            Pb = [[AR.alloc([128, 512], BF16) for _ in range(2)] for _ in range(2)]
            t_Pb = [[Trk(), Trk()], [Trk(), Trk()]]
            R0 = AR.alloc([128, 512], F32); R1 = AR.alloc([128, 512], F32)
            o_f = AR.alloc([128, 512], F32); sq_bf = AR.alloc([128, 512], BF16)
            t_R0, t_R1, t_of, t_sq = Trk(), Trk(), Trk(), Trk()
            yst = [AR.alloc([128, 4, 512], BF16)] * 2
            t_yst = [[Trk() for _ in range(4)]] * 2
            d_yst = [P.dsem("d_yst0")] * 2
            def hgrn_set(A):
                W = {}
                for nm, shp, dt_ in (("hf", [128, 128], F32), ("hg", [128, 128], F32), ("hk", [128, 128], F32),
                                     ("E12", [128, 256], F32), ("E3", [128, 128], F32), ("E4", [128, 128], F32),
                                     ("qdz", [128, 2, 128], BF16), ("qm", [128, 128], BF16), ("km", [128, 128], BF16),
                                     ("kl", [128, 128], BF16), ("attm", [128, 128], BF16), ("gg", [128, 128], F32),
                                     ("yb", [128, 128], BF16), ("hst", [128, 8], F32)):
                    W[nm] = A.alloc(shp, dt_)
                    W["t_" + nm] = Trk()
                return W

            HW = [hgrn_set(AR)]
            AR1 = Arena(arena_t, uT_off + 16 * 1024)
            AR1.off = uT_off
            HW.append(hgrn_set(AR1))
            hw1_trks = [v for k_, v in HW[1].items() if k_.startswith("t_")]
            t_qdz = HW[0]["t_qdz"]
            qdz = HW[0]["qdz"]
            S32 = [AR.alloc([128, 128], F32) for _ in range(2)]; t_S32 = [Trk(), Trk()]
            Sbf = [[AR.alloc([128, 128], BF16) for _ in range(2)] for _ in range(2)]
            t_Sbf = [[Trk(), Trk()], [Trk(), Trk()]]
            for hh in range(2):
                memset("pool", S32[hh], 0.0, [t_S32[hh]])
                memset("pool", Sbf[hh][0], 0.0, [t_Sbf[hh][0]])
                memset("pool", Sbf[hh][1], 0.0, [t_Sbf[hh][1]])
            memset("pool", qdz, 0.0, [t_qdz])
            print("phase1 arena bytes", AR.off)

            psT = [banks[0][:, :].bitcast(BF16), banks[1][:, :].bitcast(BF16)]

            def norm_s1(i, x_ap_dram, xbuf, t_x, d_x):
                ub, t_ub, stt_, t_st = u_bfs[i % 2], t_us[i % 2], stats[i % 2], t_stats[i % 2]
                P.dma("sp", xbuf, x_ap_dram, d_x, writes=[t_x])
                act(ub, xbuf, AF.Square, [t_x], [t_ub, t_st], accum_out=stt_[:, 0:1])
                act(stt_[:, 1:2], stt_[:, 0:1], AF.Ln, [t_st], [t_st], scale=1.0 / D, bias=EPS)
                act(stt_[:, 2:3], stt_[:, 1:2], AF.Exp, [t_st], [t_st], scale=-0.5)
                stt("dve", xbuf, xbuf, stt_[:, 2:3], Arow, ALU.mult, ALU.mult, [t_x, t_st, t_Arow], [t_x])
                tt("pool", ub, xbuf, Brow, ALU.add, [t_x, t_Brow], [t_ub])

            def norm_s2(i, uT_dst, t_uTdst):
                ub, t_ub = u_bfs[i % 2], t_us[i % 2]
                for hlf in range(2):
                    for k8 in range(8):
                        kt = hlf * 8 + k8
                        tr(psT[hlf][:, k8 * 128:(k8 + 1) * 128], ub[:, kt * 128:(kt + 1) * 128], ident_bf,
                           [t_ub, t_idb], [bk[hlf]], signal=(k8 == 7))
                    src = psT[hlf].rearrange("p (a b) -> p a b", a=8)
                    cp("act" if hlf == 0 else "dve", uT_dst[:, hlf * 8:(hlf + 1) * 8, :], src, [bk[hlf]], [t_uTdst])

            def fm_proj(wsb, t_w, col0, rhs_fn, t_rhs, bank_i, nk=16):
                for kt in range(nk):
                    mm(banks[bank_i][:, :], wsb[:, kt, col0:col0 + 128], rhs_fn(kt), kt == 0, kt == nk - 1,
                       [t_w] + t_rhs, [bk[bank_i]], signal=(kt == nk - 1))

            rot = [2]

            def next_bank(lo=2, hi=7):
                b_ = rot[0]
                rot[0] = lo + (rot[0] - lo + 1) % (hi - lo + 1)
                return b_

            for blk in range(8):
                def _s1(t4_):
                    T_ = blk * 4 + t4_
                    norm_s1(T_, xb[T_ * 128:(T_ + 1) * 128, :], xs[T_ % 2], t_xs[T_ % 2], d_xs[T_ % 2])

                _s1(0)
                for t4 in range(4):
                    if t4 + 1 < 4:
                        _s1(t4 + 1)
                    norm_s2(blk * 4 + t4, uT[:, :, t4 * 128:(t4 + 1) * 128], t_uT[t4])
                for hh in range(2):
                    for ci in range(4):
                        b_ = next_bank()
                        fm_proj(wfm, t_wfmh[0], hh * 512 + ci * 128, lambda kt: uT[:, kt, :], t_uT + [t_wfmh[1]], b_)
                        if ci == 0:
                            cp("act", QT[hh], banks[b_], [bk[b_]], [t_QT[hh]])
                        elif ci == 1:
                            cp("dve", KT[hh][:, blk * 512:(blk + 1) * 512], banks[b_], [bk[b_]], [t_KT[hh][blk]])
                        elif ci == 2:
                            cp("act", rqT[hh], banks[b_], [bk[b_]], [t_rqT[hh]])
                        else:
                            act(snT[hh], banks[b_], AF.Sigmoid, [bk[b_]], [t_snT[hh]], scale=-1.0)
                    for t4 in range(4):
                        b_ = next_bank()
                        for kt in range(16):
                            mm(banks[b_][:, :], uT[:, kt, t4 * 128:(t4 + 1) * 128], wtm[:, kt, hh * 512:(hh + 1) * 512],
                               kt == 0, kt == 15, [t_uT[t4]] + t_wtmh, [bk[b_]], signal=(kt == 15))
                        cp("dve", V[hh][:, blk * 4 + t4, :], banks[b_][:, 0:128], [bk[b_]], [t_V[hh][blk]])
                        cp("dve", vtok[hh][:, t4, :], banks[b_][:, 128:256], [bk[b_]], [t_vtok[hh][t4]])
                        act(sgf[hh][:, t4, :], banks[b_][:, 256:512], AF.Sigmoid, [bk[b_], t_vtok[hh][t4]], [t_sgf[hh][t4]])
                checkpoint("proj%d" % blk)
                ys = yst[blk % 2]
                t_ys = t_yst[blk % 2]
                for hh in range(2):
                    O = [banks[2], banks[3]]; Dn = [banks[4], banks[5]]
                    Sxs = [[banks[6], banks[7]], [banks[0], banks[1]]]
                    bSx = [[6, 7], [0, 1]]
                    nkt = 4 * (blk + 1)

                    def emit_S(kt_):
                        r_ = kt_ - 4 * blk
                        c0_ = 128 * r_ if r_ > 0 else 0
                        for m in range(2):
                            mm(Sxs[kt_ % 2][m][:, c0_:512], KT[hh][64 * m:64 * (m + 1), kt_ * 128:(kt_ + 1) * 128],
                               QT[hh][64 * m:64 * (m + 1), c0_:512], True, True,
                               [t_KT[hh][kt_ // 4], t_QT[hh]], [bk[bSx[kt_ % 2][m]]], signal=True,
                               tile_position=(64 * m, 0))

                    emit_S(0)
                    for kt in range(nkt):
                        r = kt - 4 * blk
                        c0 = 128 * r if r > 0 else 0
                        kb_ = kt // 4
                        pb = kt % 2
                        if kt + 1 < nkt:
                            emit_S(kt + 1)
                        for m in range(2):
                            act(Pb[pb][m][:, c0:512], Sxs[pb][m][:, c0:512], AF.Exp, [bk[bSx[pb][m]]], [t_Pb[pb][m]],
                                scale=0.125)
                            if r >= 0:
                                memset("pool", Pb[pb][m][64:128, c0:c0 + 64], 0.0, [t_Pb[pb][m]])
                        last = (kt == nkt - 1)
                        for m in range(2):
                            mm(O[m][:, c0:512], V[hh][:, kt, :], Pb[pb][m][:, c0:512], kt == 0, last,
                               [t_V[hh][kb_], t_Pb[pb][m]], [bk[2 + m]], signal=last)
                        for m in range(2):
                            mm(Dn[m][:, c0:512], ones_bf, Pb[pb][m][:, c0:512], kt == 0, last,
                               [t_ones, t_Pb[pb][m]], [bk[4 + m]], signal=last)
                    recip(R0, Dn[0], [bk[4]], [t_R0])
                    recip(R1, Dn[1], [bk[5]], [t_R1])
                    tt("dve", R0, O[0], R0, ALU.mult, [bk[2], t_R0], [t_R0])
                    tt("dve", R1, O[1], R1, ALU.mult, [bk[3], t_R1], [t_R1])
                    stt("dve", o_f, R1, neglam, R0, ALU.mult, ALU.add, [t_R0, t_R1, t_lam], [t_of])
                    act(sq_bf, o_f, AF.Square, [t_of], [t_sq])
                    mm(banks[6][:, :], ones_bf, sq_bf, True, True, [t_ones, t_sq], [bk[6]])
                    act(R0, banks[6], AF.Ln, [bk[6]], [t_R0], scale=1.0 / 128, bias=EPS)
                    act(R1, R0, AF.Exp, [t_R0], [t_R1], scale=-0.5)
                    stt("dve", ys[:, hh * 2 + 0, :], o_f, gsub, R1, ALU.mult, ALU.mult, [t_of, t_gsub, t_R1],
                        [t_ys[hh * 2 + 0]])
                checkpoint("attn%d" % blk)
                for e_ in ("pe", "act", "dve", "pool"):
                    P.wait_all(e_, t_uT)
                memset("pool", HW[1]["qdz"], 0.0, [HW[1]["t_qdz"]])

                def hgrn_gen(hh, t4, W, bnk):
                    cs, bB, bC, bD = banks[bnk[0]], banks[bnk[1]], banks[bnk[2]], banks[bnk[3]]
                    kcs, kB, kC, kD = bk[bnk[0]], bk[bnk[1]], bk[bnk[2]], bk[bnk[3]]
                    hf, hg, hk, E12, E3, E4 = W["hf"], W["hg"], W["hk"], W["E12"], W["E3"], W["E4"]
                    qdz_, qm, km, kl, attm, gg, yb, hst = (W["qdz"], W["qm"], W["km"], W["kl"], W["attm"], W["gg"],
                                                           W["yb"], W["hst"])
                    t_hf, t_hg, t_hk, t_E12, t_E3, t_E4 = (W["t_hf"], W["t_hg"], W["t_hk"], W["t_E12"], W["t_E3"],
                                                           W["t_E4"])
                    t_qdz_, t_qm, t_km, t_kl, t_attm, t_gg, t_yb, t_hst = (W["t_qdz"], W["t_qm"], W["t_km"], W["t_kl"],
                                                                          W["t_attm"], W["t_gg"], W["t_yb"], W["t_hst"])
                    tsl = slice(t4 * 128, (t4 + 1) * 128)
                    sig_rg = sgf[hh][:, t4, 0:128]
                    sig_rf = sgf[hh][:, t4, 128:256]
                    tt("dve", hf, sig_rf, omlrow[:, hh, :], ALU.mult, [t_sgf[hh][t4], t_lbrow], [t_hf])
                    tt("dve", hf, hf, lbrow[:, hh, :], ALU.add, [t_hf, t_lbrow], [t_hf])
                    yield
                    act(hg, hf, AF.Ln, [t_hf], [t_hg])
                    ts("dve", hk, hf, -1.0, 1.0, ALU.mult, ALU.add, [t_hf], [t_hk])
                    tt("pool", gg, sig_rg, gnrow[:, hh, :], ALU.mult, [t_sgf[hh][t4], t_gnr], [t_gg])
                    yield
                    mm(cs[:, 0:128], hg, U32, True, True, [t_hg, t_c32], [kcs], signal=False)
                    mm(cs[:, 128:256], hg, Um32, True, True, [t_hg, t_c32], [kcs], signal=False)
                    mm(cs[:, 256:384], L32, hg, True, True, [t_hg, t_c32], [kcs])
                    yield
                    act(E12, cs[:, 0:256], AF.Exp, [kcs], [t_E12])
                    act(E3, cs[:, 128:256], AF.Exp, [kcs], [t_E3], scale=-1.0)
                    act(E4, cs[:, 256:384], AF.Exp, [kcs], [t_E4])
                    yield
                    tt("dve", qdz_[:, 0, 0:64], rqT[hh][:, t4 * 128:t4 * 128 + 64], E12[:, 0:64], ALU.mult,
                       [t_rqT[hh], t_E12], [t_qdz_])
                    tt("dve", qdz_[:, 1, 64:128], rqT[hh][:, t4 * 128 + 64:t4 * 128 + 128], E12[:, 64:128], ALU.mult,
                       [t_rqT[hh], t_E12], [t_qdz_])
                    tt("pool", qm, rqT[hh][:, tsl], E12[:, 128:256], ALU.mult, [t_rqT[hh], t_E12], [t_qm])
                    stt("dve", km, snT[hh][:, tsl], omlcol[:, hh:hh + 1], E3, ALU.mult, ALU.mult,
                        [t_snT[hh], t_lbcol, t_E3], [t_km])
                    tt("pool", kl, hk, E4, ALU.mult, [t_hk, t_E4], [t_kl])
                    yield
                    mm(bB[:, 0:128], km, qm, True, True, [t_km, t_qm], [kB])
                    yield
                    tt("dve", attm, bB[:, 0:128], maskU_bf, ALU.mult, [kB, t_mku], [t_attm])
                    yield
                    vt = vtok[hh][:, t4, :]
                    mm(bC[:, 0:128], attm, vt, True, False, [t_attm, t_vtok[hh][t4]], [kC], signal=False)
                    mm(bC[:, 0:128], qdz_[:, 0, :], Sbf[hh][0], False, False, [t_qdz_, t_Sbf[hh][0]], [kC], signal=False)
                    mm(bD[:, 0:128], kl[0:64, :], vt[0:64, :], True, True, [t_kl, t_vtok[hh][t4]], [kD],
                       tile_position=(0, 0))
                    yield
                    stt("dve", Sbf[hh][1], S32[hh], E12[:, 63:64], bD[:, 0:128], ALU.mult, ALU.add,
                        [t_S32[hh], t_E12, kD], [t_Sbf[hh][1]])
                    stt("dve", S32[hh], S32[hh], E12[:, 63:64], bD[:, 0:128], ALU.mult, ALU.add,
                        [t_S32[hh], t_E12, kD], [t_S32[hh]])
                    yield
                    mm(bC[:, 0:128], qdz_[:, 1, :], Sbf[hh][1], False, True, [t_qdz_, t_Sbf[hh][1]], [kC])
                    mm(bD[:, 128:256], kl[64:128, :], vt[64:128, :], True, True, [t_kl, t_vtok[hh][t4]], [kD],
                       tile_position=(64, 0))
                    yield
                    stt("dve", Sbf[hh][0], S32[hh], E12[:, 127:128], bD[:, 128:256], ALU.mult, ALU.add,
                        [t_S32[hh], t_E12, kD], [t_Sbf[hh][0]])
                    stt("dve", S32[hh], S32[hh], E12[:, 127:128], bD[:, 128:256], ALU.mult, ALU.add,
                        [t_S32[hh], t_E12, kD], [t_S32[hh]])
                    act(attm, bC[:, 0:128], AF.Square, [kC], [t_attm, t_hst], accum_out=hst[:, 0:1])
                    yield
                    act(hst[:, 1:2], hst[:, 0:1], AF.Ln, [t_hst], [t_hst], scale=1.0 / 128, bias=EPS)
                    act(hst[:, 2:3], hst[:, 1:2], AF.Exp, [t_hst], [t_hst], scale=-0.5)
                    yield
                    stt("dve", yb, bC[:, 0:128], hst[:, 2:3], gg, ALU.mult, ALU.mult, [kC, t_hst, t_gg], [t_yb])
                    yield
                    ytp = bD[:, 256:320].bitcast(BF16)
                    tr(ytp, yb, ident_bf, [t_yb, t_idb], [kD])
                    yield
                    cp("act", ys[:, hh * 2 + 1, tsl], ytp, [kD], [t_ys[hh * 2 + 1]])

                for t4 in range(4):
                    alive = [hgrn_gen(0, t4, HW[0], (2, 3, 4, 5)), hgrn_gen(1, t4, HW[1], (6, 7, 0, 1))]
                    while alive:
                        for g_ in list(alive):
                            try:
                                next(g_)
                            except StopIteration:
                                alive.remove(g_)
                for e_ in ("act", "dve"):
                    P.wait_all(e_, hw1_trks)
                checkpoint("hgrn%d" % blk)
                t_exb = Trk()
                P.dma("sp", ex_in.ap()[blk].rearrange("(a p) t -> p a t", p=128), ys,
                      d_yst[0], reads=t_ys, writes=[t_exb])
                P.wait_all("pool", [t_exb])
                if cc_n[0] > 0:
                    P._wait_for("pool", ("dma", ccsd, cc_n[0]))
                cc_n[0] += 1
                P.raw("pool", lambda e, blk=blk: e.collective_compute(
                    "AllGather", ALU.bypass, replica_groups=[[0, 1, 2, 3], [4, 5, 6, 7]],
                    ins=[ex_in.ap()[blk]], outs=[ex_out.ap()[blk]]).then_inc(ccs, 1))
                t_exout.w = ("dma", ccsd, cc_n[0])

            checkpoint("p1")
            if DEBUG:
                P.dma("sp", dbg_y, ex_in.ap(), P.dsem("dbgy"), reads=[t_exout])
            checkpoint("ag")
            P.barrier()
            AR.off = base_off

            uTo = AR.alloc([128, 16, NTOK], BF16)
            t_uTo = [Trk() for _ in range(8)]
            hole_off = AR.off
            mT = AR.alloc([128, 16, NTOK], BF16)
            t_mT = [[Trk() for _ in range(2)] for _ in range(16)]
            r2_off = AR.off
            yT = AR.alloc([128, 2, 8, NTOK], BF16); t_yT = Trk()
            d_y = P.dsem("d_y")
            P.wait_all("sp", [t_joff, t_exout])
            ex_v = ex_out.ap().rearrange("b (r hh k p) t -> p k (r hh) b t", r=4, hh=2, k=2)
            reg_holder = {}

            def _ld_reg(e):
                for b2 in range(2):
                    reg = e.alloc_register("joff%d" % b2)
                    e.reg_load(reg, joff_sb[0:1, b2:b2 + 1])
                    reg_holder[b2] = e.snap(reg, min_val=0, max_val=7)

            P.raw("sp", _ld_reg)
            for kind in range(2):
                for b2 in range(2):
                    d_y["val"] += 16
                    P.raw("sp", lambda e, kind=kind, b2=b2: e.dma_start(
                        out=yT[:, kind, :, b2 * 512:(b2 + 1) * 512],
                        in_=ex_v[:, kind, :, reg_holder[b2], :]).then_inc(d_y["sem"], 16))
            t_yT.w = ("dma", d_y, d_y["val"])

            checkpoint("exch")
            xs2 = [AR.alloc([128, D], F32) for _ in range(2)]
            t_xs2 = [Trk(), Trk()]
            u_bfs = [AR.alloc([128, D], BF16) for _ in range(2)]; t_us = [Trk(), Trk()]
            stats = [AR.alloc([128, 8], F32) for _ in range(2)]; t_stats = [Trk(), Trk()]

            def _s1o(T_):
                norm_s1(T_, xo[T_ * 128:(T_ + 1) * 128, :], xs2[T_ % 2], t_xs2[T_ % 2], d_xs[T_ % 2])

            _s1o(0)
            for T in range(8):
                if T + 1 < 8:
                    _s1o(T + 1)
                norm_s2(T, uTo[:, :, T * 128:(T + 1) * 128], t_uTo[T])
            checkpoint("s0")
            wga = [AR.alloc([128, 16, 256], BF16) for _ in range(2)]
            wgr = [AR.alloc([128, 16, 256], BF16) for _ in range(2)]
            wao = [AR.alloc([128, 8, 256], BF16) for _ in range(2)]
            wro = [AR.alloc([128, 8, 256], BF16) for _ in range(2)]
            t_wa = [Trk(), Trk()]
            d_wa = [P.dsem("d_wa0"), P.dsem("d_wa1")]
            sga = AR.alloc([128, 512], F32); sgr = AR.alloc([128, 512], F32)
            t_sga, t_sgr = Trk(), Trk()
            print("phase2a arena bytes", AR.off)
            wg_v = wg_d.rearrange("(kt p) n -> p kt n", p=128)
            wao_v = wao_d.rearrange("(kt p) n -> p kt n", p=128)
            wro_v = wro_d.rearrange("(kt p) n -> p kt n", p=128)
            def _ld_a(cg_):
                s_ = cg_ % 2
                c0_ = cg_ * 256
                P.dma("pool", wga[s_], wg_v[:, :, c0_:c0_ + 256], d_wa[s_], writes=[t_wa[s_]])
                P.dma("pool", wgr[s_], wg_v[:, :, 2048 + c0_:2048 + c0_ + 256], d_wa[s_], writes=[t_wa[s_]])
                P.dma("pool", wao[s_], wao_v[:, :, c0_:c0_ + 256], d_wa[s_], writes=[t_wa[s_]])
                P.dma("pool", wro[s_], wro_v[:, :, c0_:c0_ + 256], d_wa[s_], writes=[t_wa[s_]])

            _ld_a(0)
            for cg in range(8):
                s = cg % 2
                if cg + 1 < 8:
                    _ld_a(cg + 1)
                for c2 in range(2):
                    ct = cg * 2 + c2
                    for tb in range(2):
                        tks = slice(tb * 512, (tb + 1) * 512)
                        t_u4 = t_uTo[tb * 4:(tb + 1) * 4]
                        b_ga, b_gr, b_a, b_r = 0 + 4 * tb, 1 + 4 * tb, 2 + 4 * tb, 3 + 4 * tb
                        fm_proj(wga[s], t_wa[s], c2 * 128, lambda kt: uTo[:, kt, tks], t_u4, b_ga)
                        fm_proj(wgr[s], t_wa[s], c2 * 128, lambda kt: uTo[:, kt, tks], t_u4, b_gr)
                        fm_proj(wao[s], t_wa[s], c2 * 128, lambda kt: yT[:, 0, kt, tks], [t_yT], b_a, nk=8)
                        fm_proj(wro[s], t_wa[s], c2 * 128, lambda kt: yT[:, 1, kt, tks], [t_yT], b_r, nk=8)
                        act(sga, banks[b_ga], AF.Sigmoid, [bk[b_ga]], [t_sga])
                        act(sgr, banks[b_gr], AF.Sigmoid, [bk[b_gr]], [t_sgr])
                        tt("dve", sga, banks[b_a], sga, ALU.mult, [bk[b_a], t_sga], [t_sga])
                        tt("dve", sgr, banks[b_r], sgr, ALU.mult, [bk[b_r], t_sgr], [t_sgr])
                        tt("dve", mT[:, ct, tks], sga, sgr, ALU.add, [t_sga, t_sgr], [t_mT[ct][tb]])
            P.barrier()
            AR.off = r2_off
            checkpoint("sa")
            h = AR.alloc([128, 8, D], F32)
            t_h = [Trk() for _ in range(8)]
            d_h = P.dsem("d_h")
            for T in range(8):
                P.dma("sp", h[:, T, :], xo[T * 128:(T + 1) * 128, :], d_h, writes=[t_h[T]])
            for T in range(8):
                t_h[T].w = ("dma", d_h, d_h["val"])
            Grow = AR.alloc([128, D], F32); t_Grow = Trk()
            load_row(Grow, modflat[0:1, 4096:6144], None, [t_Grow], [t_modout])
            r3_off = AR.off
            wos = [AR.alloc([128, 16, 512], BF16) for _ in range(2)]
            t_wos = [Trk(), Trk()]
            d_wos = [P.dsem("d_wos0"), P.dsem("d_wos1")]
            print("phase2b arena bytes", AR.off)
            wo_v = wo_d.rearrange("(kt p) n -> p kt n", p=128)
            def _ld_b(cb_):
                s_ = cb_ % 2
                cols_ = slice(cb_ * 512, (cb_ + 1) * 512)
                P.dma("pool", wos[s_], wo_v[:, :, cols_], d_wos[s_], writes=[t_wos[s_]])

            def _sc_b(cb_):
                s_ = cb_ % 2
                cols_ = slice(cb_ * 512, (cb_ + 1) * 512)
                for ct_ in range(16):
                    tt("dve", wos[s_][:, ct_, :], wos[s_][:, ct_, :], Grow[:, cols_], ALU.mult,
                       [t_wos[s_], t_Grow], [t_wos[s_]])

            _ld_b(0)
            for cb in range(4):
                s = cb % 2
                cols = slice(cb * 512, (cb + 1) * 512)
                if cb + 1 < 4:
                    _ld_b(cb + 1)
                _sc_b(cb)
                for T in range(8):
                    b_ = T % 4
                    for ct in range(16):
                        mm(banks[b_][:, :], mT[:, ct, T * 128:(T + 1) * 128], wos[s][:, ct, :], ct == 0, ct == 15,
                           [t_mT[ct][T // 4], t_wos[s]], [bk[b_]], signal=(ct == 15))
                    tt("dve", h[:, T, cols], h[:, T, cols], banks[b_], ALU.add, [t_h[T], bk[b_]], [t_h[T]])
            P.barrier()
            checkpoint("sb")
            load_row(Arow, modflat[0:1, 8192:10240], None, [t_Arow], [t_modout])
            load_row(Brow, n2g_d, None, [t_Brow], [])
            stt("dve", Arow, Arow, 1.0, Brow, ALU.add, ALU.mult, [t_Arow, t_Brow], [t_Arow])
            load_row(Brow, modflat[0:1, 6144:8192], None, [t_Brow], [t_modout])
            AR.off = hole_off
            aTs = [AR.alloc([128, 4, NTOK], BF16) for _ in range(2)]
            t_aTs = [[[Trk() for _ in range(2)] for _ in range(4)] for _ in range(2)]
            wd = [AR.alloc([128, 4, 512], BF16) for _ in range(2)]
            t_wd = [Trk(), Trk()]
            d_wd = [P.dsem("d_wd0"), P.dsem("d_wd1")]
            stats = [AR.alloc([128, 8], F32) for _ in range(2)]; t_stats = [Trk(), Trk()]
            assert AR.off <= r2_off, (AR.off, r2_off)
            AR.off = r3_off
            tmpxs = [AR.alloc([128, D], F32) for _ in range(2)]; t_tmpxs = [Trk(), Trk()]
            u_bfs = [AR.alloc([128, D], BF16) for _ in range(2)]; t_us = [Trk(), Trk()]

            def norm_h(T):
                st_, t_st = stats[T % 2], t_stats[T % 2]
                ub, t_ub = u_bfs[T % 2], t_us[T % 2]
                act(ub, h[:, T, :], AF.Square, [t_h[T]], [t_ub, t_st], accum_out=st_[:, 0:1])
                act(st_[:, 1:2], st_[:, 0:1], AF.Sqrt, [t_st], [t_st], scale=1.0 / D, bias=EPS)
                recip(st_[:, 2:3], st_[:, 1:2], [t_st], [t_st])

            def _c1(T):
                norm_h(T)
                stt("dve", tmpxs[T % 2], h[:, T, :], stats[T % 2][:, 2:3], Arow, ALU.mult, ALU.mult,
                    [t_h[T], t_stats[T % 2], t_Arow], [t_tmpxs[T % 2]])
                tt("pool", u_bfs[T % 2], tmpxs[T % 2], Brow, ALU.add, [t_tmpxs[T % 2], t_Brow], [t_us[T % 2]])

            _c1(0)
            for T in range(8):
                if T + 1 < 8:
                    _c1(T + 1)
                norm_s2(T, uTo[:, :, T * 128:(T + 1) * 128], t_uTo[T])
            P.wait_all("pool", t_tmpxs + t_us)
            checkpoint("sc")
            load_row(Grow, modflat[0:1, 10240:12288], None, [t_Grow], [t_modout])
            AR.off = r3_off
            wgu = [AR.alloc([128, 2, 16, 256], BF16) for _ in range(2)]
            t_wgu = [Trk(), Trk()]
            d_wgu = [P.dsem("d_wgu0"), P.dsem("d_wgu1")]
            sil = [AR.alloc([128, 512], F32) for _ in range(2)]
            t_sil = [Trk(), Trk()]
            print("phase2d arena bytes", AR.off)
            wfg_v = wfg_d.rearrange("(kt p) n -> p kt n", p=128)
            wfu_v = wfu_d.rearrange("(kt p) n -> p kt n", p=128)
            wfd_v = wfd_d.rearrange("(ht p) n -> p ht n", p=128)
            stages = []
            gcount = 0
            dcount = 0
            for grp in range(11):
                for sg in range(2):
                    stages.append(("gu", grp, sg, gcount % 2))
                    gcount += 1
                for cb in range(4):
                    stages.append(("dn", grp, cb, dcount % 2))
                    dcount += 1

            def _ld_d(st_):
                kind_, grp_, x_, s_ = st_
                if kind_ == "gu":
                    hc0 = (grp_ * 4 + x_ * 2) * 128
                    P.dma("pool", wgu[s_][:, 0, :, :], wfg_v[:, :, hc0:hc0 + 256], d_wgu[s_], writes=[t_wgu[s_]])
                    P.dma("pool", wgu[s_][:, 1, :, :], wfu_v[:, :, hc0:hc0 + 256], d_wgu[s_], writes=[t_wgu[s_]])
                else:
                    cols_ = slice(x_ * 512, (x_ + 1) * 512)
                    P.dma("pool", wd[s_], wfd_v[:, grp_ * 4:(grp_ + 1) * 4, cols_], d_wd[s_], writes=[t_wd[s_]])

            def _sc_d(st_):
                kind_, grp_, x_, s_ = st_
                if kind_ == "dn":
                    cols_ = slice(x_ * 512, (x_ + 1) * 512)
                    for hl_ in range(4):
                        tt("dve", wd[s_][:, hl_, :], wd[s_][:, hl_, :], Grow[:, cols_], ALU.mult,
                           [t_wd[s_], t_Grow], [t_wd[s_]])

            def _cmp_d(st_):
                kind_, grp_, x_, s = st_
                if kind_ == "gu":
                    sg = x_
                    for gi in range(2):
                        hl = sg * 2 + gi
                        for tb in range(2):
                            tks = slice(tb * 512, (tb + 1) * 512)
                            t_u4 = t_uTo[tb * 4:(tb + 1) * 4]
                            b_g = 4 + 2 * tb
                            b_u = 5 + 2 * tb
                            for kt in range(16):
                                mm(banks[b_g][:, :], wgu[s][:, 0, kt, gi * 128:(gi + 1) * 128], uTo[:, kt, tks],
                                   kt == 0, kt == 15, [t_wgu[s]] + t_u4, [bk[b_g]], signal=(kt == 15))
                            for kt in range(16):
                                mm(banks[b_u][:, :], wgu[s][:, 1, kt, gi * 128:(gi + 1) * 128], uTo[:, kt, tks],
                                   kt == 0, kt == 15, [t_wgu[s]] + t_u4, [bk[b_u]], signal=(kt == 15))
                            act(sil[tb], banks[b_g], AF.Silu, [bk[b_g]], [t_sil[tb]])
                            tt("dve", aTs[grp_ % 2][:, hl, tks], banks[b_u], sil[tb], ALU.mult, [bk[b_u], t_sil[tb]],
                               [t_aTs[grp_ % 2][hl][tb]])
                else:
                    cols = slice(x_ * 512, (x_ + 1) * 512)
                    for T in range(8):
                        b_ = T % 4
                        for hl in range(4):
                            mm(banks[b_][:, :], aTs[grp_ % 2][:, hl, T * 128:(T + 1) * 128], wd[s][:, hl, :], hl == 0,
                               hl == 3, [t_aTs[grp_ % 2][hl][T // 4], t_wd[s]], [bk[b_]], signal=(hl == 3))
                        tt("dve", h[:, T, cols], h[:, T, cols], banks[b_], ALU.add, [t_h[T], bk[b_]], [t_h[T]])

            gu_list = [st_ for st_ in stages if st_[0] == "gu"]
            dn_list = [st_ for st_ in stages if st_[0] == "dn"]
            nxt = {"gu": 2, "dn": 2}
            lists = {"gu": gu_list, "dn": dn_list}
            for st_ in gu_list[:2] + dn_list[:2]:
                _ld_d(st_)
            for st_ in stages:
                _sc_d(st_)
                _cmp_d(st_)
                k_ = st_[0]
                if nxt[k_] < len(lists[k_]):
                    _ld_d(lists[k_][nxt[k_]])
                    nxt[k_] += 1
            checkpoint("sd")
            load_row(Arow, fg_d, None, [t_Arow], [])
            if DEBUG:
                for T in range(8):
                    P.dma("sp", dbg_h[T * 128:(T + 1) * 128, :], h[:, T, :], P.dsem("dbgh%d" % T), reads=[t_h[T]])
            for T in range(8):
                norm_h(T)
                stt("dve", h[:, T, :], h[:, T, :], stats[T % 2][:, 2:3], Arow, ALU.mult, ALU.mult,
                    [t_h[T], t_stats[T % 2], t_Arow], [t_h[T]])
                P.dma("sp", out_d[T * 128:(T + 1) * 128, :], h[:, T, :], dout, reads=[t_h[T]])

        except _Stop:
            P.barrier()
        if DEBUG:
            for d_ in P.dma_sems:
                if d_["name"].startswith("dbg"):
                    P._wait_for("sp", ("dma", d_, d_["val"]))
        P.raw("sp", lambda e: e.wait_ge(dout["sem"], dout["val"]))
        P.finish()
        print("instructions", P.ninstr, "waits", P.nwaits, "counts", P.cnt)
    return nc


def _consts():
    idx = np.arange(128)
    ch = idx // 64
    same = ch[:, None] == ch[None, :]
    ident = np.eye(128, dtype=np.float32)
    U = (same & (idx[:, None] <= idx[None, :])).astype(np.float32)
    mid = ch * 64 + 31
    Umid = (same & (idx[:, None] <= mid[None, :])).astype(np.float32)
    Um = U - Umid
    L = (same & (idx[:, None] > idx[None, :])).astype(np.float32)
    return np.ascontiguousarray(np.concatenate([ident, U, Um, L], axis=1))


def make_in_maps(x, c, w_mod, b_mod, norm1_g, w_in, lambda_q1, lambda_k1, lambda_q2, lambda_k2,
                 subln_g, lb_logits, gnorm_g, w_att_out, w_rec_out, w_o, norm2_g,
                 w_ffn_gate, w_ffn_up, w_ffn_down, final_g):
    f = lambda a: np.ascontiguousarray(np.asarray(a, dtype=np.float32))
    x = f(x); c = f(c); w_in0 = f(w_in)[0]; w_mod0 = f(w_mod)[0]; b_mod0 = f(b_mod)[0]
    lbl = f(lb_logits); gn = f(gnorm_g)[0]
    AQ, AK, AV, RQ, RF, RI, RG, GA = 0, 1024, 2048, 3072, 4096, 5120, 6144, 7168
    shared = {
        "w_g": np.ascontiguousarray(w_in0[:, GA:GA + 4096]),
        "w_ao": f(w_att_out)[0], "w_ro": f(w_rec_out)[0], "w_o": f(w_o)[0],
        "w_fg": f(w_ffn_gate)[0], "w_fu": f(w_ffn_up)[0], "w_fd": f(w_ffn_down)[0],
        "n1g": f(norm1_g)[0][None], "n2g": f(norm2_g)[0][None], "fg": f(final_g)[None],
        "subg": f(subln_g)[0][:, None],
        "lamv": np.concatenate([f(lambda_q1)[0], f(lambda_k1)[0], f(lambda_q2)[0], f(lambda_k2)[0]])[None],
        "consts": _consts(),
    }
    shared = {k: np.ascontiguousarray(v) for k, v in shared.items()}
    maps = []
    for core in range(8):
        b, j = core // 4, core % 4
        heads = (2 * j, 2 * j + 1)
        sl = lambda off, h: w_in0[:, off + h * 128: off + (h + 1) * 128]
        w_fm = np.concatenate([np.concatenate([sl(AQ, h), sl(AK, h), sl(RQ, h), sl(RF, h)], axis=1) for h in heads], axis=1)
        w_tm = np.concatenate([np.concatenate([sl(AV, h), sl(RI, h), sl(RG, h), sl(RF, h)], axis=1) for h in heads], axis=1)
        lbc = np.stack([np.stack([lbl[l, h * 128:(h + 1) * 128] for l in range(2)], axis=1) for h in heads], axis=1)
        lbr = np.stack([np.stack([lbl[l, h * 128:(h + 1) * 128] for l in range(2)], axis=0) for h in heads], axis=0)
        m = dict(shared)
        m.update({
            "xb": x[b],
            "xo": np.ascontiguousarray(x[b, j * NTOK:(j + 1) * NTOK]),
            "cT": np.ascontiguousarray(c[b].reshape(16, 128).T),
            "wmod": np.ascontiguousarray(w_mod0[:, j * 3072:(j + 1) * 3072]),
            "bmod": np.ascontiguousarray(b_mod0[j * 3072:(j + 1) * 3072][None]),
            "w_fm": np.ascontiguousarray(w_fm), "w_tm": np.ascontiguousarray(w_tm),
            "lbc": np.ascontiguousarray(lbc.reshape(128, 4)),
            "lbr": np.ascontiguousarray(lbr.reshape(1, 512)),
            "gnr": np.ascontiguousarray(np.stack([gn[h] for h in heads], axis=0).reshape(1, 256)),
            "joff": np.array([[2 * j, 2 * j + 1]], dtype=np.int32),
        })
        maps.append(m)
    return maps


_NC_CACHE = {}


def kernel(**inputs):
    in_maps = make_in_maps(**inputs)
    if "nc" not in _NC_CACHE:
        _NC_CACHE["nc"] = build_nc()
    nc = _NC_CACHE["nc"]
    res = run_bass_kernel_spmd(nc, in_maps, core_ids=list(range(8)))
    out = np.zeros((2, SEQ, D), dtype=np.float32)
    for core in range(8):
        b, j = core // 4, core % 4
        out[b, j * NTOK:(j + 1) * NTOK] = np.asarray(res.results[core]["out"], dtype=np.float32)
    if DEBUG:
        kernel.last = res
    return out
```

```python
import math
from contextlib import ExitStack

import numpy as np
import concourse.bass as bass
import concourse.mybir as mybir
from concourse.bass_utils import run_bass_kernel_spmd

F32 = mybir.dt.float32
BF16 = mybir.dt.bfloat16
I32 = mybir.dt.int32
U8 = mybir.dt.uint8
AF = mybir.ActivationFunctionType
ALU = mybir.AluOpType

D = 2048
SEQ = 4096
NTOK = 1024
FF = 5632
EPS = 1e-6
LAM_INIT = 0.8 - 0.6 * math.exp(-0.3 * 0)
ENGS = ("pe", "act", "dve", "pool", "sp")
DEBUG = False
STOP = None
SHAPE_OVR = {}


class Trk:
    __slots__ = ("w", "r")

    def __init__(self):
        self.w = None
        self.r = []


class Prog:
    def __init__(self, nc, stack):
        self.nc = nc
        self.stack = stack
        self.q = {e: [] for e in ENGS}
        self.cnt = {e: 0 for e in ENGS}
        self.sem = {e: stack.enter_context(nc.semaphore("s_" + e)) for e in ENGS}
        self.seen = {e: {} for e in ENGS}
        self.dma_sems = []
        self.ninstr = 0
        self.nwaits = 0

    def ps(self, name, shape, dt):
        return self.stack.enter_context(self.nc.psum_tensor(name, list(shape), dt))

    def dsem(self, name):
        s = self.stack.enter_context(self.nc.semaphore(name))
        d = {"sem": s, "val": 0, "name": name}
        self.dma_sems.append(d)
        return d

    def _wait_for(self, eng, ev):
        if ev is None:
            return
        kind, key, val = ev
        if kind == "eng" and key == "pe" and eng == "pe":
            return
        if kind == "eng":
            semkey = "E" + key
            sem = self.sem[key]
        else:
            semkey = "D" + key["name"]
            sem = key["sem"]
        if self.seen[eng].get(semkey, 0) >= val:
            return
        self.seen[eng][semkey] = val
        self.nwaits += 1
        self.q[eng].append(lambda e, sem=sem, val=val: e.wait_ge(sem, val))

    def _deps(self, eng, reads, writes):
        for t in reads:
            self._wait_for(eng, t.w)
        for t in writes:
            self._wait_for(eng, t.w)
            for ev in t.r:
                self._wait_for(eng, ev)

    def _commit(self, ev, reads, writes):
        for t in reads:
            t.r.append(ev)
            if len(t.r) > 48:
                last = {}
                for e_ in t.r:
                    k = (e_[0], e_[1] if e_[0] == "eng" else e_[1]["name"])
                    if k not in last or last[k][2] < e_[2]:
                        last[k] = e_
                t.r = list(last.values())
        for t in writes:
            t.w = ev
            t.r = []

    def op(self, eng, fn, reads=(), writes=(), signal=True):
        self._deps(eng, reads, writes)
        self.ninstr += 1
        if signal:
            self.cnt[eng] += 1
            sem = self.sem[eng]
            self.q[eng].append(lambda e, fn=fn, sem=sem: fn(e).then_inc(sem, 1))
            ev = ("eng", eng, self.cnt[eng])
        else:
            assert eng == "pe"
            self.q[eng].append(lambda e, fn=fn: fn(e))
            ev = ("eng", eng, self.cnt[eng] + 1)
        self._commit(ev, reads, writes)
        return ev

    def dma(self, eng, out, in_, ds, reads=(), writes=(), **kw):
        self._deps(eng, reads, writes)
        self.ninstr += 1
        ds["val"] += 16
        sem = ds["sem"]
        self.q[eng].append(
            lambda e, out=out, in_=in_, sem=sem, kw=kw: e.dma_start(out=out, in_=in_, **kw).then_inc(sem, 16))
        ev = ("dma", ds, ds["val"])
        self._commit(ev, reads, writes)
        return ev

    def raw(self, eng, fn):
        self.q[eng].append(fn)

    def wait_all(self, eng, trks):
        for t in trks:
            self._wait_for(eng, t.w)
            for ev in t.r:
                self._wait_for(eng, ev)

    def barrier(self):
        for e in ENGS:
            for e2 in ENGS:
                if e2 != e and self.cnt[e2] > 0:
                    self._wait_for(e, ("eng", e2, self.cnt[e2]))
            for d in self.dma_sems:
                if d["val"] > 0:
                    self._wait_for(e, ("dma", d, d["val"]))

    def finish(self):
        nc = self.nc
        q = self.q
        with nc.Block() as block:
            @block.tensor
            def _(e):
                for f in q["pe"]:
                    f(e)

            @block.scalar
            def _(e):
                for f in q["act"]:
                    f(e)

            @block.vector
            def _(e):
                for f in q["dve"]:
                    f(e)

            @block.gpsimd
            def _(e):
                for f in q["pool"]:
                    f(e)

            @block.sync
            def _(e):
                for f in q["sp"]:
                    f(e)


class Arena:
    def __init__(self, ap_u8, size):
        self.ap = ap_u8
        self.size = size
        self.off = 0

    def alloc(self, shape, dt, parts=None):
        esz = {F32: 4, BF16: 2, I32: 4}[dt]
        n = 1
        for s in shape[1:]:
            n *= s
        nb = (n * esz + 31) // 32 * 32
        assert self.off + nb <= self.size, ("arena overflow", self.off, nb, self.size)
        v = self.ap[0:shape[0], self.off:self.off + n * esz].bitcast(dt)
        self.off += nb
        if len(shape) == 3:
            v = v.rearrange("p (a b) -> p a b", a=shape[1])
        elif len(shape) == 4:
            v = v.rearrange("p (a b c) -> p a b c", a=shape[1], b=shape[2])
        return v


def build_nc():
    nc = bass.Bass("TRN2", target_bir_lowering=False)

    def din(name, shape, dt=F32):
        shape = SHAPE_OVR.get(name, shape)
        return nc.dram_tensor(name, list(shape), dt, kind="ExternalInput").ap()

    xb = din("xb", [SEQ, D])
    xo = din("xo", [NTOK, D])
    cT_d = din("cT", [128, 16])
    wmod_d = din("wmod", [D, 3072])
    bmod_d = din("bmod", [1, 3072])
    wfm_d = din("w_fm", [D, 1024])
    wtm_d = din("w_tm", [D, 1024])
    wg_d = din("w_g", [D, 4096])
    wao_d = din("w_ao", [1024, D])
    wro_d = din("w_ro", [1024, D])
    wo_d = din("w_o", [D, D])
    wfg_d = din("w_fg", [D, FF])
    wfu_d = din("w_fu", [D, FF])
    wfd_d = din("w_fd", [FF, D])
    n1g_d = din("n1g", [1, D])
    n2g_d = din("n2g", [1, D])
    fg_d = din("fg", [1, D])
    lbc_d = din("lbc", [128, 4])
    lbr_d = din("lbr", [1, 512])
    gnr_d = din("gnr", [1, 256])
    subg_d = din("subg", [128, 1])
    lamv_d = din("lamv", [1, 256])
    consts_d = din("consts", [128, 512])
    joff_d = din("joff", [1, 2], I32)
    out_d = nc.dram_tensor("out", [NTOK, D], F32, kind="ExternalOutput").ap()
    if DEBUG:
        dbg_y = nc.dram_tensor("dbg_y", [8, 512, 512], BF16, kind="ExternalOutput").ap()
        dbg_h = nc.dram_tensor("dbg_h", [NTOK, D], F32, kind="ExternalOutput").ap()
        dbg_mod = nc.dram_tensor("dbg_mod", [4, 3072], F32, kind="ExternalOutput").ap()

    mod_in = nc.dram_tensor("mod_in", [1, 3072], F32)
    mod_out = nc.dram_tensor("mod_out", [4, 3072], F32)
    ex_in = nc.dram_tensor("ex_in", [8, 512, 512], BF16)
    ex_out = nc.dram_tensor("ex_out", [8, 2048, 512], BF16)
    t_modin, t_modout, t_exin, t_exout = Trk(), Trk(), Trk(), Trk()

    with ExitStack() as st:
        P = Prog(nc, st)
        ARENA_SZ = 206 * 1024
        arena_t = st.enter_context(nc.sbuf_tensor("arena", [128, ARENA_SZ], U8))
        AR = Arena(arena_t, ARENA_SZ)
        banks = [P.ps("bank%d" % i, [128, 512], F32)[:, :] for i in range(8)]
        bk = [Trk() for _ in range(8)]
        dout = P.dsem("dout")
        ccs = st.enter_context(nc.semaphore("ccs"))
        cc_n = [0]
        ccsd = {"sem": ccs, "val": 0, "name": "ccs"}

        class _Stop(Exception):
            pass

        def checkpoint(name):
            if STOP == name:
                raise _Stop()

        def act(out, in_, func, R, W, **kw):
            P.op("act", lambda e: e.activation(out=out, in_=in_, func=func, **kw), R, W)

        def tt(eng, out, in0, in1, op, R, W):
            P.op(eng, lambda e: e.tensor_tensor(out=out, in0=in0, in1=in1, op=op), R, W)

        def ts(eng, out, in0, s1, s2, op0, op1, R, W):
            P.op(eng, lambda e: e.tensor_scalar(out=out, in0=in0, scalar1=s1, scalar2=s2, op0=op0, op1=op1), R, W)

        def stt(eng, out, in0, scalar, in1, op0, op1, R, W):
            P.op(eng, lambda e: e.scalar_tensor_tensor(out=out, in0=in0, scalar=scalar, in1=in1, op0=op0, op1=op1),
                 R, W)

        def cp(eng, out, in_, R, W):
            if eng == "act":
                act(out, in_, AF.Copy, R, W)
            else:
                P.op(eng, lambda e: e.tensor_copy(out=out, in_=in_), R, W)

        def recip(out, in_, R, W):
            P.op("dve", lambda e: e.reciprocal(out=out, in_=in_), R, W)

        def mm(out, lhsT, rhs, start, stop, R, W, signal=True, **kw):
            P.op("pe", lambda e: e.matmul(out, lhsT=lhsT, rhs=rhs, start=start, stop=stop, **kw), R, W, signal=signal)

        def tr(out, in_, ident, R, W, signal=True):
            P.op("pe", lambda e: e.transpose(out=out, in_=in_, identity=ident), R, W, signal=signal)

        def memset(eng, ap, val, W):
            P.op(eng, lambda e: e.memset(ap, val), (), W)

        def collective(kind, groups, src, dst, R, W):
            P.wait_all("pool", R + W)
            cc_n[0] += 1
            n = cc_n[0]
            P.raw("pool", lambda e: e.collective_compute(kind, ALU.bypass, replica_groups=groups,
                                                         ins=[src], outs=[dst]).then_inc(ccs, 1))
            P.raw("pool", lambda e: e.wait_ge(ccs, n))
            ev = P.op("pool", lambda e: e.memset(cc_dummy[:], 0.0), (), [t_ccd])
            for t in W:
                t.w = ev
                t.r = []

        try:
            cc_dummy = AR.alloc([128, 8], F32)
            t_ccd = Trk()
            consts32 = AR.alloc([128, 4, 128], F32); t_c32 = Trk()
            ident_bf = AR.alloc([128, 128], BF16); t_idb = Trk()
            maskU_bf = AR.alloc([128, 128], BF16); t_mku = Trk()
            ones_bf = AR.alloc([128, 128], BF16); t_ones = Trk()
            d_c = P.dsem("d_c")
            P.dma("sp", consts32.rearrange("p a b -> p (a b)"), consts_d, d_c, writes=[t_c32])
            cp("dve", ident_bf, consts32[:, 0, :], [t_c32], [t_idb])
            cp("dve", maskU_bf, consts32[:, 1, :], [t_c32], [t_mku])
            memset("pool", ones_bf, 1.0, [t_ones])
            ones32 = AR.alloc([128, 128], F32); t_ones32 = Trk()
            memset("pool", ones32, 1.0, [t_ones32])
            U32 = consts32[:, 1, :]
            Um32 = consts32[:, 2, :]
            L32 = consts32[:, 3, :]

            lbc = AR.alloc([128, 4], F32); t_lbc = Trk()
            lbr = AR.alloc([128, 512], F32); t_lbr = Trk()
            gnr = AR.alloc([128, 256], F32); t_gnr = Trk()
            subg = AR.alloc([128, 1], F32); t_subg = Trk()
            lamv = AR.alloc([128, 256], F32); t_lamv = Trk()
            d_sm = P.dsem("d_sm")
            P.dma("sp", lbc, lbc_d, d_sm, writes=[t_lbc])
            P.dma("sp", lbr.unsqueeze(1), lbr_d.partition_broadcast(128), d_sm, writes=[t_lbr])
            P.dma("sp", gnr.unsqueeze(1), gnr_d.partition_broadcast(128), d_sm, writes=[t_gnr])
            P.dma("sp", subg, subg_d, d_sm, writes=[t_subg])
            P.dma("sp", lamv.unsqueeze(1), lamv_d.partition_broadcast(128), d_sm, writes=[t_lamv])
            joff_sb = AR.alloc([1, 8], I32); t_joff = Trk()
            P.dma("sp", joff_sb[0:1, 0:2], joff_d, d_sm, writes=[t_joff])
            cT = AR.alloc([128, 16], F32); t_cT = Trk()
            P.dma("sp", cT, cT_d, d_sm, writes=[t_cT])
            for t_ in (t_lbc, t_lbr, t_gnr, t_subg, t_lamv, t_joff, t_cT):
                t_.w = ("dma", d_sm, d_sm["val"])
            lbcol = AR.alloc([128, 2], F32)
            omlcol = AR.alloc([128, 2], F32)
            dcol = AR.alloc([128, 2], F32)
            t_lbcol = Trk()
            lbc3 = lbc.rearrange("p (h l) -> p h l", h=2)
            tt("dve", dcol, lbc3[:, :, 0], lbc3[:, :, 1], ALU.subtract, [t_lbc], [t_lbcol])
            act(lbcol, dcol, AF.Sigmoid, [t_lbcol], [t_lbcol])
            act(omlcol, dcol, AF.Sigmoid, [t_lbcol], [t_lbcol], scale=-1.0)
            lbrow = AR.alloc([128, 2, 128], F32)
            omlrow = AR.alloc([128, 2, 128], F32)
            drow = AR.alloc([128, 2, 128], F32)
            t_lbrow = Trk()
            lbr4 = lbr.rearrange("p (h l d) -> p h l d", h=2, l=2)
            tt("dve", drow, lbr4[:, :, 0, :], lbr4[:, :, 1, :], ALU.subtract, [t_lbr], [t_lbrow])
            act(lbrow, drow, AF.Sigmoid, [t_lbrow], [t_lbrow])
            act(omlrow, drow, AF.Sigmoid, [t_lbrow], [t_lbrow], scale=-1.0)
            gnrow = gnr.rearrange("p (h e) -> p h e", h=2)
            gsub = AR.alloc([128, 1], F32); t_gsub = Trk()
            ts("dve", gsub, subg, 1.0 - LAM_INIT, 0.0, ALU.mult, ALU.add, [t_subg], [t_gsub])
            lam_t = AR.alloc([128, 8], F32); t_lam = Trk()
            lprod = AR.alloc([128, 2, 64], F32)
            lv = lamv.rearrange("p (a d) -> p a d", a=4)
            tt("dve", lprod[:, 0, :], lv[:, 0, :], lv[:, 1, :], ALU.mult, [t_lamv], [t_lam])
            tt("dve", lprod[:, 1, :], lv[:, 2, :], lv[:, 3, :], ALU.mult, [t_lamv], [t_lam])
            P.op("dve", lambda e: e.tensor_reduce(out=lam_t[:, 0:2], in_=lprod, axis=mybir.AxisListType.X, op=ALU.add),
                 [t_lam], [t_lam])
            act(lam_t[:, 2:4], lam_t[:, 0:2], AF.Exp, [t_lam], [t_lam])
            tt("dve", lam_t[:, 4:5], lam_t[:, 3:4], lam_t[:, 2:3], ALU.subtract, [t_lam], [t_lam])
            ts("dve", lam_t[:, 5:6], lam_t[:, 4:5], 1.0, -LAM_INIT, ALU.mult, ALU.add, [t_lam], [t_lam])
            neglam = lam_t[:, 5:6]

            Arow = AR.alloc([128, D], F32); t_Arow = Trk()
            Brow = AR.alloc([128, D], F32); t_Brow = Trk()
            base_off = AR.off

            wfm = AR.alloc([128, 16, 1024], BF16)
            wtm = AR.alloc([128, 16, 1024], BF16)
            p1_off = AR.off
            cond = AR.alloc([128, 16], BF16); t_cond = Trk()
            bmod = AR.alloc([1, 3072], F32); t_bmod = Trk()
            modrow = AR.alloc([1, 3072], F32); t_modrow = Trk()
            wm = [AR.alloc([128, 16, 512], BF16) for _ in range(2)]
            t_wm = [Trk(), Trk()]
            d_wm = [P.dsem("d_wm0"), P.dsem("d_wm1")]
            d_bm = P.dsem("d_bm")
            P.dma("sp", bmod, bmod_d, d_bm, writes=[t_bmod])
            act(cond, cT, AF.Silu, [t_cT], [t_cond])
            wmod_v = wmod_d.rearrange("(kt p) n -> p kt n", p=128)
            for cb in range(6):
                s = cb % 2
                P.dma("pool", wm[s], wmod_v[:, :, cb * 512:(cb + 1) * 512], d_wm[s], writes=[t_wm[s]])
                for kt in range(16):
                    mm(banks[cb % 2][0:1, :], cond[:, kt:kt + 1], wm[s][:, kt, :], kt == 0, kt == 15,
                       [t_cond, t_wm[s]], [bk[cb % 2]], signal=(kt == 15))
                tt("dve", modrow[0:1, cb * 512:(cb + 1) * 512], banks[cb % 2][0:1, :], bmod[0:1, cb * 512:(cb + 1) * 512],
                   ALU.add, [bk[cb % 2], t_bmod], [t_modrow])
            d_mod = P.dsem("d_mod")
            P.dma("sp", mod_in.ap(), modrow, d_mod, reads=[t_modrow], writes=[t_modin])
            wfm_v = wfm_d.rearrange("(kt p) n -> p kt n", p=128)
            wtm_v = wtm_d.rearrange("(kt p) n -> p kt n", p=128)
            t_wfmh = [Trk(), Trk()]
            t_wtmh = [Trk(), Trk()]
            for hlf in range(2):
                P.dma("pool", wfm[:, hlf * 8:(hlf + 1) * 8, :], wfm_v[:, hlf * 8:(hlf + 1) * 8, :],
                      P.dsem("d_wfm%d" % hlf), writes=[t_wfmh[hlf]])
                P.dma("pool", wtm[:, hlf * 8:(hlf + 1) * 8, :], wtm_v[:, hlf * 8:(hlf + 1) * 8, :],
                      P.dsem("d_wtm%d" % hlf), writes=[t_wtmh[hlf]])
            collective("AllGather", [[0, 1, 2, 3], [4, 5, 6, 7]], mod_in.ap(), mod_out.ap(), [t_modin], [t_modout])
            modflat = mod_out.ap().rearrange("r n -> (r n)").unsqueeze(0)

            row_sems = {}

            def load_row(dst, src_row_ap, ds, W, R=()):
                key = id(W[0])
                if key not in row_sems:
                    row_sems[key] = P.dsem("d_row%d" % len(row_sems))
                P.dma("sp", dst.unsqueeze(1), src_row_ap.partition_broadcast(128), row_sems[key], reads=list(R), writes=W)

            load_row(Arow, modflat[0:1, 2048:4096], d_mod, [t_Arow], [t_modout])
            load_row(Brow, n1g_d, d_mod, [t_Brow], [])
            stt("dve", Arow, Arow, 1.0, Brow, ALU.add, ALU.mult, [t_Arow, t_Brow], [t_Arow])
            load_row(Brow, modflat[0:1, 0:2048], d_mod, [t_Brow], [t_modout])
            if DEBUG:
                P.dma("sp", dbg_mod, mod_out.ap(), P.dsem("dbgm"), reads=[t_modout])

            P.barrier()
            checkpoint("p0")
            AR.off = p1_off

            KT = [AR.alloc([128, SEQ], BF16) for _ in range(2)]
            t_KT = [[Trk() for _ in range(8)] for _ in range(2)]
            V = [AR.alloc([128, 32, 128], BF16) for _ in range(2)]
            t_V = [[Trk() for _ in range(8)] for _ in range(2)]
            xs = [AR.alloc([128, D], F32) for _ in range(2)]
            t_xs = [Trk(), Trk()]
            d_xs = [P.dsem("d_xs0"), P.dsem("d_xs1")]
            u_bfs = [AR.alloc([128, D], BF16) for _ in range(2)]; t_us = [Trk(), Trk()]
            uT_off = AR.off
            uT = AR.alloc([128, 16, 512], BF16)
            t_uT = [Trk() for _ in range(4)]
            stats = [AR.alloc([128, 8], F32) for _ in range(2)]; t_stats = [Trk(), Trk()]
            QT = [AR.alloc([128, 512], BF16) for _ in range(2)]; t_QT = [Trk(), Trk()]
            rqT = [AR.alloc([128, 512], F32) for _ in range(2)]; t_rqT = [Trk(), Trk()]
            snT = [AR.alloc([128, 512], F32) for _ in range(2)]; t_snT = [Trk(), Trk()]
            vtok = [AR.alloc([128, 4, 128], BF16) for _ in range(2)]
            t_vtok = [[Trk() for _ in range(4)] for _ in range(2)]
            sgf = [AR.alloc([128, 4, 256], F32) for _ in range(2)]
            t_sgf = [[Trk() for _ in range(4)] for _ in range(2)]
            Pb = [[AR.alloc([128, 512], BF16) for _ in range(2)] for _ in range(2)]
            t_Pb = [[Trk(), Trk()], [Trk(), Trk()]]
            R0 = AR.alloc([128, 512], F32); R1 = AR.alloc([128, 512], F32)
            o_f = AR.alloc([128, 512], F32); sq_bf = AR.alloc([128, 512], BF16)
            t_R0, t_R1, t_of, t_sq = Trk(), Trk(), Trk(), Trk()
            yst = [AR.alloc([128, 4, 512], BF16)] * 2
            t_yst = [[Trk() for _ in range(4)]] * 2
            d_yst = [P.dsem("d_yst0")] * 2
            def hgrn_set(A):
                W = {}
                for nm, shp, dt_ in (("hf", [128, 128], F32), ("hg", [128, 128], F32), ("hk", [128, 128], F32),
                                     ("E12", [128, 256], F32), ("E3", [128, 128], F32), ("E4", [128, 128], F32),
                                     ("qdz", [128, 2, 128], BF16), ("qm", [128, 128], BF16), ("km", [128, 128], BF16),
                                     ("kl", [128, 128], BF16), ("attm", [128, 128], BF16), ("gg", [128, 128], F32),
                                     ("yb", [128, 128], BF16), ("hst", [128, 8], F32)):
                    W[nm] = A.alloc(shp, dt_)
                    W["t_" + nm] = Trk()
                return W

            HW = [hgrn_set(AR)]
            AR1 = Arena(arena_t, uT_off + 16 * 1024)
            AR1.off = uT_off
            HW.append(hgrn_set(AR1))
            hw1_trks = [v for k_, v in HW[1].items() if k_.startswith("t_")]
            t_qdz = HW[0]["t_qdz"]
            qdz = HW[0]["qdz"]
            S32 = [AR.alloc([128, 128], F32) for _ in range(2)]; t_S32 = [Trk(), Trk()]
            Sbf = [[AR.alloc([128, 128], BF16) for _ in range(2)] for _ in range(2)]
            t_Sbf = [[Trk(), Trk()], [Trk(), Trk()]]
            for hh in range(2):
                memset("pool", S32[hh], 0.0, [t_S32[hh]])
                memset("pool", Sbf[hh][0], 0.0, [t_Sbf[hh][0]])
                memset("pool", Sbf[hh][1], 0.0, [t_Sbf[hh][1]])
            memset("pool", qdz, 0.0, [t_qdz])
            print("phase1 arena bytes", AR.off)

            psT = [banks[0][:, :].bitcast(BF16), banks[1][:, :].bitcast(BF16)]

            def norm_s1(i, x_ap_dram, xbuf, t_x, d_x):
                ub, t_ub, stt_, t_st = u_bfs[i % 2], t_us[i % 2], stats[i % 2], t_stats[i % 2]
                P.dma("sp", xbuf, x_ap_dram, d_x, writes=[t_x])
                act(ub, xbuf, AF.Square, [t_x], [t_ub, t_st], accum_out=stt_[:, 0:1])
                act(stt_[:, 1:2], stt_[:, 0:1], AF.Ln, [t_st], [t_st], scale=1.0 / D, bias=EPS)
                act(stt_[:, 2:3], stt_[:, 1:2], AF.Exp, [t_st], [t_st], scale=-0.5)
                stt("dve", xbuf, xbuf, stt_[:, 2:3], Arow, ALU.mult, ALU.mult, [t_x, t_st, t_Arow], [t_x])
                tt("pool", ub, xbuf, Brow, ALU.add, [t_x, t_Brow], [t_ub])

            def norm_s2(i, uT_dst, t_uTdst):
                ub, t_ub = u_bfs[i % 2], t_us[i % 2]
                for hlf in range(2):
                    for k8 in range(8):
                        kt = hlf * 8 + k8
                        tr(psT[hlf][:, k8 * 128:(k8 + 1) * 128], ub[:, kt * 128:(kt + 1) * 128], ident_bf,
                           [t_ub, t_idb], [bk[hlf]], signal=(k8 == 7))
                    src = psT[hlf].rearrange("p (a b) -> p a b", a=8)
                    cp("act" if hlf == 0 else "dve", uT_dst[:, hlf * 8:(hlf + 1) * 8, :], src, [bk[hlf]], [t_uTdst])

            def fm_proj(wsb, t_w, col0, rhs_fn, t_rhs, bank_i, nk=16):
                for kt in range(nk):
                    mm(banks[bank_i][:, :], wsb[:, kt, col0:col0 + 128], rhs_fn(kt), kt == 0, kt == nk - 1,
                       [t_w] + t_rhs, [bk[bank_i]], signal=(kt == nk - 1))

            rot = [2]

            def next_bank(lo=2, hi=7):
                b_ = rot[0]
                rot[0] = lo + (rot[0] - lo + 1) % (hi - lo + 1)
                return b_

            for blk in range(8):
                def _s1(t4_):
                    T_ = blk * 4 + t4_
                    norm_s1(T_, xb[T_ * 128:(T_ + 1) * 128, :], xs[T_ % 2], t_xs[T_ % 2], d_xs[T_ % 2])

                _s1(0)
                for t4 in range(4):
                    if t4 + 1 < 4:
                        _s1(t4 + 1)
                    norm_s2(blk * 4 + t4, uT[:, :, t4 * 128:(t4 + 1) * 128], t_uT[t4])
                for hh in range(2):
                    for ci in range(4):
                        b_ = next_bank()
                        fm_proj(wfm, t_wfmh[0], hh * 512 + ci * 128, lambda kt: uT[:, kt, :], t_uT + [t_wfmh[1]], b_)
                        if ci == 0:
                            cp("act", QT[hh], banks[b_], [bk[b_]], [t_QT[hh]])
                        elif ci == 1:
                            cp("dve", KT[hh][:, blk * 512:(blk + 1) * 512], banks[b_], [bk[b_]], [t_KT[hh][blk]])
                        elif ci == 2:
                            cp("act", rqT[hh], banks[b_], [bk[b_]], [t_rqT[hh]])
                        else:
                            act(snT[hh], banks[b_], AF.Sigmoid, [bk[b_]], [t_snT[hh]], scale=-1.0)
                    for t4 in range(4):
                        b_ = next_bank()
                        for kt in range(16):
                            mm(banks[b_][:, :], uT[:, kt, t4 * 128:(t4 + 1) * 128], wtm[:, kt, hh * 512:(hh + 1) * 512],
                               kt == 0, kt == 15, [t_uT[t4]] + t_wtmh, [bk[b_]], signal=(kt == 15))
                        cp("dve", V[hh][:, blk * 4 + t4, :], banks[b_][:, 0:128], [bk[b_]], [t_V[hh][blk]])
                        cp("dve", vtok[hh][:, t4, :], banks[b_][:, 128:256], [bk[b_]], [t_vtok[hh][t4]])
                        act(sgf[hh][:, t4, :], banks[b_][:, 256:512], AF.Sigmoid, [bk[b_], t_vtok[hh][t4]], [t_sgf[hh][t4]])
                checkpoint("proj%d" % blk)
                ys = yst[blk % 2]
                t_ys = t_yst[blk % 2]
                for hh in range(2):
                    O = [banks[2], banks[3]]; Dn = [banks[4], banks[5]]
                    Rm = [R0, R1]; t_Rm = [t_R0, t_R1]
                    Sxs = [[banks[6], banks[7]], [banks[0], banks[1]]]
                    bSx = [[6, 7], [0, 1]]
                    nkt = 4 * (blk + 1)

                    def emit_S(kt_):
                        r_ = kt_ - 4 * blk
                        c0_ = 128 * r_ if r_ > 0 else 0
                        for m in range(2):
                            mm(Sxs[kt_ % 2][m][:, c0_:512], KT[hh][64 * m:64 * (m + 1), kt_ * 128:(kt_ + 1) * 128],
                               QT[hh][64 * m:64 * (m + 1), c0_:512], True, True,
                               [t_KT[hh][kt_ // 4], t_QT[hh]], [bk[bSx[kt_ % 2][m]]], signal=True,
                               tile_position=(64 * m, 0))

                    emit_S(0)
                    for kt in range(nkt):
                        r = kt - 4 * blk
                        c0 = 128 * r if r > 0 else 0
                        kb_ = kt // 4
                        pb = kt % 2
                        if kt + 1 < nkt:
                            emit_S(kt + 1)
                        for m in range(2):
                            act(Pb[pb][m][:, c0:512], Sxs[pb][m][:, c0:512], AF.Exp, [bk[bSx[pb][m]]], [t_Pb[pb][m]],
                                scale=0.125)
                            if r >= 0:
                                memset("pool", Pb[pb][m][64:128, c0:c0 + 64], 0.0, [t_Pb[pb][m]])
                        last = (kt == nkt - 1)
                        for m in range(2):
                            mm(O[m][:, c0:512], V[hh][:, kt, :], Pb[pb][m][:, c0:512], kt == 0, last,
                               [t_V[hh][kb_], t_Pb[pb][m]], [bk[2 + m]], signal=last)
                        for m in range(2):
                            if kt == 0:
                                cp("dve", Rm[m], Pb[pb][m], [t_Pb[pb][m]], [t_Rm[m]])
                            else:
                                tt("dve", Rm[m][:, c0:512], Rm[m][:, c0:512], Pb[pb][m][:, c0:512], ALU.add,
                                   [t_Rm[m], t_Pb[pb][m]], [t_Rm[m]])
                    for m in range(2):
                        mm(Dn[m][:, :], ones32, Rm[m], True, True, [t_ones32, t_Rm[m]], [bk[4 + m]])
                    recip(R0, Dn[0], [bk[4]], [t_R0])
                    recip(R1, Dn[1], [bk[5]], [t_R1])
                    tt("dve", R0, O[0], R0, ALU.mult, [bk[2], t_R0], [t_R0])
                    tt("dve", R1, O[1], R1, ALU.mult, [bk[3], t_R1], [t_R1])
                    stt("dve", o_f, R1, neglam, R0, ALU.mult, ALU.add, [t_R0, t_R1, t_lam], [t_of])
                    act(sq_bf, o_f, AF.Square, [t_of], [t_sq])
                    mm(banks[6][:, :], ones_bf, sq_bf, True, True, [t_ones, t_sq], [bk[6]])
                    act(R0, banks[6], AF.Ln, [bk[6]], [t_R0], scale=1.0 / 128, bias=EPS)
                    act(R1, R0, AF.Exp, [t_R0], [t_R1], scale=-0.5)
                    stt("dve", ys[:, hh * 2 + 0, :], o_f, gsub, R1, ALU.mult, ALU.mult, [t_of, t_gsub, t_R1],
                        [t_ys[hh * 2 + 0]])
                checkpoint("attn%d" % blk)
                for e_ in ("pe", "act", "dve", "pool"):
                    P.wait_all(e_, t_uT)
                memset("pool", HW[1]["qdz"], 0.0, [HW[1]["t_qdz"]])

                def hgrn_gen(hh, t4, W, bnk):
                    cs, bB, bC, bD = banks[bnk[0]], banks[bnk[1]], banks[bnk[2]], banks[bnk[3]]
                    kcs, kB, kC, kD = bk[bnk[0]], bk[bnk[1]], bk[bnk[2]], bk[bnk[3]]
                    hf, hg, hk, E12, E3, E4 = W["hf"], W["hg"], W["hk"], W["E12"], W["E3"], W["E4"]
                    qdz_, qm, km, kl, attm, gg, yb, hst = (W["qdz"], W["qm"], W["km"], W["kl"], W["attm"], W["gg"],
                                                           W["yb"], W["hst"])
                    t_hf, t_hg, t_hk, t_E12, t_E3, t_E4 = (W["t_hf"], W["t_hg"], W["t_hk"], W["t_E12"], W["t_E3"],
                                                           W["t_E4"])
                    t_qdz_, t_qm, t_km, t_kl, t_attm, t_gg, t_yb, t_hst = (W["t_qdz"], W["t_qm"], W["t_km"], W["t_kl"],
                                                                          W["t_attm"], W["t_gg"], W["t_yb"], W["t_hst"])
                    tsl = slice(t4 * 128, (t4 + 1) * 128)
                    sig_rg = sgf[hh][:, t4, 0:128]
                    sig_rf = sgf[hh][:, t4, 128:256]
                    tt("dve", hf, sig_rf, omlrow[:, hh, :], ALU.mult, [t_sgf[hh][t4], t_lbrow], [t_hf])
                    tt("dve", hf, hf, lbrow[:, hh, :], ALU.add, [t_hf, t_lbrow], [t_hf])
                    yield
                    act(hg, hf, AF.Ln, [t_hf], [t_hg])
                    ts("dve", hk, hf, -1.0, 1.0, ALU.mult, ALU.add, [t_hf], [t_hk])
                    tt("pool", gg, sig_rg, gnrow[:, hh, :], ALU.mult, [t_sgf[hh][t4], t_gnr], [t_gg])
                    yield
                    mm(cs[:, 0:128], hg, U32, True, True, [t_hg, t_c32], [kcs], signal=False)
                    mm(cs[:, 128:256], hg, Um32, True, True, [t_hg, t_c32], [kcs], signal=False)
                    mm(cs[:, 256:384], L32, hg, True, True, [t_hg, t_c32], [kcs])
                    yield
                    act(E12, cs[:, 0:256], AF.Exp, [kcs], [t_E12])
                    act(E3, cs[:, 128:256], AF.Exp, [kcs], [t_E3], scale=-1.0)
                    act(E4, cs[:, 256:384], AF.Exp, [kcs], [t_E4])
                    yield
                    tt("dve", qdz_[:, 0, 0:64], rqT[hh][:, t4 * 128:t4 * 128 + 64], E12[:, 0:64], ALU.mult,
                       [t_rqT[hh], t_E12], [t_qdz_])
                    tt("dve", qdz_[:, 1, 64:128], rqT[hh][:, t4 * 128 + 64:t4 * 128 + 128], E12[:, 64:128], ALU.mult,
                       [t_rqT[hh], t_E12], [t_qdz_])
                    tt("pool", qm, rqT[hh][:, tsl], E12[:, 128:256], ALU.mult, [t_rqT[hh], t_E12], [t_qm])
                    stt("dve", km, snT[hh][:, tsl], omlcol[:, hh:hh + 1], E3, ALU.mult, ALU.mult,
                        [t_snT[hh], t_lbcol, t_E3], [t_km])
                    tt("pool", kl, hk, E4, ALU.mult, [t_hk, t_E4], [t_kl])
                    yield
                    mm(bB[:, 0:128], km, qm, True, True, [t_km, t_qm], [kB])
                    yield
                    tt("dve", attm, bB[:, 0:128], maskU_bf, ALU.mult, [kB, t_mku], [t_attm])
                    yield
                    vt = vtok[hh][:, t4, :]
                    mm(bC[:, 0:128], attm, vt, True, False, [t_attm, t_vtok[hh][t4]], [kC], signal=False)
                    mm(bC[:, 0:128], qdz_[:, 0, :], Sbf[hh][0], False, False, [t_qdz_, t_Sbf[hh][0]], [kC], signal=False)
                    mm(bD[:, 0:128], kl[0:64, :], vt[0:64, :], True, True, [t_kl, t_vtok[hh][t4]], [kD],
                       tile_position=(0, 0))
                    yield
                    stt("dve", Sbf[hh][1], S32[hh], E12[:, 63:64], bD[:, 0:128], ALU.mult, ALU.add,
                        [t_S32[hh], t_E12, kD], [t_Sbf[hh][1]])
                    stt("dve", S32[hh], S32[hh], E12[:, 63:64], bD[:, 0:128], ALU.mult, ALU.add,
                        [t_S32[hh], t_E12, kD], [t_S32[hh]])
                    yield
                    mm(bC[:, 0:128], qdz_[:, 1, :], Sbf[hh][1], False, True, [t_qdz_, t_Sbf[hh][1]], [kC])
                    mm(bD[:, 128:256], kl[64:128, :], vt[64:128, :], True, True, [t_kl, t_vtok[hh][t4]], [kD],
                       tile_position=(64, 0))
                    yield
                    stt("dve", Sbf[hh][0], S32[hh], E12[:, 127:128], bD[:, 128:256], ALU.mult, ALU.add,
                        [t_S32[hh], t_E12, kD], [t_Sbf[hh][0]])
                    stt("dve", S32[hh], S32[hh], E12[:, 127:128], bD[:, 128:256], ALU.mult, ALU.add,
                        [t_S32[hh], t_E12, kD], [t_S32[hh]])
                    act(attm, bC[:, 0:128], AF.Square, [kC], [t_attm, t_hst], accum_out=hst[:, 0:1])
                    yield
                    act(hst[:, 1:2], hst[:, 0:1], AF.Ln, [t_hst], [t_hst], scale=1.0 / 128, bias=EPS)
                    act(hst[:, 2:3], hst[:, 1:2], AF.Exp, [t_hst], [t_hst], scale=-0.5)
                    yield
                    stt("dve", yb, bC[:, 0:128], hst[:, 2:3], gg, ALU.mult, ALU.mult, [kC, t_hst, t_gg], [t_yb])
                    yield
                    ytp = bD[:, 256:320].bitcast(BF16)
                    tr(ytp, yb, ident_bf, [t_yb, t_idb], [kD])
                    yield
                    cp("act", ys[:, hh * 2 + 1, tsl], ytp, [kD], [t_ys[hh * 2 + 1]])

                for t4 in range(4):
                    alive = [hgrn_gen(0, t4, HW[0], (2, 3, 4, 5)), hgrn_gen(1, t4, HW[1], (6, 7, 0, 1))]
                    while alive:
                        for g_ in list(alive):
                            try:
                                next(g_)
                            except StopIteration:
                                alive.remove(g_)
                for e_ in ("act", "dve"):
                    P.wait_all(e_, hw1_trks)
                checkpoint("hgrn%d" % blk)
                t_exb = Trk()
                P.dma("sp", ex_in.ap()[blk].rearrange("(a p) t -> p a t", p=128), ys,
                      d_yst[0], reads=t_ys, writes=[t_exb])
                P.wait_all("pool", [t_exb])
                if cc_n[0] > 0:
                    P._wait_for("pool", ("dma", ccsd, cc_n[0]))
                cc_n[0] += 1
                P.raw("pool", lambda e, blk=blk: e.collective_compute(
                    "AllGather", ALU.bypass, replica_groups=[[0, 1, 2, 3], [4, 5, 6, 7]],
                    ins=[ex_in.ap()[blk]], outs=[ex_out.ap()[blk]]).then_inc(ccs, 1))
                t_exout.w = ("dma", ccsd, cc_n[0])

            checkpoint("p1")
            if DEBUG:
                P.dma("sp", dbg_y, ex_in.ap(), P.dsem("dbgy"), reads=[t_exout])
            checkpoint("ag")
            P.barrier()
            AR.off = base_off

            uTo = AR.alloc([128, 16, NTOK], BF16)
            t_uTo = [Trk() for _ in range(8)]
            hole_off = AR.off
            mT = AR.alloc([128, 16, NTOK], BF16)
            t_mT = [[Trk() for _ in range(2)] for _ in range(16)]
            r2_off = AR.off
            yT = AR.alloc([128, 2, 8, NTOK], BF16); t_yT = Trk()
            d_y = P.dsem("d_y")
            P.wait_all("sp", [t_joff, t_exout])
            ex_v = ex_out.ap().rearrange("b (r hh k p) t -> p k (r hh) b t", r=4, hh=2, k=2)
            reg_holder = {}

            def _ld_reg(e):
                for b2 in range(2):
                    reg = e.alloc_register("joff%d" % b2)
                    e.reg_load(reg, joff_sb[0:1, b2:b2 + 1])
                    reg_holder[b2] = e.snap(reg, min_val=0, max_val=7)

            P.raw("sp", _ld_reg)
            for kind in range(2):
                for b2 in range(2):
                    d_y["val"] += 16
                    P.raw("sp", lambda e, kind=kind, b2=b2: e.dma_start(
                        out=yT[:, kind, :, b2 * 512:(b2 + 1) * 512],
                        in_=ex_v[:, kind, :, reg_holder[b2], :]).then_inc(d_y["sem"], 16))
            t_yT.w = ("dma", d_y, d_y["val"])

            checkpoint("exch")
            xs2 = [AR.alloc([128, D], F32) for _ in range(2)]
            t_xs2 = [Trk(), Trk()]
            u_bfs = [AR.alloc([128, D], BF16) for _ in range(2)]; t_us = [Trk(), Trk()]
            stats = [AR.alloc([128, 8], F32) for _ in range(2)]; t_stats = [Trk(), Trk()]

            def _s1o(T_):
                norm_s1(T_, xo[T_ * 128:(T_ + 1) * 128, :], xs2[T_ % 2], t_xs2[T_ % 2], d_xs[T_ % 2])

            _s1o(0)
            for T in range(8):
                if T + 1 < 8:
                    _s1o(T + 1)
                norm_s2(T, uTo[:, :, T * 128:(T + 1) * 128], t_uTo[T])
            checkpoint("s0")
            wga = [AR.alloc([128, 16, 256], BF16) for _ in range(2)]
            wgr = [AR.alloc([128, 16, 256], BF16) for _ in range(2)]
            wao = [AR.alloc([128, 8, 256], BF16) for _ in range(2)]
            wro = [AR.alloc([128, 8, 256], BF16) for _ in range(2)]
            t_wa = [Trk(), Trk()]
            d_wa = [P.dsem("d_wa0"), P.dsem("d_wa1")]
            sga = AR.alloc([128, 512], F32); sgr = AR.alloc([128, 512], F32)
            t_sga, t_sgr = Trk(), Trk()
            print("phase2a arena bytes", AR.off)
            wg_v = wg_d.rearrange("(kt p) n -> p kt n", p=128)
            wao_v = wao_d.rearrange("(kt p) n -> p kt n", p=128)
            wro_v = wro_d.rearrange("(kt p) n -> p kt n", p=128)
            def _ld_a(cg_):
                s_ = cg_ % 2
                c0_ = cg_ * 256
                P.dma("pool", wga[s_], wg_v[:, :, c0_:c0_ + 256], d_wa[s_], writes=[t_wa[s_]])
                P.dma("pool", wgr[s_], wg_v[:, :, 2048 + c0_:2048 + c0_ + 256], d_wa[s_], writes=[t_wa[s_]])
                P.dma("pool", wao[s_], wao_v[:, :, c0_:c0_ + 256], d_wa[s_], writes=[t_wa[s_]])
                P.dma("pool", wro[s_], wro_v[:, :, c0_:c0_ + 256], d_wa[s_], writes=[t_wa[s_]])

            _ld_a(0)
            for cg in range(8):
                s = cg % 2
                if cg + 1 < 8:
                    _ld_a(cg + 1)
                for c2 in range(2):
                    ct = cg * 2 + c2
                    for tb in range(2):
                        tks = slice(tb * 512, (tb + 1) * 512)
                        t_u4 = t_uTo[tb * 4:(tb + 1) * 4]
                        b_ga, b_gr, b_a, b_r = 0 + 4 * tb, 1 + 4 * tb, 2 + 4 * tb, 3 + 4 * tb
                        fm_proj(wga[s], t_wa[s], c2 * 128, lambda kt: uTo[:, kt, tks], t_u4, b_ga)
                        fm_proj(wgr[s], t_wa[s], c2 * 128, lambda kt: uTo[:, kt, tks], t_u4, b_gr)
                        fm_proj(wao[s], t_wa[s], c2 * 128, lambda kt: yT[:, 0, kt, tks], [t_yT], b_a, nk=8)
                        fm_proj(wro[s], t_wa[s], c2 * 128, lambda kt: yT[:, 1, kt, tks], [t_yT], b_r, nk=8)
                        act(sga, banks[b_ga], AF.Sigmoid, [bk[b_ga]], [t_sga])
                        act(sgr, banks[b_gr], AF.Sigmoid, [bk[b_gr]], [t_sgr])
                        tt("dve", sga, banks[b_a], sga, ALU.mult, [bk[b_a], t_sga], [t_sga])
                        tt("dve", sgr, banks[b_r], sgr, ALU.mult, [bk[b_r], t_sgr], [t_sgr])
                        tt("dve", mT[:, ct, tks], sga, sgr, ALU.add, [t_sga, t_sgr], [t_mT[ct][tb]])
            P.barrier()
            AR.off = r2_off
            checkpoint("sa")
            h = AR.alloc([128, 8, D], F32)
            t_h = [Trk() for _ in range(8)]
            d_h = P.dsem("d_h")
            for T in range(8):
                P.dma("sp", h[:, T, :], xo[T * 128:(T + 1) * 128, :], d_h, writes=[t_h[T]])
            for T in range(8):
                t_h[T].w = ("dma", d_h, d_h["val"])
            Grow = AR.alloc([128, D], F32); t_Grow = Trk()
            load_row(Grow, modflat[0:1, 4096:6144], None, [t_Grow], [t_modout])
            r3_off = AR.off
            wos = [AR.alloc([128, 16, 512], BF16) for _ in range(2)]
            t_wos = [Trk(), Trk()]
            d_wos = [P.dsem("d_wos0"), P.dsem("d_wos1")]
            print("phase2b arena bytes", AR.off)
            wo_v = wo_d.rearrange("(kt p) n -> p kt n", p=128)
            def _ld_b(cb_):
                s_ = cb_ % 2
                cols_ = slice(cb_ * 512, (cb_ + 1) * 512)
                P.dma("pool", wos[s_], wo_v[:, :, cols_], d_wos[s_], writes=[t_wos[s_]])

            def _sc_b(cb_):
                s_ = cb_ % 2
                cols_ = slice(cb_ * 512, (cb_ + 1) * 512)
                for ct_ in range(16):
                    tt("dve", wos[s_][:, ct_, :], wos[s_][:, ct_, :], Grow[:, cols_], ALU.mult,
                       [t_wos[s_], t_Grow], [t_wos[s_]])

            _ld_b(0)
            for cb in range(4):
                s = cb % 2
                cols = slice(cb * 512, (cb + 1) * 512)
                if cb + 1 < 4:
                    _ld_b(cb + 1)
                _sc_b(cb)
                for T in range(8):
                    b_ = T % 4
                    for ct in range(16):
                        mm(banks[b_][:, :], mT[:, ct, T * 128:(T + 1) * 128], wos[s][:, ct, :], ct == 0, ct == 15,
                           [t_mT[ct][T // 4], t_wos[s]], [bk[b_]], signal=(ct == 15))
                    tt("dve", h[:, T, cols], h[:, T, cols], banks[b_], ALU.add, [t_h[T], bk[b_]], [t_h[T]])
            P.barrier()
            checkpoint("sb")
            load_row(Arow, modflat[0:1, 8192:10240], None, [t_Arow], [t_modout])
            load_row(Brow, n2g_d, None, [t_Brow], [])
            stt("dve", Arow, Arow, 1.0, Brow, ALU.add, ALU.mult, [t_Arow, t_Brow], [t_Arow])
            load_row(Brow, modflat[0:1, 6144:8192], None, [t_Brow], [t_modout])
            AR.off = hole_off
            aTs = [AR.alloc([128, 4, NTOK], BF16) for _ in range(2)]
            t_aTs = [[[Trk() for _ in range(2)] for _ in range(4)] for _ in range(2)]
            wd = [AR.alloc([128, 4, 512], BF16) for _ in range(2)]
            t_wd = [Trk(), Trk()]
            d_wd = [P.dsem("d_wd0"), P.dsem("d_wd1")]
            stats = [AR.alloc([128, 8], F32) for _ in range(2)]; t_stats = [Trk(), Trk()]
            assert AR.off <= r2_off, (AR.off, r2_off)
            AR.off = r3_off
            tmpxs = [AR.alloc([128, D], F32) for _ in range(2)]; t_tmpxs = [Trk(), Trk()]
            u_bfs = [AR.alloc([128, D], BF16) for _ in range(2)]; t_us = [Trk(), Trk()]

            def norm_h(T):
                st_, t_st = stats[T % 2], t_stats[T % 2]
                ub, t_ub = u_bfs[T % 2], t_us[T % 2]
                act(ub, h[:, T, :], AF.Square, [t_h[T]], [t_ub, t_st], accum_out=st_[:, 0:1])
                act(st_[:, 1:2], st_[:, 0:1], AF.Sqrt, [t_st], [t_st], scale=1.0 / D, bias=EPS)
                recip(st_[:, 2:3], st_[:, 1:2], [t_st], [t_st])

            def _c1(T):
                norm_h(T)
                stt("dve", tmpxs[T % 2], h[:, T, :], stats[T % 2][:, 2:3], Arow, ALU.mult, ALU.mult,
                    [t_h[T], t_stats[T % 2], t_Arow], [t_tmpxs[T % 2]])
                tt("pool", u_bfs[T % 2], tmpxs[T % 2], Brow, ALU.add, [t_tmpxs[T % 2], t_Brow], [t_us[T % 2]])

            _c1(0)
            for T in range(8):
                if T + 1 < 8:
                    _c1(T + 1)
                norm_s2(T, uTo[:, :, T * 128:(T + 1) * 128], t_uTo[T])
            P.wait_all("pool", t_tmpxs + t_us)
            checkpoint("sc")
            load_row(Grow, modflat[0:1, 10240:12288], None, [t_Grow], [t_modout])
            AR.off = r3_off
            wgu = [AR.alloc([128, 2, 16, 256], BF16) for _ in range(2)]
            t_wgu = [Trk(), Trk()]
            d_wgu = [P.dsem("d_wgu0"), P.dsem("d_wgu1")]
            sil = [AR.alloc([128, 512], F32) for _ in range(2)]
            t_sil = [Trk(), Trk()]
            print("phase2d arena bytes", AR.off)
            wfg_v = wfg_d.rearrange("(kt p) n -> p kt n", p=128)
            wfu_v = wfu_d.rearrange("(kt p) n -> p kt n", p=128)
            wfd_v = wfd_d.rearrange("(ht p) n -> p ht n", p=128)
            stages = []
            gcount = 0
            dcount = 0
            for grp in range(11):
                for sg in range(2):
                    stages.append(("gu", grp, sg, gcount % 2))
                    gcount += 1
                for cb in range(4):
                    stages.append(("dn", grp, cb, dcount % 2))
                    dcount += 1

            def _ld_d(st_):
                kind_, grp_, x_, s_ = st_
                if kind_ == "gu":
                    hc0 = (grp_ * 4 + x_ * 2) * 128
                    P.dma("pool", wgu[s_][:, 0, :, :], wfg_v[:, :, hc0:hc0 + 256], d_wgu[s_], writes=[t_wgu[s_]])
                    P.dma("pool", wgu[s_][:, 1, :, :], wfu_v[:, :, hc0:hc0 + 256], d_wgu[s_], writes=[t_wgu[s_]])
                else:
                    cols_ = slice(x_ * 512, (x_ + 1) * 512)
                    P.dma("pool", wd[s_], wfd_v[:, grp_ * 4:(grp_ + 1) * 4, cols_], d_wd[s_], writes=[t_wd[s_]])

            def _sc_d(st_):
                kind_, grp_, x_, s_ = st_
                if kind_ == "dn":
                    cols_ = slice(x_ * 512, (x_ + 1) * 512)
                    for hl_ in range(4):
                        tt("dve", wd[s_][:, hl_, :], wd[s_][:, hl_, :], Grow[:, cols_], ALU.mult,
                           [t_wd[s_], t_Grow], [t_wd[s_]])

            def _cmp_d(st_):
                kind_, grp_, x_, s = st_
                if kind_ == "gu":
                    sg = x_
                    for gi in range(2):
                        hl = sg * 2 + gi
                        for tb in range(2):
                            tks = slice(tb * 512, (tb + 1) * 512)
                            t_u4 = t_uTo[tb * 4:(tb + 1) * 4]
                            b_g = 4 + 2 * tb
                            b_u = 5 + 2 * tb
                            for kt in range(16):
                                mm(banks[b_g][:, :], wgu[s][:, 0, kt, gi * 128:(gi + 1) * 128], uTo[:, kt, tks],
                                   kt == 0, kt == 15, [t_wgu[s]] + t_u4, [bk[b_g]], signal=(kt == 15))
                            for kt in range(16):
                                mm(banks[b_u][:, :], wgu[s][:, 1, kt, gi * 128:(gi + 1) * 128], uTo[:, kt, tks],
                                   kt == 0, kt == 15, [t_wgu[s]] + t_u4, [bk[b_u]], signal=(kt == 15))
                            act(sil[tb], banks[b_g], AF.Silu, [bk[b_g]], [t_sil[tb]])
                            tt("dve", aTs[grp_ % 2][:, hl, tks], banks[b_u], sil[tb], ALU.mult, [bk[b_u], t_sil[tb]],
                               [t_aTs[grp_ % 2][hl][tb]])
                else:
                    cols = slice(x_ * 512, (x_ + 1) * 512)
                    for T in range(8):
                        b_ = T
                        for hl in range(4):
                            mm(banks[b_][:, :], aTs[grp_ % 2][:, hl, T * 128:(T + 1) * 128], wd[s][:, hl, :], hl == 0,
                               hl == 3, [t_aTs[grp_ % 2][hl][T // 4], t_wd[s]], [bk[b_]], signal=(hl == 3))
                        tt("dve", h[:, T, cols], h[:, T, cols], banks[b_], ALU.add, [t_h[T], bk[b_]], [t_h[T]])

            gu_list = [st_ for st_ in stages if st_[0] == "gu"]
            dn_list = [st_ for st_ in stages if st_[0] == "dn"]
            nxt = {"gu": 2, "dn": 2}
            lists = {"gu": gu_list, "dn": dn_list}
            for st_ in gu_list[:2] + dn_list[:2]:
                _ld_d(st_)
            for st_ in stages:
                _sc_d(st_)
                _cmp_d(st_)
                k_ = st_[0]
                if nxt[k_] < len(lists[k_]):
                    _ld_d(lists[k_][nxt[k_]])
                    nxt[k_] += 1
            checkpoint("sd")
            load_row(Arow, fg_d, None, [t_Arow], [])
            if DEBUG:
                for T in range(8):
                    P.dma("sp", dbg_h[T * 128:(T + 1) * 128, :], h[:, T, :], P.dsem("dbgh%d" % T), reads=[t_h[T]])
            for T in range(8):
                norm_h(T)
                stt("dve", h[:, T, :], h[:, T, :], stats[T % 2][:, 2:3], Arow, ALU.mult, ALU.mult,
                    [t_h[T], t_stats[T % 2], t_Arow], [t_h[T]])
                P.dma("sp", out_d[T * 128:(T + 1) * 128, :], h[:, T, :], dout, reads=[t_h[T]])

        except _Stop:
            P.barrier()
        if DEBUG:
            for d_ in P.dma_sems:
                if d_["name"].startswith("dbg"):
                    P._wait_for("sp", ("dma", d_, d_["val"]))
        P.raw("sp", lambda e: e.wait_ge(dout["sem"], dout["val"]))
        P.finish()
        print("instructions", P.ninstr, "waits", P.nwaits, "counts", P.cnt)
    return nc


def _consts():
    idx = np.arange(128)
    ch = idx // 64
    same = ch[:, None] == ch[None, :]
    ident = np.eye(128, dtype=np.float32)
    U = (same & (idx[:, None] <= idx[None, :])).astype(np.float32)
    mid = ch * 64 + 31
    Umid = (same & (idx[:, None] <= mid[None, :])).astype(np.float32)
    Um = U - Umid
    L = (same & (idx[:, None] > idx[None, :])).astype(np.float32)
    return np.ascontiguousarray(np.concatenate([ident, U, Um, L], axis=1))


def make_in_maps(x, c, w_mod, b_mod, norm1_g, w_in, lambda_q1, lambda_k1, lambda_q2, lambda_k2,
                 subln_g, lb_logits, gnorm_g, w_att_out, w_rec_out, w_o, norm2_g,
                 w_ffn_gate, w_ffn_up, w_ffn_down, final_g):
    f = lambda a: np.ascontiguousarray(np.asarray(a, dtype=np.float32))
    x = f(x); c = f(c); w_in0 = f(w_in)[0]; w_mod0 = f(w_mod)[0]; b_mod0 = f(b_mod)[0]
    lbl = f(lb_logits); gn = f(gnorm_g)[0]
    AQ, AK, AV, RQ, RF, RI, RG, GA = 0, 1024, 2048, 3072, 4096, 5120, 6144, 7168
    shared = {
        "w_g": np.ascontiguousarray(w_in0[:, GA:GA + 4096]),
        "w_ao": f(w_att_out)[0], "w_ro": f(w_rec_out)[0], "w_o": f(w_o)[0],
        "w_fg": f(w_ffn_gate)[0], "w_fu": f(w_ffn_up)[0], "w_fd": f(w_ffn_down)[0],
        "n1g": f(norm1_g)[0][None], "n2g": f(norm2_g)[0][None], "fg": f(final_g)[None],
        "subg": f(subln_g)[0][:, None],
        "lamv": np.concatenate([f(lambda_q1)[0], f(lambda_k1)[0], f(lambda_q2)[0], f(lambda_k2)[0]])[None],
        "consts": _consts(),
    }
    shared = {k: np.ascontiguousarray(v) for k, v in shared.items()}
    maps = []
    for core in range(8):
        b, j = core // 4, core % 4
        heads = (2 * j, 2 * j + 1)
        sl = lambda off, h: w_in0[:, off + h * 128: off + (h + 1) * 128]
        w_fm = np.concatenate([np.concatenate([sl(AQ, h), sl(AK, h), sl(RQ, h), sl(RF, h)], axis=1) for h in heads], axis=1)
        w_tm = np.concatenate([np.concatenate([sl(AV, h), sl(RI, h), sl(RG, h), sl(RF, h)], axis=1) for h in heads], axis=1)
        lbc = np.stack([np.stack([lbl[l, h * 128:(h + 1) * 128] for l in range(2)], axis=1) for h in heads], axis=1)
        lbr = np.stack([np.stack([lbl[l, h * 128:(h + 1) * 128] for l in range(2)], axis=0) for h in heads], axis=0)
        m = dict(shared)
        m.update({
            "xb": x[b],
            "xo": np.ascontiguousarray(x[b, j * NTOK:(j + 1) * NTOK]),
            "cT": np.ascontiguousarray(c[b].reshape(16, 128).T),
            "wmod": np.ascontiguousarray(w_mod0[:, j * 3072:(j + 1) * 3072]),
            "bmod": np.ascontiguousarray(b_mod0[j * 3072:(j + 1) * 3072][None]),
            "w_fm": np.ascontiguousarray(w_fm), "w_tm": np.ascontiguousarray(w_tm),
            "lbc": np.ascontiguousarray(lbc.reshape(128, 4)),
            "lbr": np.ascontiguousarray(lbr.reshape(1, 512)),
            "gnr": np.ascontiguousarray(np.stack([gn[h] for h in heads], axis=0).reshape(1, 256)),
            "joff": np.array([[2 * j, 2 * j + 1]], dtype=np.int32),
        })
        maps.append(m)
    return maps


_NC_CACHE = {}


def kernel(**inputs):
    in_maps = make_in_maps(**inputs)
    if "nc" not in _NC_CACHE:
        _NC_CACHE["nc"] = build_nc()
    nc = _NC_CACHE["nc"]
    res = run_bass_kernel_spmd(nc, in_maps, core_ids=list(range(8)))
    out = np.zeros((2, SEQ, D), dtype=np.float32)
    for core in range(8):
        b, j = core // 4, core % 4
        out[b, j * NTOK:(j + 1) * NTOK] = np.asarray(res.results[core]["out"], dtype=np.float32)
    if DEBUG:
        kernel.last = res
    return out
```

```python
import math
from contextlib import ExitStack

import numpy as np
import concourse.bass as bass
import concourse.mybir as mybir
from concourse.bass_utils import run_bass_kernel_spmd

F32 = mybir.dt.float32
BF16 = mybir.dt.bfloat16
I32 = mybir.dt.int32
U8 = mybir.dt.uint8
AF = mybir.ActivationFunctionType
ALU = mybir.AluOpType

D = 2048
SEQ = 4096
NTOK = 1024
FF = 5632
EPS = 1e-6
LAM_INIT = 0.8 - 0.6 * math.exp(-0.3 * 0)
ENGS = ("pe", "act", "dve", "pool", "sp")
DEBUG = False
STOP = None
SHAPE_OVR = {}


class Trk:
    __slots__ = ("w", "r")

    def __init__(self):
        self.w = None
        self.r = []


class Prog:
    def __init__(self, nc, stack):
        self.nc = nc
        self.stack = stack
        self.q = {e: [] for e in ENGS}
        self.cnt = {e: 0 for e in ENGS}
        self.sem = {e: stack.enter_context(nc.semaphore("s_" + e)) for e in ENGS}
        self.seen = {e: {} for e in ENGS}
        self.dma_sems = []
        self.ninstr = 0
        self.nwaits = 0

    def ps(self, name, shape, dt):
        return self.stack.enter_context(self.nc.psum_tensor(name, list(shape), dt))

    def dsem(self, name):
        s = self.stack.enter_context(self.nc.semaphore(name))
        d = {"sem": s, "val": 0, "name": name}
        self.dma_sems.append(d)
        return d

    def _wait_for(self, eng, ev):
        if ev is None:
            return
        kind, key, val = ev
        if kind == "eng" and key == "pe" and eng == "pe":
            return
        if kind == "eng":
            semkey = "E" + key
            sem = self.sem[key]
        else:
            semkey = "D" + key["name"]
            sem = key["sem"]
        if self.seen[eng].get(semkey, 0) >= val:
            return
        self.seen[eng][semkey] = val
        self.nwaits += 1
        self.q[eng].append(lambda e, sem=sem, val=val: e.wait_ge(sem, val))

    def _deps(self, eng, reads, writes):
        for t in reads:
            self._wait_for(eng, t.w)
        for t in writes:
            self._wait_for(eng, t.w)
            for ev in t.r:
                self._wait_for(eng, ev)

    def _commit(self, ev, reads, writes):
        for t in reads:
            t.r.append(ev)
            if len(t.r) > 48:
                last = {}
                for e_ in t.r:
                    k = (e_[0], e_[1] if e_[0] == "eng" else e_[1]["name"])
                    if k not in last or last[k][2] < e_[2]:
                        last[k] = e_
                t.r = list(last.values())
        for t in writes:
            t.w = ev
            t.r = []

    def op(self, eng, fn, reads=(), writes=(), signal=True):
        self._deps(eng, reads, writes)
        self.ninstr += 1
        if signal:
            self.cnt[eng] += 1
            sem = self.sem[eng]
            self.q[eng].append(lambda e, fn=fn, sem=sem: fn(e).then_inc(sem, 1))
            ev = ("eng", eng, self.cnt[eng])
        else:
            assert eng == "pe"
            self.q[eng].append(lambda e, fn=fn: fn(e))
            ev = ("eng", eng, self.cnt[eng] + 1)
        self._commit(ev, reads, writes)
        return ev

    def dma(self, eng, out, in_, ds, reads=(), writes=(), **kw):
        self._deps(eng, reads, writes)
        self.ninstr += 1
        ds["val"] += 16
        sem = ds["sem"]
        self.q[eng].append(
            lambda e, out=out, in_=in_, sem=sem, kw=kw: e.dma_start(out=out, in_=in_, **kw).then_inc(sem, 16))
        ev = ("dma", ds, ds["val"])
        self._commit(ev, reads, writes)
        return ev

    def raw(self, eng, fn):
        self.q[eng].append(fn)

    def wait_all(self, eng, trks):
        for t in trks:
            self._wait_for(eng, t.w)
            for ev in t.r:
                self._wait_for(eng, ev)

    def barrier(self):
        for e in ENGS:
            for e2 in ENGS:
                if e2 != e and self.cnt[e2] > 0:
                    self._wait_for(e, ("eng", e2, self.cnt[e2]))
            for d in self.dma_sems:
                if d["val"] > 0:
                    self._wait_for(e, ("dma", d, d["val"]))

    def finish(self):
        nc = self.nc
        q = self.q
        with nc.Block() as block:
            @block.tensor
            def _(e):
                for f in q["pe"]:
                    f(e)

            @block.scalar
            def _(e):
                for f in q["act"]:
                    f(e)

            @block.vector
            def _(e):
                for f in q["dve"]:
                    f(e)

            @block.gpsimd
            def _(e):
                for f in q["pool"]:
                    f(e)

            @block.sync
            def _(e):
                for f in q["sp"]:
                    f(e)


class Arena:
    def __init__(self, ap_u8, size):
        self.ap = ap_u8
        self.size = size
        self.off = 0

    def alloc(self, shape, dt, parts=None):
        esz = {F32: 4, BF16: 2, I32: 4}[dt]
        n = 1
        for s in shape[1:]:
            n *= s
        nb = (n * esz + 31) // 32 * 32
        assert self.off + nb <= self.size, ("arena overflow", self.off, nb, self.size)
        v = self.ap[0:shape[0], self.off:self.off + n * esz].bitcast(dt)
        self.off += nb
        if len(shape) == 3:
            v = v.rearrange("p (a b) -> p a b", a=shape[1])
        elif len(shape) == 4:
            v = v.rearrange("p (a b c) -> p a b c", a=shape[1], b=shape[2])
        return v


def build_nc():
    nc = bass.Bass("TRN2", target_bir_lowering=False)

    def din(name, shape, dt=F32):
        shape = SHAPE_OVR.get(name, shape)
        return nc.dram_tensor(name, list(shape), dt, kind="ExternalInput").ap()

    xb = din("xb", [SEQ, D])
    xo = din("xo", [NTOK, D])
    cT_d = din("cT", [128, 16])
    wmod_d = din("wmod", [D, 3072])
    bmod_d = din("bmod", [1, 3072])
    wfm_d = din("w_fm", [D, 1024])
    wtm_d = din("w_tm", [D, 1024])
    wg_d = din("w_g", [D, 4096])
    wao_d = din("w_ao", [1024, D])
    wro_d = din("w_ro", [1024, D])
    wo_d = din("w_o", [D, D])
    wfg_d = din("w_fg", [D, FF])
    wfu_d = din("w_fu", [D, FF])
    wfd_d = din("w_fd", [FF, D])
    n1g_d = din("n1g", [1, D])
    n2g_d = din("n2g", [1, D])
    fg_d = din("fg", [1, D])
    lbc_d = din("lbc", [128, 4])
    lbr_d = din("lbr", [1, 512])
    gnr_d = din("gnr", [1, 256])
    subg_d = din("subg", [128, 1])
    lamv_d = din("lamv", [1, 256])
    consts_d = din("consts", [128, 512])
    joff_d = din("joff", [1, 2], I32)
    out_d = nc.dram_tensor("out", [NTOK, D], F32, kind="ExternalOutput").ap()
    if DEBUG:
        dbg_y = nc.dram_tensor("dbg_y", [8, 512, 512], BF16, kind="ExternalOutput").ap()
        dbg_h = nc.dram_tensor("dbg_h", [NTOK, D], F32, kind="ExternalOutput").ap()
        dbg_mod = nc.dram_tensor("dbg_mod", [4, 3072], F32, kind="ExternalOutput").ap()

    mod_in = nc.dram_tensor("mod_in", [1, 3072], F32)
    mod_out = nc.dram_tensor("mod_out", [4, 3072], F32)
    ex_in = nc.dram_tensor("ex_in", [8, 512, 512], BF16)
    ex_out = nc.dram_tensor("ex_out", [8, 2048, 512], BF16)
    t_modin, t_modout, t_exin, t_exout = Trk(), Trk(), Trk(), Trk()

    with ExitStack() as st:
        P = Prog(nc, st)
        ARENA_SZ = 206 * 1024
        arena_t = st.enter_context(nc.sbuf_tensor("arena", [128, ARENA_SZ], U8))
        AR = Arena(arena_t, ARENA_SZ)
        banks = [P.ps("bank%d" % i, [128, 512], F32)[:, :] for i in range(8)]
        bk = [Trk() for _ in range(8)]
        dout = P.dsem("dout")
        ccs = st.enter_context(nc.semaphore("ccs"))
        cc_n = [0]
        ccsd = {"sem": ccs, "val": 0, "name": "ccs"}

        class _Stop(Exception):
            pass

        def checkpoint(name):
            if STOP == name:
                raise _Stop()

        def act(out, in_, func, R, W, **kw):
            P.op("act", lambda e: e.activation(out=out, in_=in_, func=func, **kw), R, W)

        def tt(eng, out, in0, in1, op, R, W):
            P.op(eng, lambda e: e.tensor_tensor(out=out, in0=in0, in1=in1, op=op), R, W)

        def ts(eng, out, in0, s1, s2, op0, op1, R, W):
            P.op(eng, lambda e: e.tensor_scalar(out=out, in0=in0, scalar1=s1, scalar2=s2, op0=op0, op1=op1), R, W)

        def stt(eng, out, in0, scalar, in1, op0, op1, R, W):
            P.op(eng, lambda e: e.scalar_tensor_tensor(out=out, in0=in0, scalar=scalar, in1=in1, op0=op0, op1=op1),
                 R, W)

        def cp(eng, out, in_, R, W):
            if eng == "act":
                act(out, in_, AF.Copy, R, W)
            else:
                P.op(eng, lambda e: e.tensor_copy(out=out, in_=in_), R, W)

        def recip(out, in_, R, W):
            P.op("dve", lambda e: e.reciprocal(out=out, in_=in_), R, W)

        def mm(out, lhsT, rhs, start, stop, R, W, signal=True, **kw):
            P.op("pe", lambda e: e.matmul(out, lhsT=lhsT, rhs=rhs, start=start, stop=stop, **kw), R, W, signal=signal)

        def tr(out, in_, ident, R, W, signal=True):
            P.op("pe", lambda e: e.transpose(out=out, in_=in_, identity=ident), R, W, signal=signal)

        def memset(eng, ap, val, W):
            P.op(eng, lambda e: e.memset(ap, val), (), W)

        def collective(kind, groups, src, dst, R, W):
            P.wait_all("pool", R + W)
            cc_n[0] += 1
            n = cc_n[0]
            P.raw("pool", lambda e: e.collective_compute(kind, ALU.bypass, replica_groups=groups,
                                                         ins=[src], outs=[dst]).then_inc(ccs, 1))
            P.raw("pool", lambda e: e.wait_ge(ccs, n))
            ev = P.op("pool", lambda e: e.memset(cc_dummy[:], 0.0), (), [t_ccd])
            for t in W:
                t.w = ev
                t.r = []

        try:
            cc_dummy = AR.alloc([128, 8], F32)
            t_ccd = Trk()
            consts32 = AR.alloc([128, 4, 128], F32); t_c32 = Trk()
            ident_bf = AR.alloc([128, 128], BF16); t_idb = Trk()
            maskU_bf = AR.alloc([128, 128], BF16); t_mku = Trk()
            ones_bf = AR.alloc([128, 128], BF16); t_ones = Trk()
            d_c = P.dsem("d_c")
            P.dma("sp", consts32.rearrange("p a b -> p (a b)"), consts_d, d_c, writes=[t_c32])
            cp("dve", ident_bf, consts32[:, 0, :], [t_c32], [t_idb])
            cp("dve", maskU_bf, consts32[:, 1, :], [t_c32], [t_mku])
            memset("pool", ones_bf, 1.0, [t_ones])
            ones32 = AR.alloc([128, 128], F32); t_ones32 = Trk()
            memset("pool", ones32, 1.0, [t_ones32])
            U32 = consts32[:, 1, :]
            Um32 = consts32[:, 2, :]
            L32 = consts32[:, 3, :]

            lbc = AR.alloc([128, 4], F32); t_lbc = Trk()
            lbr = AR.alloc([128, 512], F32); t_lbr = Trk()
            gnr = AR.alloc([128, 256], F32); t_gnr = Trk()
            subg = AR.alloc([128, 1], F32); t_subg = Trk()
            lamv = AR.alloc([128, 256], F32); t_lamv = Trk()
            d_sm = P.dsem("d_sm")
            P.dma("sp", lbc, lbc_d, d_sm, writes=[t_lbc])
            P.dma("sp", lbr.unsqueeze(1), lbr_d.partition_broadcast(128), d_sm, writes=[t_lbr])
            P.dma("sp", gnr.unsqueeze(1), gnr_d.partition_broadcast(128), d_sm, writes=[t_gnr])
            P.dma("sp", subg, subg_d, d_sm, writes=[t_subg])
            P.dma("sp", lamv.unsqueeze(1), lamv_d.partition_broadcast(128), d_sm, writes=[t_lamv])
            joff_sb = AR.alloc([1, 8], I32); t_joff = Trk()
            P.dma("sp", joff_sb[0:1, 0:2], joff_d, d_sm, writes=[t_joff])
            cT = AR.alloc([128, 16], F32); t_cT = Trk()
            P.dma("sp", cT, cT_d, d_sm, writes=[t_cT])
            for t_ in (t_lbc, t_lbr, t_gnr, t_subg, t_lamv, t_joff, t_cT):
                t_.w = ("dma", d_sm, d_sm["val"])
            lbcol = AR.alloc([128, 2], F32)
            omlcol = AR.alloc([128, 2], F32)
            dcol = AR.alloc([128, 2], F32)
            t_lbcol = Trk()
            lbc3 = lbc.rearrange("p (h l) -> p h l", h=2)
            tt("dve", dcol, lbc3[:, :, 0], lbc3[:, :, 1], ALU.subtract, [t_lbc], [t_lbcol])
            act(lbcol, dcol, AF.Sigmoid, [t_lbcol], [t_lbcol])
            act(omlcol, dcol, AF.Sigmoid, [t_lbcol], [t_lbcol], scale=-1.0)
            lbrow = AR.alloc([128, 2, 128], F32)
            omlrow = AR.alloc([128, 2, 128], F32)
            drow = AR.alloc([128, 2, 128], F32)
            t_lbrow = Trk()
            lbr4 = lbr.rearrange("p (h l d) -> p h l d", h=2, l=2)
            tt("dve", drow, lbr4[:, :, 0, :], lbr4[:, :, 1, :], ALU.subtract, [t_lbr], [t_lbrow])
            act(lbrow, drow, AF.Sigmoid, [t_lbrow], [t_lbrow])
            act(omlrow, drow, AF.Sigmoid, [t_lbrow], [t_lbrow], scale=-1.0)
            gnrow = gnr.rearrange("p (h e) -> p h e", h=2)
            gsub = AR.alloc([128, 1], F32); t_gsub = Trk()
            ts("dve", gsub, subg, 1.0 - LAM_INIT, 0.0, ALU.mult, ALU.add, [t_subg], [t_gsub])
            lam_t = AR.alloc([128, 8], F32); t_lam = Trk()
            lprod = AR.alloc([128, 2, 64], F32)
            lv = lamv.rearrange("p (a d) -> p a d", a=4)
            tt("dve", lprod[:, 0, :], lv[:, 0, :], lv[:, 1, :], ALU.mult, [t_lamv], [t_lam])
            tt("dve", lprod[:, 1, :], lv[:, 2, :], lv[:, 3, :], ALU.mult, [t_lamv], [t_lam])
            P.op("dve", lambda e: e.tensor_reduce(out=lam_t[:, 0:2], in_=lprod, axis=mybir.AxisListType.X, op=ALU.add),
                 [t_lam], [t_lam])
            act(lam_t[:, 2:4], lam_t[:, 0:2], AF.Exp, [t_lam], [t_lam])
            tt("dve", lam_t[:, 4:5], lam_t[:, 3:4], lam_t[:, 2:3], ALU.subtract, [t_lam], [t_lam])
            ts("dve", lam_t[:, 5:6], lam_t[:, 4:5], 1.0, -LAM_INIT, ALU.mult, ALU.add, [t_lam], [t_lam])
            neglam = lam_t[:, 5:6]

            Arow = AR.alloc([128, D], F32); t_Arow = Trk()
            Brow = AR.alloc([128, D], F32); t_Brow = Trk()
            base_off = AR.off

            wfm = AR.alloc([128, 16, 1024], BF16)
            wtm = AR.alloc([128, 16, 1024], BF16)
            p1_off = AR.off
            cond = AR.alloc([128, 16], BF16); t_cond = Trk()
            bmod = AR.alloc([1, 3072], F32); t_bmod = Trk()
            modrow = AR.alloc([1, 3072], F32); t_modrow = Trk()
            wm = [AR.alloc([128, 16, 512], BF16) for _ in range(2)]
            t_wm = [Trk(), Trk()]
            d_wm = [P.dsem("d_wm0"), P.dsem("d_wm1")]
            d_bm = P.dsem("d_bm")
            P.dma("sp", bmod, bmod_d, d_bm, writes=[t_bmod])
            act(cond, cT, AF.Silu, [t_cT], [t_cond])
            wmod_v = wmod_d.rearrange("(kt p) n -> p kt n", p=128)
            for cb in range(6):
                s = cb % 2
                P.dma("pool", wm[s], wmod_v[:, :, cb * 512:(cb + 1) * 512], d_wm[s], writes=[t_wm[s]])
                for kt in range(16):
                    mm(banks[cb % 2][0:1, :], cond[:, kt:kt + 1], wm[s][:, kt, :], kt == 0, kt == 15,
                       [t_cond, t_wm[s]], [bk[cb % 2]], signal=(kt == 15))
                tt("dve", modrow[0:1, cb * 512:(cb + 1) * 512], banks[cb % 2][0:1, :], bmod[0:1, cb * 512:(cb + 1) * 512],
                   ALU.add, [bk[cb % 2], t_bmod], [t_modrow])
            d_mod = P.dsem("d_mod")
            P.dma("sp", mod_in.ap(), modrow, d_mod, reads=[t_modrow], writes=[t_modin])
            wfm_v = wfm_d.rearrange("(kt p) n -> p kt n", p=128)
            wtm_v = wtm_d.rearrange("(kt p) n -> p kt n", p=128)
            t_wfmh = [Trk(), Trk()]
            t_wtmh = [Trk(), Trk()]
            for hlf in range(2):
                P.dma("pool", wfm[:, hlf * 8:(hlf + 1) * 8, :], wfm_v[:, hlf * 8:(hlf + 1) * 8, :],
                      P.dsem("d_wfm%d" % hlf), writes=[t_wfmh[hlf]])
                P.dma("pool", wtm[:, hlf * 8:(hlf + 1) * 8, :], wtm_v[:, hlf * 8:(hlf + 1) * 8, :],
                      P.dsem("d_wtm%d" % hlf), writes=[t_wtmh[hlf]])
            collective("AllGather", [[0, 1, 2, 3], [4, 5, 6, 7]], mod_in.ap(), mod_out.ap(), [t_modin], [t_modout])
            modflat = mod_out.ap().rearrange("r n -> (r n)").unsqueeze(0)

            row_sems = {}

            def load_row(dst, src_row_ap, ds, W, R=()):
                key = id(W[0])
                if key not in row_sems:
                    row_sems[key] = P.dsem("d_row%d" % len(row_sems))
                P.dma("sp", dst.unsqueeze(1), src_row_ap.partition_broadcast(128), row_sems[key], reads=list(R), writes=W)

            load_row(Arow, modflat[0:1, 2048:4096], d_mod, [t_Arow], [t_modout])
            load_row(Brow, n1g_d, d_mod, [t_Brow], [])
            stt("dve", Arow, Arow, 1.0, Brow, ALU.add, ALU.mult, [t_Arow, t_Brow], [t_Arow])
            load_row(Brow, modflat[0:1, 0:2048], d_mod, [t_Brow], [t_modout])
            if DEBUG:
                P.dma("sp", dbg_mod, mod_out.ap(), P.dsem("dbgm"), reads=[t_modout])

            P.barrier()
            checkpoint("p0")
            AR.off = p1_off

            KT = [AR.alloc([128, SEQ], BF16) for _ in range(2)]
            t_KT = [[Trk() for _ in range(8)] for _ in range(2)]
            V = [AR.alloc([128, 32, 128], BF16) for _ in range(2)]
            t_V = [[Trk() for _ in range(8)] for _ in range(2)]
            xs = [AR.alloc([128, D], F32) for _ in range(2)]
            t_xs = [Trk(), Trk()]
            d_xs = [P.dsem("d_xs0"), P.dsem("d_xs1")]
            u_bfs = [AR.alloc([128, D], BF16) for _ in range(2)]; t_us = [Trk(), Trk()]
            uT_off = AR.off
            uT = AR.alloc([128, 16, 512], BF16)
            t_uT = [Trk() for _ in range(4)]
            stats = [AR.alloc([128, 8], F32) for _ in range(2)]; t_stats = [Trk(), Trk()]
            QT = [AR.alloc([128, 512], BF16) for _ in range(2)]; t_QT = [Trk(), Trk()]
            rqT = [AR.alloc([128, 512], F32) for _ in range(2)]; t_rqT = [Trk(), Trk()]
            snT = [AR.alloc([128, 512], F32) for _ in range(2)]; t_snT = [Trk(), Trk()]
            vtok = [AR.alloc([128, 4, 128], BF16) for _ in range(2)]
            t_vtok = [[Trk() for _ in range(4)] for _ in range(2)]
            sgf = [AR.alloc([128, 4, 256], F32) for _ in range(2)]
            t_sgf = [[Trk() for _ in range(4)] for _ in range(2)]
            Pb = [[AR.alloc([128, 512], BF16) for _ in range(2)] for _ in range(2)]
            t_Pb = [[Trk(), Trk()], [Trk(), Trk()]]
            R0 = AR.alloc([128, 512], F32); R1 = AR.alloc([128, 512], F32)
            o_f = AR.alloc([128, 512], F32); sq_bf = AR.alloc([128, 512], BF16)
            t_R0, t_R1, t_of, t_sq = Trk(), Trk(), Trk(), Trk()
            yst = [AR.alloc([128, 4, 512], BF16)] * 2
            t_yst = [[Trk() for _ in range(4)]] * 2
            d_yst = [P.dsem("d_yst0")] * 2
            def hgrn_set(A):
                W = {}
                for nm, shp, dt_ in (("hf", [128, 128], F32), ("hg", [128, 128], F32), ("hk", [128, 128], F32),
                                     ("E12", [128, 256], F32), ("E3", [128, 128], F32), ("E4", [128, 128], F32),
                                     ("qdz", [128, 2, 128], BF16), ("qm", [128, 128], BF16), ("km", [128, 128], BF16),
                                     ("kl", [128, 128], BF16), ("attm", [128, 128], BF16), ("gg", [128, 128], F32),
                                     ("yb", [128, 128], BF16), ("hst", [128, 8], F32)):
                    W[nm] = A.alloc(shp, dt_)
                    W["t_" + nm] = Trk()
                return W

            HW = [hgrn_set(AR)]
            AR1 = Arena(arena_t, uT_off + 16 * 1024)
            AR1.off = uT_off
            HW.append(hgrn_set(AR1))
            hw1_trks = [v for k_, v in HW[1].items() if k_.startswith("t_")]
            t_qdz = HW[0]["t_qdz"]
            qdz = HW[0]["qdz"]
            S32 = [AR.alloc([128, 128], F32) for _ in range(2)]; t_S32 = [Trk(), Trk()]
            Sbf = [[AR.alloc([128, 128], BF16) for _ in range(2)] for _ in range(2)]
            t_Sbf = [[Trk(), Trk()], [Trk(), Trk()]]
            for hh in range(2):
                memset("pool", S32[hh], 0.0, [t_S32[hh]])
                memset("pool", Sbf[hh][0], 0.0, [t_Sbf[hh][0]])
                memset("pool", Sbf[hh][1], 0.0, [t_Sbf[hh][1]])
            memset("pool", qdz, 0.0, [t_qdz])
            print("phase1 arena bytes", AR.off)

            psT = [banks[0][:, :].bitcast(BF16), banks[1][:, :].bitcast(BF16)]

            def norm_s1(i, x_ap_dram, xbuf, t_x, d_x):
                ub, t_ub, stt_, t_st = u_bfs[i % 2], t_us[i % 2], stats[i % 2], t_stats[i % 2]
                P.dma("sp", xbuf, x_ap_dram, d_x, writes=[t_x])
                act(ub, xbuf, AF.Square, [t_x], [t_ub, t_st], accum_out=stt_[:, 0:1])
                act(stt_[:, 1:2], stt_[:, 0:1], AF.Ln, [t_st], [t_st], scale=1.0 / D, bias=EPS)
                act(stt_[:, 2:3], stt_[:, 1:2], AF.Exp, [t_st], [t_st], scale=-0.5)
                stt("dve", xbuf, xbuf, stt_[:, 2:3], Arow, ALU.mult, ALU.mult, [t_x, t_st, t_Arow], [t_x])
                tt("pool", ub[:, 0:1024], xbuf[:, 0:1024], Brow[:, 0:1024], ALU.add, [t_x, t_Brow], [t_ub])
                tt("dve", ub[:, 1024:2048], xbuf[:, 1024:2048], Brow[:, 1024:2048], ALU.add, [t_x, t_Brow], [t_ub])

            def norm_s2(i, uT_dst, t_uTdst):
                ub, t_ub = u_bfs[i % 2], t_us[i % 2]
                for hlf in range(2):
                    for k8 in range(8):
                        kt = hlf * 8 + k8
                        tr(psT[hlf][:, k8 * 128:(k8 + 1) * 128], ub[:, kt * 128:(kt + 1) * 128], ident_bf,
                           [t_ub, t_idb], [bk[hlf]], signal=(k8 == 7))
                    src = psT[hlf].rearrange("p (a b) -> p a b", a=8)
                    cp("act" if hlf == 0 else "dve", uT_dst[:, hlf * 8:(hlf + 1) * 8, :], src, [bk[hlf]], [t_uTdst])

            def fm_proj(wsb, t_w, col0, rhs_fn, t_rhs, bank_i, nk=16):
                for kt in range(nk):
                    mm(banks[bank_i][:, :], wsb[:, kt, col0:col0 + 128], rhs_fn(kt), kt == 0, kt == nk - 1,
                       [t_w] + t_rhs, [bk[bank_i]], signal=(kt == nk - 1))

            rot = [2]

            def next_bank(lo=2, hi=7):
                b_ = rot[0]
                rot[0] = lo + (rot[0] - lo + 1) % (hi - lo + 1)
                return b_

            for blk in range(8):
                def _s1(t4_):
                    T_ = blk * 4 + t4_
                    norm_s1(T_, xb[T_ * 128:(T_ + 1) * 128, :], xs[T_ % 2], t_xs[T_ % 2], d_xs[T_ % 2])

                _s1(0)
                for t4 in range(4):
                    if t4 + 1 < 4:
                        _s1(t4 + 1)
                    norm_s2(blk * 4 + t4, uT[:, :, t4 * 128:(t4 + 1) * 128], t_uT[t4])
                for hh in range(2):
                    for ci in range(4):
                        b_ = next_bank()
                        fm_proj(wfm, t_wfmh[0], hh * 512 + ci * 128, lambda kt: uT[:, kt, :], t_uT + [t_wfmh[1]], b_)
                        if ci == 0:
                            cp("act", QT[hh], banks[b_], [bk[b_]], [t_QT[hh]])
                        elif ci == 1:
                            cp("dve", KT[hh][:, blk * 512:(blk + 1) * 512], banks[b_], [bk[b_]], [t_KT[hh][blk]])
                        elif ci == 2:
                            cp("act", rqT[hh], banks[b_], [bk[b_]], [t_rqT[hh]])
                        else:
                            act(snT[hh], banks[b_], AF.Sigmoid, [bk[b_]], [t_snT[hh]], scale=-1.0)
                    for t4 in range(4):
                        b_ = next_bank()
                        for kt in range(16):
                            mm(banks[b_][:, :], uT[:, kt, t4 * 128:(t4 + 1) * 128], wtm[:, kt, hh * 512:(hh + 1) * 512],
                               kt == 0, kt == 15, [t_uT[t4]] + t_wtmh, [bk[b_]], signal=(kt == 15))
                        cp("dve", V[hh][:, blk * 4 + t4, :], banks[b_][:, 0:128], [bk[b_]], [t_V[hh][blk]])
                        cp("dve", vtok[hh][:, t4, :], banks[b_][:, 128:256], [bk[b_]], [t_vtok[hh][t4]])
                        act(sgf[hh][:, t4, :], banks[b_][:, 256:512], AF.Sigmoid, [bk[b_], t_vtok[hh][t4]], [t_sgf[hh][t4]])
                checkpoint("proj%d" % blk)
                ys = yst[blk % 2]
                t_ys = t_yst[blk % 2]
                for hh in range(2):
                    O = [banks[2], banks[3]]; Dn = [banks[4], banks[5]]
                    Rm = [R0, R1]; t_Rm = [t_R0, t_R1]
                    Sxs = [[banks[6], banks[7]], [banks[0], banks[1]]]
                    bSx = [[6, 7], [0, 1]]
                    nkt = 4 * (blk + 1)

                    def emit_S(kt_):
                        r_ = kt_ - 4 * blk
                        c0_ = 128 * r_ if r_ > 0 else 0
                        for m in range(2):
                            mm(Sxs[kt_ % 2][m][:, c0_:512], KT[hh][64 * m:64 * (m + 1), kt_ * 128:(kt_ + 1) * 128],
                               QT[hh][64 * m:64 * (m + 1), c0_:512], True, True,
                               [t_KT[hh][kt_ // 4], t_QT[hh]], [bk[bSx[kt_ % 2][m]]], signal=True,
                               tile_position=(64 * m, 0))

                    emit_S(0)
                    for kt in range(nkt):
                        r = kt - 4 * blk
                        c0 = 128 * r if r > 0 else 0
                        kb_ = kt // 4
                        pb = kt % 2
                        if kt + 1 < nkt:
                            emit_S(kt + 1)
                        for m in range(2):
                            act(Pb[pb][m][:, c0:512], Sxs[pb][m][:, c0:512], AF.Exp, [bk[bSx[pb][m]]], [t_Pb[pb][m]],
                                scale=0.125)
                            if r >= 0:
                                memset("pool", Pb[pb][m][64:128, c0:c0 + 64], 0.0, [t_Pb[pb][m]])
                        last = (kt == nkt - 1)
                        for m in range(2):
                            mm(O[m][:, c0:512], V[hh][:, kt, :], Pb[pb][m][:, c0:512], kt == 0, last,
                               [t_V[hh][kb_], t_Pb[pb][m]], [bk[2 + m]], signal=last)
                        for m in range(2):
                            if kt == 0:
                                cp("dve", Rm[m], Pb[pb][m], [t_Pb[pb][m]], [t_Rm[m]])
                            else:
                                tt("dve", Rm[m][:, c0:512], Rm[m][:, c0:512], Pb[pb][m][:, c0:512], ALU.add,
                                   [t_Rm[m], t_Pb[pb][m]], [t_Rm[m]])
                    for m in range(2):
                        mm(Dn[m][:, :], ones32, Rm[m], True, True, [t_ones32, t_Rm[m]], [bk[4 + m]])
                    recip(R0, Dn[0], [bk[4]], [t_R0])
                    recip(R1, Dn[1], [bk[5]], [t_R1])
                    tt("dve", R0, O[0], R0, ALU.mult, [bk[2], t_R0], [t_R0])
                    tt("dve", R1, O[1], R1, ALU.mult, [bk[3], t_R1], [t_R1])
                    stt("dve", o_f, R1, neglam, R0, ALU.mult, ALU.add, [t_R0, t_R1, t_lam], [t_of])
                    act(sq_bf, o_f, AF.Square, [t_of], [t_sq])
                    mm(banks[6][:, :], ones_bf, sq_bf, True, True, [t_ones, t_sq], [bk[6]])
                    act(R0, banks[6], AF.Ln, [bk[6]], [t_R0], scale=1.0 / 128, bias=EPS)
                    act(R1, R0, AF.Exp, [t_R0], [t_R1], scale=-0.5)
                    stt("dve", ys[:, hh * 2 + 0, :], o_f, gsub, R1, ALU.mult, ALU.mult, [t_of, t_gsub, t_R1],
                        [t_ys[hh * 2 + 0]])
                checkpoint("attn%d" % blk)
                for e_ in ("pe", "act", "dve", "pool"):
                    P.wait_all(e_, t_uT)
                memset("pool", HW[1]["qdz"], 0.0, [HW[1]["t_qdz"]])

                def hgrn_gen(hh, t4, W, bnk):
                    cs, bB, bC, bD = banks[bnk[0]], banks[bnk[1]], banks[bnk[2]], banks[bnk[3]]
                    kcs, kB, kC, kD = bk[bnk[0]], bk[bnk[1]], bk[bnk[2]], bk[bnk[3]]
                    hf, hg, hk, E12, E3, E4 = W["hf"], W["hg"], W["hk"], W["E12"], W["E3"], W["E4"]
                    qdz_, qm, km, kl, attm, gg, yb, hst = (W["qdz"], W["qm"], W["km"], W["kl"], W["attm"], W["gg"],
                                                           W["yb"], W["hst"])
                    t_hf, t_hg, t_hk, t_E12, t_E3, t_E4 = (W["t_hf"], W["t_hg"], W["t_hk"], W["t_E12"], W["t_E3"],
                                                           W["t_E4"])
                    t_qdz_, t_qm, t_km, t_kl, t_attm, t_gg, t_yb, t_hst = (W["t_qdz"], W["t_qm"], W["t_km"], W["t_kl"],
                                                                          W["t_attm"], W["t_gg"], W["t_yb"], W["t_hst"])
                    tsl = slice(t4 * 128, (t4 + 1) * 128)
                    sig_rg = sgf[hh][:, t4, 0:128]
                    sig_rf = sgf[hh][:, t4, 128:256]
                    tt("dve", hf, sig_rf, omlrow[:, hh, :], ALU.mult, [t_sgf[hh][t4], t_lbrow], [t_hf])
                    tt("dve", hf, hf, lbrow[:, hh, :], ALU.add, [t_hf, t_lbrow], [t_hf])
                    yield
                    act(hg, hf, AF.Ln, [t_hf], [t_hg])
                    ts("dve", hk, hf, -1.0, 1.0, ALU.mult, ALU.add, [t_hf], [t_hk])
                    tt("pool", gg, sig_rg, gnrow[:, hh, :], ALU.mult, [t_sgf[hh][t4], t_gnr], [t_gg])
                    yield
                    mm(cs[:, 0:128], hg, U32, True, True, [t_hg, t_c32], [kcs], signal=False)
                    mm(cs[:, 128:256], hg, Um32, True, True, [t_hg, t_c32], [kcs], signal=False)
                    mm(cs[:, 256:384], L32, hg, True, True, [t_hg, t_c32], [kcs])
                    yield
                    act(E12, cs[:, 0:256], AF.Exp, [kcs], [t_E12])
                    act(E3, cs[:, 128:256], AF.Exp, [kcs], [t_E3], scale=-1.0)
                    act(E4, cs[:, 256:384], AF.Exp, [kcs], [t_E4])
                    yield
                    tt("dve", qdz_[:, 0, 0:64], rqT[hh][:, t4 * 128:t4 * 128 + 64], E12[:, 0:64], ALU.mult,
                       [t_rqT[hh], t_E12], [t_qdz_])
                    tt("dve", qdz_[:, 1, 64:128], rqT[hh][:, t4 * 128 + 64:t4 * 128 + 128], E12[:, 64:128], ALU.mult,
                       [t_rqT[hh], t_E12], [t_qdz_])
                    tt("pool", qm, rqT[hh][:, tsl], E12[:, 128:256], ALU.mult, [t_rqT[hh], t_E12], [t_qm])
                    stt("dve", km, snT[hh][:, tsl], omlcol[:, hh:hh + 1], E3, ALU.mult, ALU.mult,
                        [t_snT[hh], t_lbcol, t_E3], [t_km])
                    tt("pool", kl, hk, E4, ALU.mult, [t_hk, t_E4], [t_kl])
                    yield
                    mm(bB[:, 0:128], km, qm, True, True, [t_km, t_qm], [kB])
                    yield
                    tt("dve", attm, bB[:, 0:128], maskU_bf, ALU.mult, [kB, t_mku], [t_attm])
                    yield
                    vt = vtok[hh][:, t4, :]
                    mm(bC[:, 0:128], attm, vt, True, False, [t_attm, t_vtok[hh][t4]], [kC], signal=False)
                    mm(bC[:, 0:128], qdz_[:, 0, :], Sbf[hh][0], False, False, [t_qdz_, t_Sbf[hh][0]], [kC], signal=False)
                    mm(bD[:, 0:128], kl[0:64, :], vt[0:64, :], True, True, [t_kl, t_vtok[hh][t4]], [kD],
                       tile_position=(0, 0))
                    yield
                    stt("dve", Sbf[hh][1], S32[hh], E12[:, 63:64], bD[:, 0:128], ALU.mult, ALU.add,
                        [t_S32[hh], t_E12, kD], [t_Sbf[hh][1]])
                    stt("dve", S32[hh], S32[hh], E12[:, 63:64], bD[:, 0:128], ALU.mult, ALU.add,
                        [t_S32[hh], t_E12, kD], [t_S32[hh]])
                    yield
                    mm(bC[:, 0:128], qdz_[:, 1, :], Sbf[hh][1], False, True, [t_qdz_, t_Sbf[hh][1]], [kC])
                    mm(bD[:, 128:256], kl[64:128, :], vt[64:128, :], True, True, [t_kl, t_vtok[hh][t4]], [kD],
                       tile_position=(64, 0))
                    yield
                    stt("dve", Sbf[hh][0], S32[hh], E12[:, 127:128], bD[:, 128:256], ALU.mult, ALU.add,
                        [t_S32[hh], t_E12, kD], [t_Sbf[hh][0]])
                    stt("dve", S32[hh], S32[hh], E12[:, 127:128], bD[:, 128:256], ALU.mult, ALU.add,
                        [t_S32[hh], t_E12, kD], [t_S32[hh]])
                    act(attm, bC[:, 0:128], AF.Square, [kC], [t_attm, t_hst], accum_out=hst[:, 0:1])
                    yield
                    act(hst[:, 1:2], hst[:, 0:1], AF.Ln, [t_hst], [t_hst], scale=1.0 / 128, bias=EPS)
                    act(hst[:, 2:3], hst[:, 1:2], AF.Exp, [t_hst], [t_hst], scale=-0.5)
                    yield
                    stt("dve", yb, bC[:, 0:128], hst[:, 2:3], gg, ALU.mult, ALU.mult, [kC, t_hst, t_gg], [t_yb])
                    yield
                    ytp = bD[:, 256:320].bitcast(BF16)
                    tr(ytp, yb, ident_bf, [t_yb, t_idb], [kD])
                    yield
                    cp("act", ys[:, hh * 2 + 1, tsl], ytp, [kD], [t_ys[hh * 2 + 1]])

                for t4 in range(4):
                    alive = [hgrn_gen(0, t4, HW[0], (2, 3, 4, 5)), hgrn_gen(1, t4, HW[1], (6, 7, 0, 1))]
                    while alive:
                        for g_ in list(alive):
                            try:
                                next(g_)
                            except StopIteration:
                                alive.remove(g_)
                for e_ in ("act", "dve"):
                    P.wait_all(e_, hw1_trks)
                checkpoint("hgrn%d" % blk)
                t_exb = Trk()
                P.dma("sp", ex_in.ap()[blk].rearrange("(a p) t -> p a t", p=128), ys,
                      d_yst[0], reads=t_ys, writes=[t_exb])
                P.wait_all("pool", [t_exb])
                if cc_n[0] > 0:
                    P._wait_for("pool", ("dma", ccsd, cc_n[0]))
                cc_n[0] += 1
                P.raw("pool", lambda e, blk=blk: e.collective_compute(
                    "AllGather", ALU.bypass, replica_groups=[[0, 1, 2, 3], [4, 5, 6, 7]],
                    ins=[ex_in.ap()[blk]], outs=[ex_out.ap()[blk]]).then_inc(ccs, 1))
                t_exout.w = ("dma", ccsd, cc_n[0])

            checkpoint("p1")
            if DEBUG:
                P.dma("sp", dbg_y, ex_in.ap(), P.dsem("dbgy"), reads=[t_exout])
            checkpoint("ag")
            P.barrier()
            AR.off = base_off

            uTo = AR.alloc([128, 16, NTOK], BF16)
            t_uTo = [Trk() for _ in range(8)]
            hole_off = AR.off
            mT = AR.alloc([128, 16, NTOK], BF16)
            t_mT = [[Trk() for _ in range(2)] for _ in range(16)]
            r2_off = AR.off
            yT = AR.alloc([128, 2, 8, NTOK], BF16); t_yT = Trk()
            d_y = P.dsem("d_y")
            P.wait_all("sp", [t_joff, t_exout])
            ex_v = ex_out.ap().rearrange("b (r hh k p) t -> p k (r hh) b t", r=4, hh=2, k=2)
            reg_holder = {}

            def _ld_reg(e):
                for b2 in range(2):
                    reg = e.alloc_register("joff%d" % b2)
                    e.reg_load(reg, joff_sb[0:1, b2:b2 + 1])
                    reg_holder[b2] = e.snap(reg, min_val=0, max_val=7)

            P.raw("sp", _ld_reg)
            for kind in range(2):
                for b2 in range(2):
                    d_y["val"] += 16
                    P.raw("sp", lambda e, kind=kind, b2=b2: e.dma_start(
                        out=yT[:, kind, :, b2 * 512:(b2 + 1) * 512],
                        in_=ex_v[:, kind, :, reg_holder[b2], :]).then_inc(d_y["sem"], 16))
            t_yT.w = ("dma", d_y, d_y["val"])

            checkpoint("exch")
            xs2 = [AR.alloc([128, D], F32) for _ in range(2)]
            t_xs2 = [Trk(), Trk()]
            u_bfs = [AR.alloc([128, D], BF16) for _ in range(2)]; t_us = [Trk(), Trk()]
            stats = [AR.alloc([128, 8], F32) for _ in range(2)]; t_stats = [Trk(), Trk()]

            def _s1o(T_):
                norm_s1(T_, xo[T_ * 128:(T_ + 1) * 128, :], xs2[T_ % 2], t_xs2[T_ % 2], d_xs[T_ % 2])

            _s1o(0)
            for T in range(8):
                if T + 1 < 8:
                    _s1o(T + 1)
                norm_s2(T, uTo[:, :, T * 128:(T + 1) * 128], t_uTo[T])
            checkpoint("s0")
            wga = [AR.alloc([128, 16, 256], BF16) for _ in range(2)]
            wgr = [AR.alloc([128, 16, 256], BF16) for _ in range(2)]
            wao = [AR.alloc([128, 8, 256], BF16) for _ in range(2)]
            wro = [AR.alloc([128, 8, 256], BF16) for _ in range(2)]
            t_wa = [Trk(), Trk()]
            d_wa = [P.dsem("d_wa0"), P.dsem("d_wa1")]
            sga = AR.alloc([128, 512], F32); sgr = AR.alloc([128, 512], F32)
            t_sga, t_sgr = Trk(), Trk()
            print("phase2a arena bytes", AR.off)
            wg_v = wg_d.rearrange("(kt p) n -> p kt n", p=128)
            wao_v = wao_d.rearrange("(kt p) n -> p kt n", p=128)
            wro_v = wro_d.rearrange("(kt p) n -> p kt n", p=128)
            def _ld_a(cg_):
                s_ = cg_ % 2
                c0_ = cg_ * 256
                P.dma("pool", wga[s_], wg_v[:, :, c0_:c0_ + 256], d_wa[s_], writes=[t_wa[s_]])
                P.dma("pool", wgr[s_], wg_v[:, :, 2048 + c0_:2048 + c0_ + 256], d_wa[s_], writes=[t_wa[s_]])
                P.dma("pool", wao[s_], wao_v[:, :, c0_:c0_ + 256], d_wa[s_], writes=[t_wa[s_]])
                P.dma("pool", wro[s_], wro_v[:, :, c0_:c0_ + 256], d_wa[s_], writes=[t_wa[s_]])

            _ld_a(0)
            for cg in range(8):
                s = cg % 2
                if cg + 1 < 8:
                    _ld_a(cg + 1)
                for c2 in range(2):
                    ct = cg * 2 + c2
                    for tb in range(2):
                        tks = slice(tb * 512, (tb + 1) * 512)
                        t_u4 = t_uTo[tb * 4:(tb + 1) * 4]
                        b_ga, b_gr, b_a, b_r = 0 + 4 * tb, 1 + 4 * tb, 2 + 4 * tb, 3 + 4 * tb
                        fm_proj(wga[s], t_wa[s], c2 * 128, lambda kt: uTo[:, kt, tks], t_u4, b_ga)
                        fm_proj(wgr[s], t_wa[s], c2 * 128, lambda kt: uTo[:, kt, tks], t_u4, b_gr)
                        fm_proj(wao[s], t_wa[s], c2 * 128, lambda kt: yT[:, 0, kt, tks], [t_yT], b_a, nk=8)
                        fm_proj(wro[s], t_wa[s], c2 * 128, lambda kt: yT[:, 1, kt, tks], [t_yT], b_r, nk=8)
                        act(sga, banks[b_ga], AF.Sigmoid, [bk[b_ga]], [t_sga])
                        act(sgr, banks[b_gr], AF.Sigmoid, [bk[b_gr]], [t_sgr])
                        tt("dve", sga, banks[b_a], sga, ALU.mult, [bk[b_a], t_sga], [t_sga])
                        tt("dve", sgr, banks[b_r], sgr, ALU.mult, [bk[b_r], t_sgr], [t_sgr])
                        tt("dve", mT[:, ct, tks], sga, sgr, ALU.add, [t_sga, t_sgr], [t_mT[ct][tb]])
            P.barrier()
            AR.off = r2_off
            checkpoint("sa")
            h = AR.alloc([128, 8, D], F32)
            t_h = [Trk() for _ in range(8)]
            d_h = P.dsem("d_h")
            for T in range(8):
                P.dma("sp", h[:, T, :], xo[T * 128:(T + 1) * 128, :], d_h, writes=[t_h[T]])
            for T in range(8):
                t_h[T].w = ("dma", d_h, d_h["val"])
            Grow = AR.alloc([128, D], F32); t_Grow = Trk()
            load_row(Grow, modflat[0:1, 4096:6144], None, [t_Grow], [t_modout])
            r3_off = AR.off
            wos = [AR.alloc([128, 16, 512], BF16) for _ in range(2)]
            t_wos = [Trk(), Trk()]
            d_wos = [P.dsem("d_wos0"), P.dsem("d_wos1")]
            print("phase2b arena bytes", AR.off)
            wo_v = wo_d.rearrange("(kt p) n -> p kt n", p=128)
            def _ld_b(cb_):
                s_ = cb_ % 2
                cols_ = slice(cb_ * 512, (cb_ + 1) * 512)
                P.dma("pool", wos[s_], wo_v[:, :, cols_], d_wos[s_], writes=[t_wos[s_]])

            def _sc_b(cb_):
                s_ = cb_ % 2
                cols_ = slice(cb_ * 512, (cb_ + 1) * 512)
                for ct_ in range(16):
                    tt("dve", wos[s_][:, ct_, :], wos[s_][:, ct_, :], Grow[:, cols_], ALU.mult,
                       [t_wos[s_], t_Grow], [t_wos[s_]])

            _ld_b(0)
            _sc_b(0)
            for cb in range(4):
                s = cb % 2
                cols = slice(cb * 512, (cb + 1) * 512)
                if cb + 1 < 4:
                    _ld_b(cb + 1)
                for T in range(8):
                    b_ = T % 4
                    for ct in range(16):
                        mm(banks[b_][:, :], mT[:, ct, T * 128:(T + 1) * 128], wos[s][:, ct, :], ct == 0, ct == 15,
                           [t_mT[ct][T // 4], t_wos[s]], [bk[b_]], signal=(ct == 15))
                    tt("dve", h[:, T, cols], h[:, T, cols], banks[b_], ALU.add, [t_h[T], bk[b_]], [t_h[T]])
                if cb + 1 < 4:
                    _sc_b(cb + 1)
            P.barrier()
            checkpoint("sb")
            load_row(Arow, modflat[0:1, 8192:10240], None, [t_Arow], [t_modout])
            load_row(Brow, n2g_d, None, [t_Brow], [])
            stt("dve", Arow, Arow, 1.0, Brow, ALU.add, ALU.mult, [t_Arow, t_Brow], [t_Arow])
            load_row(Brow, modflat[0:1, 6144:8192], None, [t_Brow], [t_modout])
            AR.off = hole_off
            aTs = [AR.alloc([128, 4, NTOK], BF16) for _ in range(2)]
            t_aTs = [[[Trk() for _ in range(2)] for _ in range(4)] for _ in range(2)]
            wd = [AR.alloc([128, 4, 512], BF16) for _ in range(2)]
            t_wd = [Trk(), Trk()]
            d_wd = [P.dsem("d_wd0"), P.dsem("d_wd1")]
            stats = [AR.alloc([128, 8], F32) for _ in range(2)]; t_stats = [Trk(), Trk()]
            assert AR.off <= r2_off, (AR.off, r2_off)
            AR.off = r3_off
            tmpxs = [AR.alloc([128, D], F32) for _ in range(2)]; t_tmpxs = [Trk(), Trk()]
            u_bfs = [AR.alloc([128, D], BF16) for _ in range(2)]; t_us = [Trk(), Trk()]

            def norm_h(T):
                st_, t_st = stats[T % 2], t_stats[T % 2]
                ub, t_ub = u_bfs[T % 2], t_us[T % 2]
                act(ub, h[:, T, :], AF.Square, [t_h[T]], [t_ub, t_st], accum_out=st_[:, 0:1])
                act(st_[:, 1:2], st_[:, 0:1], AF.Sqrt, [t_st], [t_st], scale=1.0 / D, bias=EPS)
                recip(st_[:, 2:3], st_[:, 1:2], [t_st], [t_st])

            def _c1(T):
                norm_h(T)
                stt("dve", tmpxs[T % 2], h[:, T, :], stats[T % 2][:, 2:3], Arow, ALU.mult, ALU.mult,
                    [t_h[T], t_stats[T % 2], t_Arow], [t_tmpxs[T % 2]])
                tt("pool", u_bfs[T % 2][:, 0:1024], tmpxs[T % 2][:, 0:1024], Brow[:, 0:1024], ALU.add,
                   [t_tmpxs[T % 2], t_Brow], [t_us[T % 2]])
                tt("dve", u_bfs[T % 2][:, 1024:2048], tmpxs[T % 2][:, 1024:2048], Brow[:, 1024:2048], ALU.add,
                   [t_tmpxs[T % 2], t_Brow], [t_us[T % 2]])

            _c1(0)
            for T in range(8):
                if T + 1 < 8:
                    _c1(T + 1)
                norm_s2(T, uTo[:, :, T * 128:(T + 1) * 128], t_uTo[T])
            P.wait_all("pool", t_tmpxs + t_us)
            checkpoint("sc")
            load_row(Grow, modflat[0:1, 10240:12288], None, [t_Grow], [t_modout])
            AR.off = r3_off
            wgu = [AR.alloc([128, 2, 16, 256], BF16) for _ in range(2)]
            t_wgu = [Trk(), Trk()]
            d_wgu = [P.dsem("d_wgu0"), P.dsem("d_wgu1")]
            sil = [AR.alloc([128, 512], F32) for _ in range(2)]
            t_sil = [Trk(), Trk()]
            print("phase2d arena bytes", AR.off)
            wfg_v = wfg_d.rearrange("(kt p) n -> p kt n", p=128)
            wfu_v = wfu_d.rearrange("(kt p) n -> p kt n", p=128)
            wfd_v = wfd_d.rearrange("(ht p) n -> p ht n", p=128)
            stages = []
            gcount = 0
            dcount = 0
            for grp in range(11):
                for sg in range(2):
                    stages.append(("gu", grp, sg, gcount % 2))
                    gcount += 1
                for cb in range(4):
                    stages.append(("dn", grp, cb, dcount % 2))
                    dcount += 1

            def _ld_d(st_):
                kind_, grp_, x_, s_ = st_
                if kind_ == "gu":
                    hc0 = (grp_ * 4 + x_ * 2) * 128
                    P.dma("pool", wgu[s_][:, 0, :, :], wfg_v[:, :, hc0:hc0 + 256], d_wgu[s_], writes=[t_wgu[s_]])
                    P.dma("pool", wgu[s_][:, 1, :, :], wfu_v[:, :, hc0:hc0 + 256], d_wgu[s_], writes=[t_wgu[s_]])
                else:
                    cols_ = slice(x_ * 512, (x_ + 1) * 512)
                    P.dma("pool", wd[s_], wfd_v[:, grp_ * 4:(grp_ + 1) * 4, cols_], d_wd[s_], writes=[t_wd[s_]])

            def _sc_d(st_):
                kind_, grp_, x_, s_ = st_
                if kind_ == "dn":
                    cols_ = slice(x_ * 512, (x_ + 1) * 512)
                    for hl_ in range(4):
                        tt("dve", wd[s_][:, hl_, :], wd[s_][:, hl_, :], Grow[:, cols_], ALU.mult,
                           [t_wd[s_], t_Grow], [t_wd[s_]])

            def _cmp_d(st_):
                kind_, grp_, x_, s = st_
                if kind_ == "gu":
                    sg = x_
                    for gi in range(2):
                        hl = sg * 2 + gi
                        for tb in range(2):
                            tks = slice(tb * 512, (tb + 1) * 512)
                            t_u4 = t_uTo[tb * 4:(tb + 1) * 4]
                            b_g = 4 + 2 * tb
                            b_u = 5 + 2 * tb
                            for kt in range(16):
                                mm(banks[b_g][:, :], wgu[s][:, 0, kt, gi * 128:(gi + 1) * 128], uTo[:, kt, tks],
                                   kt == 0, kt == 15, [t_wgu[s]] + t_u4, [bk[b_g]], signal=(kt == 15))
                            for kt in range(16):
                                mm(banks[b_u][:, :], wgu[s][:, 1, kt, gi * 128:(gi + 1) * 128], uTo[:, kt, tks],
                                   kt == 0, kt == 15, [t_wgu[s]] + t_u4, [bk[b_u]], signal=(kt == 15))
                            act(sil[tb], banks[b_g], AF.Silu, [bk[b_g]], [t_sil[tb]])
                            tt("dve", aTs[grp_ % 2][:, hl, tks], banks[b_u], sil[tb], ALU.mult, [bk[b_u], t_sil[tb]],
                               [t_aTs[grp_ % 2][hl][tb]])
                else:
                    cols = slice(x_ * 512, (x_ + 1) * 512)
                    for T in range(8):
                        b_ = T
                        for hl in range(4):
                            mm(banks[b_][:, :], aTs[grp_ % 2][:, hl, T * 128:(T + 1) * 128], wd[s][:, hl, :], hl == 0,
                               hl == 3, [t_aTs[grp_ % 2][hl][T // 4], t_wd[s]], [bk[b_]], signal=(hl == 3))
                        tt("dve", h[:, T, cols], h[:, T, cols], banks[b_], ALU.add, [t_h[T], bk[b_]], [t_h[T]])

            gu_list = [st_ for st_ in stages if st_[0] == "gu"]
            dn_list = [st_ for st_ in stages if st_[0] == "dn"]
            nxt = {"gu": 2, "dn": 2}
            lists = {"gu": gu_list, "dn": dn_list}
            for st_ in gu_list[:2] + dn_list[:2]:
                _ld_d(st_)
            _sc_d(stages[0])
            for i_, st_ in enumerate(stages):
                _cmp_d(st_)
                k_ = st_[0]
                if nxt[k_] < len(lists[k_]):
                    _ld_d(lists[k_][nxt[k_]])
                    nxt[k_] += 1
                if i_ + 1 < len(stages):
                    _sc_d(stages[i_ + 1])
            checkpoint("sd")
            load_row(Arow, fg_d, None, [t_Arow], [])
            if DEBUG:
                for T in range(8):
                    P.dma("sp", dbg_h[T * 128:(T + 1) * 128, :], h[:, T, :], P.dsem("dbgh%d" % T), reads=[t_h[T]])
            for T in range(8):
                norm_h(T)
                stt("dve", h[:, T, :], h[:, T, :], stats[T % 2][:, 2:3], Arow, ALU.mult, ALU.mult,
                    [t_h[T], t_stats[T % 2], t_Arow], [t_h[T]])
                P.dma("sp", out_d[T * 128:(T + 1) * 128, :], h[:, T, :], dout, reads=[t_h[T]])

        except _Stop:
            P.barrier()
        if DEBUG:
            for d_ in P.dma_sems:
                if d_["name"].startswith("dbg"):
                    P._wait_for("sp", ("dma", d_, d_["val"]))
        P.raw("sp", lambda e: e.wait_ge(dout["sem"], dout["val"]))
        P.finish()
        print("instructions", P.ninstr, "waits", P.nwaits, "counts", P.cnt)
    return nc


def _consts():
    idx = np.arange(128)
    ch = idx // 64
    same = ch[:, None] == ch[None, :]
    ident = np.eye(128, dtype=np.float32)
    U = (same & (idx[:, None] <= idx[None, :])).astype(np.float32)
    mid = ch * 64 + 31
    Umid = (same & (idx[:, None] <= mid[None, :])).astype(np.float32)
    Um = U - Umid
    L = (same & (idx[:, None] > idx[None, :])).astype(np.float32)
    return np.ascontiguousarray(np.concatenate([ident, U, Um, L], axis=1))


def make_in_maps(x, c, w_mod, b_mod, norm1_g, w_in, lambda_q1, lambda_k1, lambda_q2, lambda_k2,
                 subln_g, lb_logits, gnorm_g, w_att_out, w_rec_out, w_o, norm2_g,
                 w_ffn_gate, w_ffn_up, w_ffn_down, final_g):
    f = lambda a: np.ascontiguousarray(np.asarray(a, dtype=np.float32))
    x = f(x); c = f(c); w_in0 = f(w_in)[0]; w_mod0 = f(w_mod)[0]; b_mod0 = f(b_mod)[0]
    lbl = f(lb_logits); gn = f(gnorm_g)[0]
    AQ, AK, AV, RQ, RF, RI, RG, GA = 0, 1024, 2048, 3072, 4096, 5120, 6144, 7168
    shared = {
        "w_g": np.ascontiguousarray(w_in0[:, GA:GA + 4096]),
        "w_ao": f(w_att_out)[0], "w_ro": f(w_rec_out)[0], "w_o": f(w_o)[0],
        "w_fg": f(w_ffn_gate)[0], "w_fu": f(w_ffn_up)[0], "w_fd": f(w_ffn_down)[0],
        "n1g": f(norm1_g)[0][None], "n2g": f(norm2_g)[0][None], "fg": f(final_g)[None],
        "subg": f(subln_g)[0][:, None],
        "lamv": np.concatenate([f(lambda_q1)[0], f(lambda_k1)[0], f(lambda_q2)[0], f(lambda_k2)[0]])[None],
        "consts": _consts(),
    }
    shared = {k: np.ascontiguousarray(v) for k, v in shared.items()}
    maps = []
    for core in range(8):
        b, j = core // 4, core % 4
        heads = (2 * j, 2 * j + 1)
        sl = lambda off, h: w_in0[:, off + h * 128: off + (h + 1) * 128]
        w_fm = np.concatenate([np.concatenate([sl(AQ, h), sl(AK, h), sl(RQ, h), sl(RF, h)], axis=1) for h in heads], axis=1)
        w_tm = np.concatenate([np.concatenate([sl(AV, h), sl(RI, h), sl(RG, h), sl(RF, h)], axis=1) for h in heads], axis=1)
        lbc = np.stack([np.stack([lbl[l, h * 128:(h + 1) * 128] for l in range(2)], axis=1) for h in heads], axis=1)
        lbr = np.stack([np.stack([lbl[l, h * 128:(h + 1) * 128] for l in range(2)], axis=0) for h in heads], axis=0)
        m = dict(shared)
        m.update({
            "xb": x[b],
            "xo": np.ascontiguousarray(x[b, j * NTOK:(j + 1) * NTOK]),
            "cT": np.ascontiguousarray(c[b].reshape(16, 128).T),
            "wmod": np.ascontiguousarray(w_mod0[:, j * 3072:(j + 1) * 3072]),
            "bmod": np.ascontiguousarray(b_mod0[j * 3072:(j + 1) * 3072][None]),
            "w_fm": np.ascontiguousarray(w_fm), "w_tm": np.ascontiguousarray(w_tm),
            "lbc": np.ascontiguousarray(lbc.reshape(128, 4)),
            "lbr": np.ascontiguousarray(lbr.reshape(1, 512)),
            "gnr": np.ascontiguousarray(np.stack([gn[h] for h in heads], axis=0).reshape(1, 256)),
            "joff": np.array([[2 * j, 2 * j + 1]], dtype=np.int32),
        })
        maps.append(m)
    return maps


_NC_CACHE = {}


def kernel(**inputs):
    in_maps = make_in_maps(**inputs)
    if "nc" not in _NC_CACHE:
        _NC_CACHE["nc"] = build_nc()
    nc = _NC_CACHE["nc"]
    res = run_bass_kernel_spmd(nc, in_maps, core_ids=list(range(8)))
    out = np.zeros((2, SEQ, D), dtype=np.float32)
    for core in range(8):
        b, j = core // 4, core % 4
        out[b, j * NTOK:(j + 1) * NTOK] = np.asarray(res.results[core]["out"], dtype=np.float32)
    if DEBUG:
        kernel.last = res
    return out
```

```python
import math
from contextlib import ExitStack

import numpy as np
import concourse.bass as bass
import concourse.mybir as mybir
from concourse.bass_utils import run_bass_kernel_spmd

F32 = mybir.dt.float32
BF16 = mybir.dt.bfloat16
I32 = mybir.dt.int32
U8 = mybir.dt.uint8
AF = mybir.ActivationFunctionType
ALU = mybir.AluOpType

D = 2048
SEQ = 4096
NTOK = 1024
FF = 5632
EPS = 1e-6
LAM_INIT = 0.8 - 0.6 * math.exp(-0.3 * 0)
ENGS = ("pe", "act", "dve", "pool", "sp")
DEBUG = False
STOP = None
SHAPE_OVR = {}


class Trk:
    __slots__ = ("w", "r")

    def __init__(self):
        self.w = None
        self.r = []


class Prog:
    def __init__(self, nc, stack):
        self.nc = nc
        self.stack = stack
        self.q = {e: [] for e in ENGS}
        self.cnt = {e: 0 for e in ENGS}
        self.sem = {e: stack.enter_context(nc.semaphore("s_" + e)) for e in ENGS}
        self.seen = {e: {} for e in ENGS}
        self.dma_sems = []
        self.ninstr = 0
        self.nwaits = 0

    def ps(self, name, shape, dt):
        return self.stack.enter_context(self.nc.psum_tensor(name, list(shape), dt))

    def dsem(self, name):
        s = self.stack.enter_context(self.nc.semaphore(name))
        d = {"sem": s, "val": 0, "name": name}
        self.dma_sems.append(d)
        return d

    def _wait_for(self, eng, ev):
        if ev is None:
            return
        kind, key, val = ev
        if kind == "eng" and key == "pe" and eng == "pe":
            return
        if kind == "eng":
            semkey = "E" + key
            sem = self.sem[key]
        else:
            semkey = "D" + key["name"]
            sem = key["sem"]
        if self.seen[eng].get(semkey, 0) >= val:
            return
        self.seen[eng][semkey] = val
        self.nwaits += 1
        self.q[eng].append(lambda e, sem=sem, val=val: e.wait_ge(sem, val))

    def _deps(self, eng, reads, writes):
        for t in reads:
            self._wait_for(eng, t.w)
        for t in writes:
            self._wait_for(eng, t.w)
            for ev in t.r:
                self._wait_for(eng, ev)

    def _commit(self, ev, reads, writes):
        for t in reads:
            t.r.append(ev)
            if len(t.r) > 48:
                last = {}
                for e_ in t.r:
                    k = (e_[0], e_[1] if e_[0] == "eng" else e_[1]["name"])
                    if k not in last or last[k][2] < e_[2]:
                        last[k] = e_
                t.r = list(last.values())
        for t in writes:
            t.w = ev
            t.r = []

    def op(self, eng, fn, reads=(), writes=(), signal=True):
        self._deps(eng, reads, writes)
        self.ninstr += 1
        if signal:
            self.cnt[eng] += 1
            sem = self.sem[eng]
            self.q[eng].append(lambda e, fn=fn, sem=sem: fn(e).then_inc(sem, 1))
            ev = ("eng", eng, self.cnt[eng])
        else:
            assert eng == "pe"
            self.q[eng].append(lambda e, fn=fn: fn(e))
            ev = ("eng", eng, self.cnt[eng] + 1)
        self._commit(ev, reads, writes)
        return ev

    def dma(self, eng, out, in_, ds, reads=(), writes=(), **kw):
        self._deps(eng, reads, writes)
        self.ninstr += 1
        ds["val"] += 16
        sem = ds["sem"]
        self.q[eng].append(
            lambda e, out=out, in_=in_, sem=sem, kw=kw: e.dma_start(out=out, in_=in_, **kw).then_inc(sem, 16))
        ev = ("dma", ds, ds["val"])
        self._commit(ev, reads, writes)
        return ev

    def raw(self, eng, fn):
        self.q[eng].append(fn)

    def wait_all(self, eng, trks):
        for t in trks:
            self._wait_for(eng, t.w)
            for ev in t.r:
                self._wait_for(eng, ev)

    def barrier(self):
        for e in ENGS:
            for e2 in ENGS:
                if e2 != e and self.cnt[e2] > 0:
                    self._wait_for(e, ("eng", e2, self.cnt[e2]))
            for d in self.dma_sems:
                if d["val"] > 0:
                    self._wait_for(e, ("dma", d, d["val"]))

    def finish(self):
        nc = self.nc
        q = self.q
        with nc.Block() as block:
            @block.tensor
            def _(e):
                for f in q["pe"]:
                    f(e)

            @block.scalar
            def _(e):
                for f in q["act"]:
                    f(e)

            @block.vector
            def _(e):
                for f in q["dve"]:
                    f(e)

            @block.gpsimd
            def _(e):
                for f in q["pool"]:
                    f(e)

            @block.sync
            def _(e):
                for f in q["sp"]:
                    f(e)


class Arena:
    def __init__(self, ap_u8, size):
        self.ap = ap_u8
        self.size = size
        self.off = 0

    def alloc(self, shape, dt, parts=None):
        esz = {F32: 4, BF16: 2, I32: 4}[dt]
        n = 1
        for s in shape[1:]:
            n *= s
        nb = (n * esz + 31) // 32 * 32
        assert self.off + nb <= self.size, ("arena overflow", self.off, nb, self.size)
        v = self.ap[0:shape[0], self.off:self.off + n * esz].bitcast(dt)
        self.off += nb
        if len(shape) == 3:
            v = v.rearrange("p (a b) -> p a b", a=shape[1])
        elif len(shape) == 4:
            v = v.rearrange("p (a b c) -> p a b c", a=shape[1], b=shape[2])
        return v


def build_nc():
    nc = bass.Bass("TRN2", target_bir_lowering=False)

    def din(name, shape, dt=F32):
        shape = SHAPE_OVR.get(name, shape)
        return nc.dram_tensor(name, list(shape), dt, kind="ExternalInput").ap()

    xb = din("xb", [SEQ, D])
    xo = din("xo", [NTOK, D])
    cT_d = din("cT", [128, 16])
    wmod_d = din("wmod", [D, 3072])
    bmod_d = din("bmod", [1, 3072])
    wfm_d = din("w_fm", [D, 1024])
    wtm_d = din("w_tm", [D, 1024])
    wg_d = din("w_g", [D, 4096])
    wao_d = din("w_ao", [1024, D])
    wro_d = din("w_ro", [1024, D])
    wo_d = din("w_o", [D, D])
    wfg_d = din("w_fg", [D, FF])
    wfu_d = din("w_fu", [D, FF])
    wfd_d = din("w_fd", [FF, D])
    n1g_d = din("n1g", [1, D])
    n2g_d = din("n2g", [1, D])
    fg_d = din("fg", [1, D])
    lbc_d = din("lbc", [128, 4])
    lbr_d = din("lbr", [1, 512])
    gnr_d = din("gnr", [1, 256])
    subg_d = din("subg", [128, 1])
    lamv_d = din("lamv", [1, 256])
    consts_d = din("consts", [128, 512])
    joff_d = din("joff", [1, 2], I32)
    out_d = nc.dram_tensor("out", [NTOK, D], F32, kind="ExternalOutput").ap()
    if DEBUG:
        dbg_y = nc.dram_tensor("dbg_y", [8, 512, 512], BF16, kind="ExternalOutput").ap()
        dbg_h = nc.dram_tensor("dbg_h", [NTOK, D], F32, kind="ExternalOutput").ap()
        dbg_mod = nc.dram_tensor("dbg_mod", [4, 3072], F32, kind="ExternalOutput").ap()

    mod_in = nc.dram_tensor("mod_in", [1, 3072], F32)
    mod_out = nc.dram_tensor("mod_out", [4, 3072], F32)
    ex_in = nc.dram_tensor("ex_in", [8, 512, 512], BF16)
    ex_out = nc.dram_tensor("ex_out", [8, 2048, 512], BF16)
    t_modin, t_modout, t_exin, t_exout = Trk(), Trk(), Trk(), Trk()

    with ExitStack() as st:
        P = Prog(nc, st)
        ARENA_SZ = 206 * 1024
        arena_t = st.enter_context(nc.sbuf_tensor("arena", [128, ARENA_SZ], U8))
        AR = Arena(arena_t, ARENA_SZ)
        banks = [P.ps("bank%d" % i, [128, 512], F32)[:, :] for i in range(8)]
        bk = [Trk() for _ in range(8)]
        dout = P.dsem("dout")
        ccs = st.enter_context(nc.semaphore("ccs"))
        cc_n = [0]
        ccsd = {"sem": ccs, "val": 0, "name": "ccs"}

        class _Stop(Exception):
            pass

        def checkpoint(name):
            if STOP == name:
                raise _Stop()

        def act(out, in_, func, R, W, **kw):
            P.op("act", lambda e: e.activation(out=out, in_=in_, func=func, **kw), R, W)

        def tt(eng, out, in0, in1, op, R, W):
            P.op(eng, lambda e: e.tensor_tensor(out=out, in0=in0, in1=in1, op=op), R, W)

        def ts(eng, out, in0, s1, s2, op0, op1, R, W):
            P.op(eng, lambda e: e.tensor_scalar(out=out, in0=in0, scalar1=s1, scalar2=s2, op0=op0, op1=op1), R, W)

        def stt(eng, out, in0, scalar, in1, op0, op1, R, W):
            P.op(eng, lambda e: e.scalar_tensor_tensor(out=out, in0=in0, scalar=scalar, in1=in1, op0=op0, op1=op1),
                 R, W)

        def cp(eng, out, in_, R, W):
            if eng == "act":
                act(out, in_, AF.Copy, R, W)
            else:
                P.op(eng, lambda e: e.tensor_copy(out=out, in_=in_), R, W)

        def recip(out, in_, R, W):
            P.op("dve", lambda e: e.reciprocal(out=out, in_=in_), R, W)

        def mm(out, lhsT, rhs, start, stop, R, W, signal=True, **kw):
            P.op("pe", lambda e: e.matmul(out, lhsT=lhsT, rhs=rhs, start=start, stop=stop, **kw), R, W, signal=signal)

        def tr(out, in_, ident, R, W, signal=True):
            P.op("pe", lambda e: e.transpose(out=out, in_=in_, identity=ident), R, W, signal=signal)

        def memset(eng, ap, val, W):
            P.op(eng, lambda e: e.memset(ap, val), (), W)

        def collective(kind, groups, src, dst, R, W):
            P.wait_all("pool", R + W)
            cc_n[0] += 1
            n = cc_n[0]
            P.raw("pool", lambda e: e.collective_compute(kind, ALU.bypass, replica_groups=groups,
                                                         ins=[src], outs=[dst]).then_inc(ccs, 1))
            P.raw("pool", lambda e: e.wait_ge(ccs, n))
            ev = P.op("pool", lambda e: e.memset(cc_dummy[:], 0.0), (), [t_ccd])
            for t in W:
                t.w = ev
                t.r = []

        try:
            cc_dummy = AR.alloc([128, 8], F32)
            t_ccd = Trk()
            consts32 = AR.alloc([128, 4, 128], F32); t_c32 = Trk()
            ident_bf = AR.alloc([128, 128], BF16); t_idb = Trk()
            maskU_bf = AR.alloc([128, 128], BF16); t_mku = Trk()
            ones_bf = AR.alloc([128, 128], BF16); t_ones = Trk()
            d_c = P.dsem("d_c")
            P.dma("sp", consts32.rearrange("p a b -> p (a b)"), consts_d, d_c, writes=[t_c32])
            cp("dve", ident_bf, consts32[:, 0, :], [t_c32], [t_idb])
            cp("dve", maskU_bf, consts32[:, 1, :], [t_c32], [t_mku])
            memset("pool", ones_bf, 1.0, [t_ones])
            ones32 = AR.alloc([128, 128], F32); t_ones32 = Trk()
            memset("pool", ones32, 1.0, [t_ones32])
            U32 = consts32[:, 1, :]
            Um32 = consts32[:, 2, :]
            L32 = consts32[:, 3, :]

            lbc = AR.alloc([128, 4], F32); t_lbc = Trk()
            lbr = AR.alloc([128, 512], F32); t_lbr = Trk()
            gnr = AR.alloc([128, 256], F32); t_gnr = Trk()
            subg = AR.alloc([128, 1], F32); t_subg = Trk()
            lamv = AR.alloc([128, 256], F32); t_lamv = Trk()
            d_sm = P.dsem("d_sm")
            P.dma("sp", lbc, lbc_d, d_sm, writes=[t_lbc])
            P.dma("sp", lbr.unsqueeze(1), lbr_d.partition_broadcast(128), d_sm, writes=[t_lbr])
            P.dma("sp", gnr.unsqueeze(1), gnr_d.partition_broadcast(128), d_sm, writes=[t_gnr])
            P.dma("sp", subg, subg_d, d_sm, writes=[t_subg])
            P.dma("sp", lamv.unsqueeze(1), lamv_d.partition_broadcast(128), d_sm, writes=[t_lamv])
            joff_sb = AR.alloc([1, 8], I32); t_joff = Trk()
            P.dma("sp", joff_sb[0:1, 0:2], joff_d, d_sm, writes=[t_joff])
            cT = AR.alloc([128, 16], F32); t_cT = Trk()
            P.dma("sp", cT, cT_d, d_sm, writes=[t_cT])
            for t_ in (t_lbc, t_lbr, t_gnr, t_subg, t_lamv, t_joff, t_cT):
                t_.w = ("dma", d_sm, d_sm["val"])
            lbcol = AR.alloc([128, 2], F32)
            omlcol = AR.alloc([128, 2], F32)
            dcol = AR.alloc([128, 2], F32)
            t_lbcol = Trk()
            lbc3 = lbc.rearrange("p (h l) -> p h l", h=2)
            tt("dve", dcol, lbc3[:, :, 0], lbc3[:, :, 1], ALU.subtract, [t_lbc], [t_lbcol])
            act(lbcol, dcol, AF.Sigmoid, [t_lbcol], [t_lbcol])
            act(omlcol, dcol, AF.Sigmoid, [t_lbcol], [t_lbcol], scale=-1.0)
            lbrow = AR.alloc([128, 2, 128], F32)
            omlrow = AR.alloc([128, 2, 128], F32)
            drow = AR.alloc([128, 2, 128], F32)
            t_lbrow = Trk()
            lbr4 = lbr.rearrange("p (h l d) -> p h l d", h=2, l=2)
            tt("dve", drow, lbr4[:, :, 0, :], lbr4[:, :, 1, :], ALU.subtract, [t_lbr], [t_lbrow])
            act(lbrow, drow, AF.Sigmoid, [t_lbrow], [t_lbrow])
            act(omlrow, drow, AF.Sigmoid, [t_lbrow], [t_lbrow], scale=-1.0)
            gnrow = gnr.rearrange("p (h e) -> p h e", h=2)
            gsub = AR.alloc([128, 1], F32); t_gsub = Trk()
            ts("dve", gsub, subg, 1.0 - LAM_INIT, 0.0, ALU.mult, ALU.add, [t_subg], [t_gsub])
            lam_t = AR.alloc([128, 8], F32); t_lam = Trk()
            lprod = AR.alloc([128, 2, 64], F32)
            lv = lamv.rearrange("p (a d) -> p a d", a=4)
            tt("dve", lprod[:, 0, :], lv[:, 0, :], lv[:, 1, :], ALU.mult, [t_lamv], [t_lam])
            tt("dve", lprod[:, 1, :], lv[:, 2, :], lv[:, 3, :], ALU.mult, [t_lamv], [t_lam])
            P.op("dve", lambda e: e.tensor_reduce(out=lam_t[:, 0:2], in_=lprod, axis=mybir.AxisListType.X, op=ALU.add),
                 [t_lam], [t_lam])
            act(lam_t[:, 2:4], lam_t[:, 0:2], AF.Exp, [t_lam], [t_lam])
            tt("dve", lam_t[:, 4:5], lam_t[:, 3:4], lam_t[:, 2:3], ALU.subtract, [t_lam], [t_lam])
            ts("dve", lam_t[:, 5:6], lam_t[:, 4:5], 1.0, -LAM_INIT, ALU.mult, ALU.add, [t_lam], [t_lam])
            neglam = lam_t[:, 5:6]

            Arow = AR.alloc([128, D], F32); t_Arow = Trk()
            Brow = AR.alloc([128, D], F32); t_Brow = Trk()
            base_off = AR.off

            wfm = AR.alloc([128, 16, 1024], BF16)
            wtm = AR.alloc([128, 16, 1024], BF16)
            p1_off = AR.off
            cond = AR.alloc([128, 16], BF16); t_cond = Trk()
            bmod = AR.alloc([1, 3072], F32); t_bmod = Trk()
            modrow = AR.alloc([1, 3072], F32); t_modrow = Trk()
            wm = [AR.alloc([128, 16, 512], BF16) for _ in range(2)]
            t_wm = [Trk(), Trk()]
            d_wm = [P.dsem("d_wm0"), P.dsem("d_wm1")]
            d_bm = P.dsem("d_bm")
            P.dma("sp", bmod, bmod_d, d_bm, writes=[t_bmod])
            act(cond, cT, AF.Silu, [t_cT], [t_cond])
            wmod_v = wmod_d.rearrange("(kt p) n -> p kt n", p=128)
            for cb in range(6):
                s = cb % 2
                P.dma("pool", wm[s], wmod_v[:, :, cb * 512:(cb + 1) * 512], d_wm[s], writes=[t_wm[s]])
                for kt in range(16):
                    mm(banks[cb % 2][0:1, :], cond[:, kt:kt + 1], wm[s][:, kt, :], kt == 0, kt == 15,
                       [t_cond, t_wm[s]], [bk[cb % 2]], signal=(kt == 15))
                tt("dve", modrow[0:1, cb * 512:(cb + 1) * 512], banks[cb % 2][0:1, :], bmod[0:1, cb * 512:(cb + 1) * 512],
                   ALU.add, [bk[cb % 2], t_bmod], [t_modrow])
            d_mod = P.dsem("d_mod")
            P.dma("sp", mod_in.ap(), modrow, d_mod, reads=[t_modrow], writes=[t_modin])
            collective("AllGather", [[0, 1, 2, 3], [4, 5, 6, 7]], mod_in.ap(), mod_out.ap(), [t_modin], [t_modout])
            wfm_v = wfm_d.rearrange("(kt p) n -> p kt n", p=128)
            wtm_v = wtm_d.rearrange("(kt p) n -> p kt n", p=128)
            t_wfmh = [Trk(), Trk()]
            t_wtmh = [Trk(), Trk()]
            for hlf in range(2):
                P.dma("pool", wfm[:, hlf * 8:(hlf + 1) * 8, :], wfm_v[:, hlf * 8:(hlf + 1) * 8, :],
                      P.dsem("d_wfm%d" % hlf), writes=[t_wfmh[hlf]])
                P.dma("pool", wtm[:, hlf * 8:(hlf + 1) * 8, :], wtm_v[:, hlf * 8:(hlf + 1) * 8, :],
                      P.dsem("d_wtm%d" % hlf), writes=[t_wtmh[hlf]])
            modflat = mod_out.ap().rearrange("r n -> (r n)").unsqueeze(0)

            row_sems = {}

            def load_row(dst, src_row_ap, ds, W, R=()):
                key = id(W[0])
                if key not in row_sems:
                    row_sems[key] = P.dsem("d_row%d" % len(row_sems))
                P.dma("sp", dst.unsqueeze(1), src_row_ap.partition_broadcast(128), row_sems[key], reads=list(R), writes=W)

            load_row(Arow, modflat[0:1, 2048:4096], d_mod, [t_Arow], [t_modout])
            load_row(Brow, n1g_d, d_mod, [t_Brow], [])
            stt("dve", Arow, Arow, 1.0, Brow, ALU.add, ALU.mult, [t_Arow, t_Brow], [t_Arow])
            load_row(Brow, modflat[0:1, 0:2048], d_mod, [t_Brow], [t_modout])
            if DEBUG:
                P.dma("sp", dbg_mod, mod_out.ap(), P.dsem("dbgm"), reads=[t_modout])

            P.barrier()
            checkpoint("p0")
            AR.off = p1_off

            KT = [AR.alloc([128, SEQ], BF16) for _ in range(2)]
            t_KT = [[Trk() for _ in range(8)] for _ in range(2)]
            V = [AR.alloc([128, 32, 128], BF16) for _ in range(2)]
            t_V = [[Trk() for _ in range(8)] for _ in range(2)]
            xs = [AR.alloc([128, D], F32) for _ in range(2)]
            t_xs = [Trk(), Trk()]
            d_xs = [P.dsem("d_xs0"), P.dsem("d_xs1")]
            u_bfs = [AR.alloc([128, D], BF16) for _ in range(2)]; t_us = [Trk(), Trk()]
            uT_off = AR.off
            uT = AR.alloc([128, 16, 512], BF16)
            t_uT = [Trk() for _ in range(4)]
            stats = [AR.alloc([128, 8], F32) for _ in range(2)]; t_stats = [Trk(), Trk()]
            QT = [AR.alloc([128, 512], BF16) for _ in range(2)]; t_QT = [Trk(), Trk()]
            rqT = [AR.alloc([128, 512], F32) for _ in range(2)]; t_rqT = [Trk(), Trk()]
            snT = [AR.alloc([128, 512], F32) for _ in range(2)]; t_snT = [Trk(), Trk()]
            vtok = [AR.alloc([128, 4, 128], BF16) for _ in range(2)]
            t_vtok = [[Trk() for _ in range(4)] for _ in range(2)]
            sgf = [AR.alloc([128, 4, 256], F32) for _ in range(2)]
            t_sgf = [[Trk() for _ in range(4)] for _ in range(2)]
            Pb = [[AR.alloc([128, 512], BF16) for _ in range(2)] for _ in range(2)]
            t_Pb = [[Trk(), Trk()], [Trk(), Trk()]]
            R0 = AR.alloc([128, 512], F32); R1 = AR.alloc([128, 512], F32)
            o_f = AR.alloc([128, 512], F32); sq_bf = AR.alloc([128, 512], BF16)
            t_R0, t_R1, t_of, t_sq = Trk(), Trk(), Trk(), Trk()
            yst = [AR.alloc([128, 4, 512], BF16)] * 2
            t_yst = [[Trk() for _ in range(4)]] * 2
            d_yst = [P.dsem("d_yst0")] * 2
            def hgrn_set(A):
                W = {}
                for nm, shp, dt_ in (("hf", [128, 128], F32), ("hg", [128, 128], F32), ("hk", [128, 128], F32),
                                     ("E12", [128, 256], F32), ("E3", [128, 128], F32), ("E4", [128, 128], F32),
                                     ("qdz", [128, 2, 128], BF16), ("qm", [128, 128], BF16), ("km", [128, 128], BF16),
                                     ("kl", [128, 128], BF16), ("attm", [128, 128], BF16), ("gg", [128, 128], F32),
                                     ("yb", [128, 128], BF16), ("hst", [128, 8], F32)):
                    W[nm] = A.alloc(shp, dt_)
                    W["t_" + nm] = Trk()
                return W

            HW = [hgrn_set(AR)]
            AR1 = Arena(arena_t, uT_off + 16 * 1024)
            AR1.off = uT_off
            HW.append(hgrn_set(AR1))
            hw1_trks = [v for k_, v in HW[1].items() if k_.startswith("t_")]
            t_qdz = HW[0]["t_qdz"]
            qdz = HW[0]["qdz"]
            S32 = [AR.alloc([128, 128], F32) for _ in range(2)]; t_S32 = [Trk(), Trk()]
            Sbf = [[AR.alloc([128, 128], BF16) for _ in range(2)] for _ in range(2)]
            t_Sbf = [[Trk(), Trk()], [Trk(), Trk()]]
            for hh in range(2):
                memset("pool", S32[hh], 0.0, [t_S32[hh]])
                memset("pool", Sbf[hh][0], 0.0, [t_Sbf[hh][0]])
                memset("pool", Sbf[hh][1], 0.0, [t_Sbf[hh][1]])
            memset("pool", qdz, 0.0, [t_qdz])
            print("phase1 arena bytes", AR.off)

            psT = [banks[0][:, :].bitcast(BF16), banks[1][:, :].bitcast(BF16)]

            def norm_s1(i, x_ap_dram, xbuf, t_x, d_x):
                ub, t_ub, stt_, t_st = u_bfs[i % 2], t_us[i % 2], stats[i % 2], t_stats[i % 2]
                P.dma("sp", xbuf, x_ap_dram, d_x, writes=[t_x])
                act(ub, xbuf, AF.Square, [t_x], [t_ub, t_st], accum_out=stt_[:, 0:1])
                act(stt_[:, 1:2], stt_[:, 0:1], AF.Ln, [t_st], [t_st], scale=1.0 / D, bias=EPS)
                act(stt_[:, 2:3], stt_[:, 1:2], AF.Exp, [t_st], [t_st], scale=-0.5)
                stt("dve", xbuf, xbuf, stt_[:, 2:3], Arow, ALU.mult, ALU.mult, [t_x, t_st, t_Arow], [t_x])
                tt("pool", ub[:, 0:1024], xbuf[:, 0:1024], Brow[:, 0:1024], ALU.add, [t_x, t_Brow], [t_ub])
                tt("dve", ub[:, 1024:2048], xbuf[:, 1024:2048], Brow[:, 1024:2048], ALU.add, [t_x, t_Brow], [t_ub])

            def norm_s2(i, uT_dst, t_uTdst):
                ub, t_ub = u_bfs[i % 2], t_us[i % 2]
                for hlf in range(2):
                    for k8 in range(8):
                        kt = hlf * 8 + k8
                        tr(psT[hlf][:, k8 * 128:(k8 + 1) * 128], ub[:, kt * 128:(kt + 1) * 128], ident_bf,
                           [t_ub, t_idb], [bk[hlf]], signal=(k8 == 7))
                    src = psT[hlf].rearrange("p (a b) -> p a b", a=8)
                    cp("act" if hlf == 0 else "dve", uT_dst[:, hlf * 8:(hlf + 1) * 8, :], src, [bk[hlf]], [t_uTdst])

            def fm_proj(wsb, t_w, col0, rhs_fn, t_rhs, bank_i, nk=16):
                for kt in range(nk):
                    mm(banks[bank_i][:, :], wsb[:, kt, col0:col0 + 128], rhs_fn(kt), kt == 0, kt == nk - 1,
                       [t_w] + t_rhs, [bk[bank_i]], signal=(kt == nk - 1))

            rot = [2]

            def next_bank(lo=2, hi=7):
                b_ = rot[0]
                rot[0] = lo + (rot[0] - lo + 1) % (hi - lo + 1)
                return b_

            for blk in range(8):
                def _s1(t4_):
                    T_ = blk * 4 + t4_
                    norm_s1(T_, xb[T_ * 128:(T_ + 1) * 128, :], xs[T_ % 2], t_xs[T_ % 2], d_xs[T_ % 2])

                _s1(0)
                for t4 in range(4):
                    if t4 + 1 < 4:
                        _s1(t4 + 1)
                    norm_s2(blk * 4 + t4, uT[:, :, t4 * 128:(t4 + 1) * 128], t_uT[t4])
                for hh in range(2):
                    for ci in range(4):
                        b_ = next_bank()
                        fm_proj(wfm, t_wfmh[0], hh * 512 + ci * 128, lambda kt: uT[:, kt, :], t_uT + [t_wfmh[1]], b_)
                        if ci == 0:
                            cp("act", QT[hh], banks[b_], [bk[b_]], [t_QT[hh]])
                        elif ci == 1:
                            cp("dve", KT[hh][:, blk * 512:(blk + 1) * 512], banks[b_], [bk[b_]], [t_KT[hh][blk]])
                        elif ci == 2:
                            cp("act", rqT[hh], banks[b_], [bk[b_]], [t_rqT[hh]])
                        else:
                            act(snT[hh], banks[b_], AF.Sigmoid, [bk[b_]], [t_snT[hh]], scale=-1.0)
                    for t4 in range(4):
                        b_ = next_bank()
                        for kt in range(16):
                            mm(banks[b_][:, :], uT[:, kt, t4 * 128:(t4 + 1) * 128], wtm[:, kt, hh * 512:(hh + 1) * 512],
                               kt == 0, kt == 15, [t_uT[t4]] + t_wtmh, [bk[b_]], signal=(kt == 15))
                        cp("dve", V[hh][:, blk * 4 + t4, :], banks[b_][:, 0:128], [bk[b_]], [t_V[hh][blk]])
                        cp("dve", vtok[hh][:, t4, :], banks[b_][:, 128:256], [bk[b_]], [t_vtok[hh][t4]])
                        act(sgf[hh][:, t4, :], banks[b_][:, 256:512], AF.Sigmoid, [bk[b_], t_vtok[hh][t4]], [t_sgf[hh][t4]])
                checkpoint("proj%d" % blk)
                ys = yst[blk % 2]
                t_ys = t_yst[blk % 2]
                for hh in range(2):
                    O = [banks[2], banks[3]]; Dn = [banks[4], banks[5]]
                    Rm = [R0, R1]; t_Rm = [t_R0, t_R1]
                    Sxs = [[banks[6], banks[7]], [banks[0], banks[1]]]
                    bSx = [[6, 7], [0, 1]]
                    nkt = 4 * (blk + 1)

                    def emit_S(kt_):
                        r_ = kt_ - 4 * blk
                        c0_ = 128 * r_ if r_ > 0 else 0
                        for m in range(2):
                            mm(Sxs[kt_ % 2][m][:, c0_:512], KT[hh][64 * m:64 * (m + 1), kt_ * 128:(kt_ + 1) * 128],
                               QT[hh][64 * m:64 * (m + 1), c0_:512], True, True,
                               [t_KT[hh][kt_ // 4], t_QT[hh]], [bk[bSx[kt_ % 2][m]]], signal=True,
                               tile_position=(64 * m, 0))

                    emit_S(0)
                    for kt in range(nkt):
                        r = kt - 4 * blk
                        c0 = 128 * r if r > 0 else 0
                        kb_ = kt // 4
                        pb = kt % 2
                        if kt + 1 < nkt:
                            emit_S(kt + 1)
                        for m in range(2):
                            act(Pb[pb][m][:, c0:512], Sxs[pb][m][:, c0:512], AF.Exp, [bk[bSx[pb][m]]], [t_Pb[pb][m]],
                                scale=0.125)
                            if r >= 0:
                                memset("pool", Pb[pb][m][64:128, c0:c0 + 64], 0.0, [t_Pb[pb][m]])
                        last = (kt == nkt - 1)
                        for m in range(2):
                            mm(O[m][:, c0:512], V[hh][:, kt, :], Pb[pb][m][:, c0:512], kt == 0, last,
                               [t_V[hh][kb_], t_Pb[pb][m]], [bk[2 + m]], signal=last)
                        for m in range(2):
                            if kt == 0:
                                cp("dve", Rm[m], Pb[pb][m], [t_Pb[pb][m]], [t_Rm[m]])
                            else:
                                tt("dve", Rm[m][:, c0:512], Rm[m][:, c0:512], Pb[pb][m][:, c0:512], ALU.add,
                                   [t_Rm[m], t_Pb[pb][m]], [t_Rm[m]])
                    for m in range(2):
                        mm(Dn[m][:, :], ones32, Rm[m], True, True, [t_ones32, t_Rm[m]], [bk[4 + m]])
                    recip(R0, Dn[0], [bk[4]], [t_R0])
                    recip(R1, Dn[1], [bk[5]], [t_R1])
                    tt("dve", R0, O[0], R0, ALU.mult, [bk[2], t_R0], [t_R0])
                    tt("dve", R1, O[1], R1, ALU.mult, [bk[3], t_R1], [t_R1])
                    stt("dve", o_f, R1, neglam, R0, ALU.mult, ALU.add, [t_R0, t_R1, t_lam], [t_of])
                    act(sq_bf, o_f, AF.Square, [t_of], [t_sq])
                    mm(banks[6][:, :], ones_bf, sq_bf, True, True, [t_ones, t_sq], [bk[6]])
                    act(R0, banks[6], AF.Ln, [bk[6]], [t_R0], scale=1.0 / 128, bias=EPS)
                    act(R1, R0, AF.Exp, [t_R0], [t_R1], scale=-0.5)
                    stt("dve", ys[:, hh * 2 + 0, :], o_f, gsub, R1, ALU.mult, ALU.mult, [t_of, t_gsub, t_R1],
                        [t_ys[hh * 2 + 0]])
                checkpoint("attn%d" % blk)
                for e_ in ("pe", "act", "dve", "pool"):
                    P.wait_all(e_, t_uT)
                memset("pool", HW[1]["qdz"], 0.0, [HW[1]["t_qdz"]])

                def hgrn_gen(hh, t4, W, bnk):
                    cs, bB, bC, bD = banks[bnk[0]], banks[bnk[1]], banks[bnk[2]], banks[bnk[3]]
                    kcs, kB, kC, kD = bk[bnk[0]], bk[bnk[1]], bk[bnk[2]], bk[bnk[3]]
                    hf, hg, hk, E12, E3, E4 = W["hf"], W["hg"], W["hk"], W["E12"], W["E3"], W["E4"]
                    qdz_, qm, km, kl, attm, gg, yb, hst = (W["qdz"], W["qm"], W["km"], W["kl"], W["attm"], W["gg"],
                                                           W["yb"], W["hst"])
                    t_hf, t_hg, t_hk, t_E12, t_E3, t_E4 = (W["t_hf"], W["t_hg"], W["t_hk"], W["t_E12"], W["t_E3"],
                                                           W["t_E4"])
                    t_qdz_, t_qm, t_km, t_kl, t_attm, t_gg, t_yb, t_hst = (W["t_qdz"], W["t_qm"], W["t_km"], W["t_kl"],
                                                                          W["t_attm"], W["t_gg"], W["t_yb"], W["t_hst"])
                    tsl = slice(t4 * 128, (t4 + 1) * 128)
                    sig_rg = sgf[hh][:, t4, 0:128]
                    sig_rf = sgf[hh][:, t4, 128:256]
                    tt("dve", hf, sig_rf, omlrow[:, hh, :], ALU.mult, [t_sgf[hh][t4], t_lbrow], [t_hf])
                    tt("dve", hf, hf, lbrow[:, hh, :], ALU.add, [t_hf, t_lbrow], [t_hf])
                    yield
                    act(hg, hf, AF.Ln, [t_hf], [t_hg])
                    ts("dve", hk, hf, -1.0, 1.0, ALU.mult, ALU.add, [t_hf], [t_hk])
                    tt("pool", gg, sig_rg, gnrow[:, hh, :], ALU.mult, [t_sgf[hh][t4], t_gnr], [t_gg])
                    yield
                    mm(cs[:, 0:128], hg, U32, True, True, [t_hg, t_c32], [kcs], signal=False)
                    mm(cs[:, 128:256], hg, Um32, True, True, [t_hg, t_c32], [kcs], signal=False)
                    mm(cs[:, 256:384], L32, hg, True, True, [t_hg, t_c32], [kcs])
                    yield
                    act(E12, cs[:, 0:256], AF.Exp, [kcs], [t_E12])
                    act(E3, cs[:, 128:256], AF.Exp, [kcs], [t_E3], scale=-1.0)
                    act(E4, cs[:, 256:384], AF.Exp, [kcs], [t_E4])
                    yield
                    tt("dve", qdz_[:, 0, 0:64], rqT[hh][:, t4 * 128:t4 * 128 + 64], E12[:, 0:64], ALU.mult,
                       [t_rqT[hh], t_E12], [t_qdz_])
                    tt("dve", qdz_[:, 1, 64:128], rqT[hh][:, t4 * 128 + 64:t4 * 128 + 128], E12[:, 64:128], ALU.mult,
                       [t_rqT[hh], t_E12], [t_qdz_])
                    tt("pool", qm, rqT[hh][:, tsl], E12[:, 128:256], ALU.mult, [t_rqT[hh], t_E12], [t_qm])
                    stt("dve", km, snT[hh][:, tsl], omlcol[:, hh:hh + 1], E3, ALU.mult, ALU.mult,
                        [t_snT[hh], t_lbcol, t_E3], [t_km])
                    tt("pool", kl, hk, E4, ALU.mult, [t_hk, t_E4], [t_kl])
                    yield
                    mm(bB[:, 0:128], km, qm, True, True, [t_km, t_qm], [kB])
                    yield
                    tt("dve", attm, bB[:, 0:128], maskU_bf, ALU.mult, [kB, t_mku], [t_attm])
                    yield
                    vt = vtok[hh][:, t4, :]
                    mm(bC[:, 0:128], attm, vt, True, False, [t_attm, t_vtok[hh][t4]], [kC], signal=False)
                    mm(bC[:, 0:128], qdz_[:, 0, :], Sbf[hh][0], False, False, [t_qdz_, t_Sbf[hh][0]], [kC], signal=False)
                    mm(bD[:, 0:128], kl[0:64, :], vt[0:64, :], True, True, [t_kl, t_vtok[hh][t4]], [kD],
                       tile_position=(0, 0))
                    yield
                    stt("dve", Sbf[hh][1], S32[hh], E12[:, 63:64], bD[:, 0:128], ALU.mult, ALU.add,
                        [t_S32[hh], t_E12, kD], [t_Sbf[hh][1]])
                    stt("dve", S32[hh], S32[hh], E12[:, 63:64], bD[:, 0:128], ALU.mult, ALU.add,
                        [t_S32[hh], t_E12, kD], [t_S32[hh]])
                    yield
                    mm(bC[:, 0:128], qdz_[:, 1, :], Sbf[hh][1], False, True, [t_qdz_, t_Sbf[hh][1]], [kC])
                    mm(bD[:, 128:256], kl[64:128, :], vt[64:128, :], True, True, [t_kl, t_vtok[hh][t4]], [kD],
                       tile_position=(64, 0))
                    yield
                    stt("dve", Sbf[hh][0], S32[hh], E12[:, 127:128], bD[:, 128:256], ALU.mult, ALU.add,
                        [t_S32[hh], t_E12, kD], [t_Sbf[hh][0]])
                    stt("dve", S32[hh], S32[hh], E12[:, 127:128], bD[:, 128:256], ALU.mult, ALU.add,
                        [t_S32[hh], t_E12, kD], [t_S32[hh]])
                    act(attm, bC[:, 0:128], AF.Square, [kC], [t_attm, t_hst], accum_out=hst[:, 0:1])
                    yield
                    act(hst[:, 1:2], hst[:, 0:1], AF.Ln, [t_hst], [t_hst], scale=1.0 / 128, bias=EPS)
                    act(hst[:, 2:3], hst[:, 1:2], AF.Exp, [t_hst], [t_hst], scale=-0.5)
                    yield
                    stt("dve", yb, bC[:, 0:128], hst[:, 2:3], gg, ALU.mult, ALU.mult, [kC, t_hst, t_gg], [t_yb])
                    yield
                    ytp = bD[:, 256:320].bitcast(BF16)
                    tr(ytp, yb, ident_bf, [t_yb, t_idb], [kD])
                    yield
                    cp("act", ys[:, hh * 2 + 1, tsl], ytp, [kD], [t_ys[hh * 2 + 1]])

                for t4 in range(4):
                    alive = [hgrn_gen(0, t4, HW[0], (2, 3, 4, 5)), hgrn_gen(1, t4, HW[1], (6, 7, 0, 1))]
                    while alive:
                        for g_ in list(alive):
                            try:
                                next(g_)
                            except StopIteration:
                                alive.remove(g_)
                for e_ in ("act", "dve"):
                    P.wait_all(e_, hw1_trks)
                checkpoint("hgrn%d" % blk)
                t_exb = Trk()
                P.dma("sp", ex_in.ap()[blk].rearrange("(a p) t -> p a t", p=128), ys,
                      d_yst[0], reads=t_ys, writes=[t_exb])
                P.wait_all("pool", [t_exb])
                if cc_n[0] > 0:
                    P._wait_for("pool", ("dma", ccsd, cc_n[0]))
                cc_n[0] += 1
                P.raw("pool", lambda e, blk=blk: e.collective_compute(
                    "AllGather", ALU.bypass, replica_groups=[[0, 1, 2, 3], [4, 5, 6, 7]],
                    ins=[ex_in.ap()[blk]], outs=[ex_out.ap()[blk]]).then_inc(ccs, 1))
                t_exout.w = ("dma", ccsd, cc_n[0])

            checkpoint("p1")
            if DEBUG:
                P.dma("sp", dbg_y, ex_in.ap(), P.dsem("dbgy"), reads=[t_exout])
            checkpoint("ag")
            P.barrier()
            AR.off = base_off

            uTo = AR.alloc([128, 16, NTOK], BF16)
            t_uTo = [Trk() for _ in range(8)]
            hole_off = AR.off
            mT = AR.alloc([128, 16, NTOK], BF16)
            t_mT = [[Trk() for _ in range(2)] for _ in range(16)]
            r2_off = AR.off
            yT = AR.alloc([128, 2, 8, NTOK], BF16); t_yT = Trk()
            d_y = P.dsem("d_y")
            P.wait_all("sp", [t_joff, t_exout])
            ex_v = ex_out.ap().rearrange("b (r hh k p) t -> p k (r hh) b t", r=4, hh=2, k=2)
            reg_holder = {}

            def _ld_reg(e):
                for b2 in range(2):
                    reg = e.alloc_register("joff%d" % b2)
                    e.reg_load(reg, joff_sb[0:1, b2:b2 + 1])
                    reg_holder[b2] = e.snap(reg, min_val=0, max_val=7)

            P.raw("sp", _ld_reg)
            for kind in range(2):
                for b2 in range(2):
                    d_y["val"] += 16
                    P.raw("sp", lambda e, kind=kind, b2=b2: e.dma_start(
                        out=yT[:, kind, :, b2 * 512:(b2 + 1) * 512],
                        in_=ex_v[:, kind, :, reg_holder[b2], :]).then_inc(d_y["sem"], 16))
            t_yT.w = ("dma", d_y, d_y["val"])

            checkpoint("exch")
            xs2 = [AR.alloc([128, D], F32) for _ in range(2)]
            t_xs2 = [Trk(), Trk()]
            u_bfs = [AR.alloc([128, D], BF16) for _ in range(2)]; t_us = [Trk(), Trk()]
            stats = [AR.alloc([128, 8], F32) for _ in range(2)]; t_stats = [Trk(), Trk()]

            def _s1o(T_):
                norm_s1(T_, xo[T_ * 128:(T_ + 1) * 128, :], xs2[T_ % 2], t_xs2[T_ % 2], d_xs[T_ % 2])

            _s1o(0)
            for T in range(8):
                if T + 1 < 8:
                    _s1o(T + 1)
                norm_s2(T, uTo[:, :, T * 128:(T + 1) * 128], t_uTo[T])
            checkpoint("s0")
            wga = [AR.alloc([128, 16, 256], BF16) for _ in range(2)]
            wgr = [AR.alloc([128, 16, 256], BF16) for _ in range(2)]
            wao = [AR.alloc([128, 8, 256], BF16) for _ in range(2)]
            wro = [AR.alloc([128, 8, 256], BF16) for _ in range(2)]
            t_wa = [Trk(), Trk()]
            d_wa = [P.dsem("d_wa0"), P.dsem("d_wa1")]
            sga = AR.alloc([128, 512], F32); sgr = AR.alloc([128, 512], F32)
            t_sga, t_sgr = Trk(), Trk()
            print("phase2a arena bytes", AR.off)
            wg_v = wg_d.rearrange("(kt p) n -> p kt n", p=128)
            wao_v = wao_d.rearrange("(kt p) n -> p kt n", p=128)
            wro_v = wro_d.rearrange("(kt p) n -> p kt n", p=128)
            def _ld_a(cg_):
                s_ = cg_ % 2
                c0_ = cg_ * 256
                P.dma("pool", wga[s_], wg_v[:, :, c0_:c0_ + 256], d_wa[s_], writes=[t_wa[s_]])
                P.dma("pool", wgr[s_], wg_v[:, :, 2048 + c0_:2048 + c0_ + 256], d_wa[s_], writes=[t_wa[s_]])
                P.dma("pool", wao[s_], wao_v[:, :, c0_:c0_ + 256], d_wa[s_], writes=[t_wa[s_]])
                P.dma("pool", wro[s_], wro_v[:, :, c0_:c0_ + 256], d_wa[s_], writes=[t_wa[s_]])

            _ld_a(0)
            for cg in range(8):
                s = cg % 2
                if cg + 1 < 8:
                    _ld_a(cg + 1)
                for c2 in range(2):
                    ct = cg * 2 + c2
                    for tb in range(2):
                        tks = slice(tb * 512, (tb + 1) * 512)
                        t_u4 = t_uTo[tb * 4:(tb + 1) * 4]
                        b_ga, b_gr, b_a, b_r = 0 + 4 * tb, 1 + 4 * tb, 2 + 4 * tb, 3 + 4 * tb
                        fm_proj(wga[s], t_wa[s], c2 * 128, lambda kt: uTo[:, kt, tks], t_u4, b_ga)
                        fm_proj(wgr[s], t_wa[s], c2 * 128, lambda kt: uTo[:, kt, tks], t_u4, b_gr)
                        fm_proj(wao[s], t_wa[s], c2 * 128, lambda kt: yT[:, 0, kt, tks], [t_yT], b_a, nk=8)
                        fm_proj(wro[s], t_wa[s], c2 * 128, lambda kt: yT[:, 1, kt, tks], [t_yT], b_r, nk=8)
                        act(sga, banks[b_ga], AF.Sigmoid, [bk[b_ga]], [t_sga])
                        act(sgr, banks[b_gr], AF.Sigmoid, [bk[b_gr]], [t_sgr])
                        tt("dve", sga, banks[b_a], sga, ALU.mult, [bk[b_a], t_sga], [t_sga])
                        tt("dve", sgr, banks[b_r], sgr, ALU.mult, [bk[b_r], t_sgr], [t_sgr])
                        tt("dve", mT[:, ct, tks], sga, sgr, ALU.add, [t_sga, t_sgr], [t_mT[ct][tb]])
            P.barrier()
            AR.off = r2_off
            checkpoint("sa")
            h = AR.alloc([128, 8, D], F32)
            t_h = [Trk() for _ in range(8)]
            d_h = P.dsem("d_h")
            for T in range(8):
                P.dma("sp", h[:, T, :], xo[T * 128:(T + 1) * 128, :], d_h, writes=[t_h[T]])
            for T in range(8):
                t_h[T].w = ("dma", d_h, d_h["val"])
            Grow = AR.alloc([128, D], F32); t_Grow = Trk()
            load_row(Grow, modflat[0:1, 4096:6144], None, [t_Grow], [t_modout])
            r3_off = AR.off
            wos = [AR.alloc([128, 16, 512], BF16) for _ in range(2)]
            t_wos = [Trk(), Trk()]
            d_wos = [P.dsem("d_wos0"), P.dsem("d_wos1")]
            print("phase2b arena bytes", AR.off)
            wo_v = wo_d.rearrange("(kt p) n -> p kt n", p=128)
            def _ld_b(cb_):
                s_ = cb_ % 2
                cols_ = slice(cb_ * 512, (cb_ + 1) * 512)
                P.dma("pool", wos[s_], wo_v[:, :, cols_], d_wos[s_], writes=[t_wos[s_]])

            def _sc_b(cb_):
                s_ = cb_ % 2
                cols_ = slice(cb_ * 512, (cb_ + 1) * 512)
                for ct_ in range(16):
                    tt("dve", wos[s_][:, ct_, :], wos[s_][:, ct_, :], Grow[:, cols_], ALU.mult,
                       [t_wos[s_], t_Grow], [t_wos[s_]])

            _ld_b(0)
            _sc_b(0)
            for cb in range(4):
                s = cb % 2
                cols = slice(cb * 512, (cb + 1) * 512)
                if cb + 1 < 4:
                    _ld_b(cb + 1)
                for T in range(8):
                    b_ = T % 4
                    for ct in range(16):
                        mm(banks[b_][:, :], mT[:, ct, T * 128:(T + 1) * 128], wos[s][:, ct, :], ct == 0, ct == 15,
                           [t_mT[ct][T // 4], t_wos[s]], [bk[b_]], signal=(ct == 15))
                    tt("dve", h[:, T, cols], h[:, T, cols], banks[b_], ALU.add, [t_h[T], bk[b_]], [t_h[T]])
                if cb + 1 < 4:
                    _sc_b(cb + 1)
            P.barrier()
            checkpoint("sb")
            load_row(Arow, modflat[0:1, 8192:10240], None, [t_Arow], [t_modout])
            load_row(Brow, n2g_d, None, [t_Brow], [])
            stt("dve", Arow, Arow, 1.0, Brow, ALU.add, ALU.mult, [t_Arow, t_Brow], [t_Arow])
            load_row(Brow, modflat[0:1, 6144:8192], None, [t_Brow], [t_modout])
            AR.off = hole_off
            aTs = [AR.alloc([128, 4, NTOK], BF16) for _ in range(2)]
            t_aTs = [[[Trk() for _ in range(2)] for _ in range(4)] for _ in range(2)]
            wd = [AR.alloc([128, 4, 512], BF16) for _ in range(2)]
            t_wd = [Trk(), Trk()]
            d_wd = [P.dsem("d_wd0"), P.dsem("d_wd1")]
            stats = [AR.alloc([128, 8], F32) for _ in range(2)]; t_stats = [Trk(), Trk()]
            assert AR.off <= r2_off, (AR.off, r2_off)
            AR.off = r3_off
            tmpxs = [AR.alloc([128, D], F32) for _ in range(2)]; t_tmpxs = [Trk(), Trk()]
            u_bfs = [AR.alloc([128, D], BF16) for _ in range(2)]; t_us = [Trk(), Trk()]

            def norm_h(T):
                st_, t_st = stats[T % 2], t_stats[T % 2]
                ub, t_ub = u_bfs[T % 2], t_us[T % 2]
                act(ub, h[:, T, :], AF.Square, [t_h[T]], [t_ub, t_st], accum_out=st_[:, 0:1])
                act(st_[:, 1:2], st_[:, 0:1], AF.Sqrt, [t_st], [t_st], scale=1.0 / D, bias=EPS)
                recip(st_[:, 2:3], st_[:, 1:2], [t_st], [t_st])

            def _c1(T):
                norm_h(T)
                stt("dve", tmpxs[T % 2], h[:, T, :], stats[T % 2][:, 2:3], Arow, ALU.mult, ALU.mult,
                    [t_h[T], t_stats[T % 2], t_Arow], [t_tmpxs[T % 2]])
                tt("pool", u_bfs[T % 2][:, 0:1024], tmpxs[T % 2][:, 0:1024], Brow[:, 0:1024], ALU.add,
                   [t_tmpxs[T % 2], t_Brow], [t_us[T % 2]])
                tt("dve", u_bfs[T % 2][:, 1024:2048], tmpxs[T % 2][:, 1024:2048], Brow[:, 1024:2048], ALU.add,
                   [t_tmpxs[T % 2], t_Brow], [t_us[T % 2]])

            _c1(0)
            for T in range(8):
                if T + 1 < 8:
                    _c1(T + 1)
                norm_s2(T, uTo[:, :, T * 128:(T + 1) * 128], t_uTo[T])
            P.wait_all("pool", t_tmpxs + t_us)
            checkpoint("sc")
            load_row(Grow, modflat[0:1, 10240:12288], None, [t_Grow], [t_modout])
            AR.off = r3_off
            wgu = [AR.alloc([128, 2, 16, 256], BF16) for _ in range(2)]
            t_wgu = [Trk(), Trk()]
            d_wgu = [P.dsem("d_wgu0"), P.dsem("d_wgu1")]
            sil = [AR.alloc([128, 512], F32) for _ in range(2)]
            t_sil = [Trk(), Trk()]
            evt = [AR.alloc([128, 512], F32) for _ in range(2)]
            t_evt = [Trk(), Trk()]
            print("phase2d arena bytes", AR.off)
            wfg_v = wfg_d.rearrange("(kt p) n -> p kt n", p=128)
            wfu_v = wfu_d.rearrange("(kt p) n -> p kt n", p=128)
            wfd_v = wfd_d.rearrange("(ht p) n -> p ht n", p=128)
            stages = []
            gcount = 0
            dcount = 0
            for grp in range(11):
                for sg in range(2):
                    stages.append(("gu", grp, sg, gcount % 2))
                    gcount += 1
                for cb in range(4):
                    stages.append(("dn", grp, cb, dcount % 2))
                    dcount += 1

            def _ld_d(st_):
                kind_, grp_, x_, s_ = st_
                if kind_ == "gu":
                    hc0 = (grp_ * 4 + x_ * 2) * 128
                    P.dma("pool", wgu[s_][:, 0, :, :], wfg_v[:, :, hc0:hc0 + 256], d_wgu[s_], writes=[t_wgu[s_]])
                    P.dma("pool", wgu[s_][:, 1, :, :], wfu_v[:, :, hc0:hc0 + 256], d_wgu[s_], writes=[t_wgu[s_]])
                else:
                    cols_ = slice(x_ * 512, (x_ + 1) * 512)
                    P.dma("pool", wd[s_], wfd_v[:, grp_ * 4:(grp_ + 1) * 4, cols_], d_wd[s_], writes=[t_wd[s_]])

            def _sc_d(st_):
                kind_, grp_, x_, s_ = st_
                if kind_ == "dn":
                    cols_ = slice(x_ * 512, (x_ + 1) * 512)
                    for hl_ in range(4):
                        tt("dve", wd[s_][:, hl_, :], wd[s_][:, hl_, :], Grow[:, cols_], ALU.mult,
                           [t_wd[s_], t_Grow], [t_wd[s_]])

            def _cmp_d(st_):
                kind_, grp_, x_, s = st_
                if kind_ == "gu":
                    sg = x_
                    for gi in range(2):
                        hl = sg * 2 + gi
                        for tb in range(2):
                            tks = slice(tb * 512, (tb + 1) * 512)
                            t_u4 = t_uTo[tb * 4:(tb + 1) * 4]
                            b_g = 4 + 2 * tb
                            b_u = 5 + 2 * tb
                            for kt in range(16):
                                mm(banks[b_g][:, :], wgu[s][:, 0, kt, gi * 128:(gi + 1) * 128], uTo[:, kt, tks],
                                   kt == 0, kt == 15, [t_wgu[s]] + t_u4, [bk[b_g]], signal=(kt == 15))
                            for kt in range(16):
                                mm(banks[b_u][:, :], wgu[s][:, 1, kt, gi * 128:(gi + 1) * 128], uTo[:, kt, tks],
                                   kt == 0, kt == 15, [t_wgu[s]] + t_u4, [bk[b_u]], signal=(kt == 15))
                            act(sil[tb], banks[b_g], AF.Silu, [bk[b_g]], [t_sil[tb]])
                            tt("dve", aTs[grp_ % 2][:, hl, tks], banks[b_u], sil[tb], ALU.mult, [bk[b_u], t_sil[tb]],
                               [t_aTs[grp_ % 2][hl][tb]])
                else:
                    cols = slice(x_ * 512, (x_ + 1) * 512)
                    for T in range(8):
                        b_ = T
                        for hl in range(4):
                            mm(banks[b_][:, :], aTs[grp_ % 2][:, hl, T * 128:(T + 1) * 128], wd[s][:, hl, :], hl == 0,
                               hl == 3, [t_aTs[grp_ % 2][hl][T // 4], t_wd[s]], [bk[b_]], signal=(hl == 3))
                        if T % 3 == 2:
                            ev_ = evt[(T // 3) % 2]
                            t_ev = t_evt[(T // 3) % 2]
                            cp("act", ev_, banks[b_], [bk[b_]], [t_ev])
                            tt("pool", h[:, T, cols], h[:, T, cols], ev_, ALU.add, [t_h[T], t_ev], [t_h[T]])
                        else:
                            tt("dve", h[:, T, cols], h[:, T, cols], banks[b_], ALU.add, [t_h[T], bk[b_]], [t_h[T]])

            gu_list = [st_ for st_ in stages if st_[0] == "gu"]
            dn_list = [st_ for st_ in stages if st_[0] == "dn"]
            nxt = {"gu": 2, "dn": 2}
            lists = {"gu": gu_list, "dn": dn_list}
            for st_ in gu_list[:2] + dn_list[:2]:
                _ld_d(st_)
            _sc_d(stages[0])
            for i_, st_ in enumerate(stages):
                _cmp_d(st_)
                k_ = st_[0]
                if nxt[k_] < len(lists[k_]):
                    _ld_d(lists[k_][nxt[k_]])
                    nxt[k_] += 1
                if i_ + 1 < len(stages):
                    _sc_d(stages[i_ + 1])
            checkpoint("sd")
            load_row(Arow, fg_d, None, [t_Arow], [])
            if DEBUG:
                for T in range(8):
                    P.dma("sp", dbg_h[T * 128:(T + 1) * 128, :], h[:, T, :], P.dsem("dbgh%d" % T), reads=[t_h[T]])
            for T in range(8):
                norm_h(T)
                stt("dve", h[:, T, :], h[:, T, :], stats[T % 2][:, 2:3], Arow, ALU.mult, ALU.mult,
                    [t_h[T], t_stats[T % 2], t_Arow], [t_h[T]])
                P.dma("sp", out_d[T * 128:(T + 1) * 128, :], h[:, T, :], dout, reads=[t_h[T]])

        except _Stop:
            P.barrier()
        if DEBUG:
            for d_ in P.dma_sems:
                if d_["name"].startswith("dbg"):
                    P._wait_for("sp", ("dma", d_, d_["val"]))
        P.raw("sp", lambda e: e.wait_ge(dout["sem"], dout["val"]))
        P.finish()
        print("instructions", P.ninstr, "waits", P.nwaits, "counts", P.cnt)
    return nc


def _consts():
    idx = np.arange(128)
    ch = idx // 64
    same = ch[:, None] == ch[None, :]
    ident = np.eye(128, dtype=np.float32)
    U = (same & (idx[:, None] <= idx[None, :])).astype(np.float32)
    mid = ch * 64 + 31
    Umid = (same & (idx[:, None] <= mid[None, :])).astype(np.float32)
    Um = U - Umid
    L = (same & (idx[:, None] > idx[None, :])).astype(np.float32)
    return np.ascontiguousarray(np.concatenate([ident, U, Um, L], axis=1))


def make_in_maps(x, c, w_mod, b_mod, norm1_g, w_in, lambda_q1, lambda_k1, lambda_q2, lambda_k2,
                 subln_g, lb_logits, gnorm_g, w_att_out, w_rec_out, w_o, norm2_g,
                 w_ffn_gate, w_ffn_up, w_ffn_down, final_g):
    f = lambda a: np.ascontiguousarray(np.asarray(a, dtype=np.float32))
    x = f(x); c = f(c); w_in0 = f(w_in)[0]; w_mod0 = f(w_mod)[0]; b_mod0 = f(b_mod)[0]
    lbl = f(lb_logits); gn = f(gnorm_g)[0]
    AQ, AK, AV, RQ, RF, RI, RG, GA = 0, 1024, 2048, 3072, 4096, 5120, 6144, 7168
    shared = {
        "w_g": np.ascontiguousarray(w_in0[:, GA:GA + 4096]),
        "w_ao": f(w_att_out)[0], "w_ro": f(w_rec_out)[0], "w_o": f(w_o)[0],
        "w_fg": f(w_ffn_gate)[0], "w_fu": f(w_ffn_up)[0], "w_fd": f(w_ffn_down)[0],
        "n1g": f(norm1_g)[0][None], "n2g": f(norm2_g)[0][None], "fg": f(final_g)[None],
        "subg": f(subln_g)[0][:, None],
        "lamv": np.concatenate([f(lambda_q1)[0], f(lambda_k1)[0], f(lambda_q2)[0], f(lambda_k2)[0]])[None],
        "consts": _consts(),
    }
    shared = {k: np.ascontiguousarray(v) for k, v in shared.items()}
    maps = []
    for core in range(8):
        b, j = core // 4, core % 4
        heads = (2 * j, 2 * j + 1)
        sl = lambda off, h: w_in0[:, off + h * 128: off + (h + 1) * 128]
        w_fm = np.concatenate([np.concatenate([sl(AQ, h), sl(AK, h), sl(RQ, h), sl(RF, h)], axis=1) for h in heads], axis=1)
        w_tm = np.concatenate([np.concatenate([sl(AV, h), sl(RI, h), sl(RG, h), sl(RF, h)], axis=1) for h in heads], axis=1)
        lbc = np.stack([np.stack([lbl[l, h * 128:(h + 1) * 128] for l in range(2)], axis=1) for h in heads], axis=1)
        lbr = np.stack([np.stack([lbl[l, h * 128:(h + 1) * 128] for l in range(2)], axis=0) for h in heads], axis=0)
        m = dict(shared)
        m.update({
            "xb": x[b],
            "xo": np.ascontiguousarray(x[b, j * NTOK:(j + 1) * NTOK]),
            "cT": np.ascontiguousarray(c[b].reshape(16, 128).T),
            "wmod": np.ascontiguousarray(w_mod0[:, j * 3072:(j + 1) * 3072]),
            "bmod": np.ascontiguousarray(b_mod0[j * 3072:(j + 1) * 3072][None]),
            "w_fm": np.ascontiguousarray(w_fm), "w_tm": np.ascontiguousarray(w_tm),
            "lbc": np.ascontiguousarray(lbc.reshape(128, 4)),
            "lbr": np.ascontiguousarray(lbr.reshape(1, 512)),
            "gnr": np.ascontiguousarray(np.stack([gn[h] for h in heads], axis=0).reshape(1, 256)),
            "joff": np.array([[2 * j, 2 * j + 1]], dtype=np.int32),
        })
        maps.append(m)
    return maps


_NC_CACHE = {}


def kernel(**inputs):
    in_maps = make_in_maps(**inputs)
    if "nc" not in _NC_CACHE:
        _NC_CACHE["nc"] = build_nc()
    nc = _NC_CACHE["nc"]
    res = run_bass_kernel_spmd(nc, in_maps, core_ids=list(range(8)))
    out = np.zeros((2, SEQ, D), dtype=np.float32)
    for core in range(8):
        b, j = core // 4, core % 4
        out[b, j * NTOK:(j + 1) * NTOK] = np.asarray(res.results[core]["out"], dtype=np.float32)
    if DEBUG:
        kernel.last = res
    return out
```
